# Optimizing a Trainium2 kernel written in Bass

```python
import jax, jax.numpy as jnp
from jax import lax
import numpy as np

D_MODEL = 1024
BATCH = 2
SEQ = 16384
DEPTH = 4

CHUNK = 64
Q_BLOCK = 128
RMS_EPS = 1e-6
D_FF = 2816
POOL_WINDOWS = (2, 4, 8, 16)
POOL_GROUP = D_MODEL // 8
POOL_WIDTH = POOL_GROUP * len(POOL_WINDOWS)
FOX_HEADS = 8
FOX_HEAD_DIM = D_MODEL // (2 * FOX_HEADS)
FOX_WIDTH = FOX_HEADS * FOX_HEAD_DIM
AB_IN = POOL_WIDTH + 3 * FOX_WIDTH + FOX_HEADS
AB_MIX = POOL_WIDTH + FOX_WIDTH
MLA_HEADS = 16
MLA_NOPE = 64
MLA_ROPE = 32
MLA_V = 64
MLA_Q_LORA = 256
MLA_KV_LORA = 128
MLA_IN = MLA_Q_LORA + MLA_KV_LORA + MLA_ROPE
ROPE_THETA = 10000.0
N_EVEN = (DEPTH + 1) // 2
N_ODD = DEPTH // 2

kernel_name = 'hybrid_pool_fox_mla_macaron_encoder'


def rms_norm(x, g):
    xf = x.astype(jnp.float32)
    y = xf * lax.rsqrt(jnp.mean(xf * xf, axis=-1, keepdims=True) + RMS_EPS)
    return (y * g.astype(jnp.float32)).astype(x.dtype)


def swiglu(x, w_gate, w_up, w_down):
    return (jax.nn.silu(x @ w_gate) * (x @ w_up)) @ w_down


def rope_tables(positions):
    inv_freq = ROPE_THETA ** (-jnp.arange(0, MLA_ROPE, 2, dtype=jnp.float32) / MLA_ROPE)
    ang = positions.astype(jnp.float32)[..., None] * inv_freq
    return jnp.cos(ang), jnp.sin(ang)


def apply_rope(x, cos, sin):
    xf = x.astype(jnp.float32)
    half = xf.shape[-1] // 2
    x1, x2 = xf[..., :half], xf[..., half:]
    return jnp.concatenate([x1 * cos - x2 * sin, x2 * cos + x1 * sin], axis=-1).astype(x.dtype)


def blocked_attention(q, k, v, scale, chunk, log_decay=None):
    b, s, h, dk = q.shape
    dv = v.shape[-1]
    nb = s // Q_BLOCK
    q = q * scale
    qb = q.reshape(b, nb, Q_BLOCK, h, dk).transpose(1, 0, 2, 3, 4)
    k_chunk = jnp.arange(s) // chunk
    xs = (qb, jnp.arange(nb))
    if log_decay is not None:
        fk = log_decay.transpose(0, 2, 1)
        fqb = log_decay.reshape(b, nb, Q_BLOCK, h).transpose(1, 0, 3, 2)
        xs = xs + (fqb,)

    def attend(args):
        q_blk, blk = args[0], args[1]
        logits = jnp.einsum('bqhd,bkhd->bhqk', q_blk, k).astype(jnp.float32)
        if log_decay is not None:
            logits = logits + (args[2][..., None] - fk[:, :, None, :])
        t_pos = blk * Q_BLOCK + jnp.arange(Q_BLOCK)
        allowed = k_chunk[None, :] <= (t_pos // chunk)[:, None]
        logits = jnp.where(allowed, logits, -jnp.inf)
        p = jax.nn.softmax(logits, axis=-1)
        return jnp.einsum('bhqk,bkhd->bqhd', p.astype(v.dtype), v)

    out = lax.map(attend, xs)
    return out.transpose(1, 0, 2, 3, 4).reshape(b, s, h, dv)


def pool_mixer(u, w_pool, pool_scale):
    s = u.shape[1]
    cs = jnp.cumsum(u.astype(jnp.float32), axis=1)
    t_count = jnp.arange(s) + 1
    outs = []
    for g, w in enumerate(POOL_WINDOWS):
        lo, hi = g * POOL_GROUP, (g + 1) * POOL_GROUP
        cs_g = cs[..., lo:hi]
        lagged = jnp.pad(cs_g, ((0, 0), (w, 0), (0, 0)))[:, :s]
        count = jnp.minimum(t_count, w).astype(jnp.float32)[None, :, None]
        diff = (cs_g - lagged) / count - u[..., lo:hi].astype(jnp.float32)
        outs.append(diff.astype(u.dtype) @ w_pool[g])
    return jnp.concatenate(outs, axis=-1) * pool_scale


def pool_fox_mixer(h, w_in, b_forget, w_pool, pool_scale, w_out):
    b, s, _ = h.shape
    proj = h @ w_in
    o1 = POOL_WIDTH
    o2 = o1 + FOX_WIDTH
    o3 = o2 + FOX_WIDTH
    o4 = o3 + FOX_WIDTH
    u = proj[..., :o1]
    q = proj[..., o1:o2].reshape(b, s, FOX_HEADS, FOX_HEAD_DIM)
    k = proj[..., o2:o3].reshape(b, s, FOX_HEADS, FOX_HEAD_DIM)
    v = proj[..., o3:o4].reshape(b, s, FOX_HEADS, FOX_HEAD_DIM)
    f_logit = proj[..., o4:] + b_forget
    y_pool = pool_mixer(u, w_pool, pool_scale)
    log_f = jax.nn.log_sigmoid(f_logit.astype(jnp.float32))
    cum_log_f = jnp.cumsum(log_f, axis=1)
    y_fox = blocked_attention(q, k, v, FOX_HEAD_DIM ** -0.5, 1, cum_log_f)
    y = jnp.concatenate([y_pool, y_fox.reshape(b, s, FOX_WIDTH)], axis=-1)
    return y @ w_out


def mla_mixer(h, cos, sin, w_in, q_norm, kv_norm, w_q_b, w_kv_b, w_out):
    b, s, _ = h.shape
    proj = h @ w_in
    c_q = proj[..., :MLA_Q_LORA]
    c_kv = proj[..., MLA_Q_LORA:MLA_Q_LORA + MLA_KV_LORA]
    k_rope = apply_rope(proj[..., MLA_Q_LORA + MLA_KV_LORA:], cos, sin)
    q = (rms_norm(c_q, q_norm) @ w_q_b).reshape(b, s, MLA_HEADS, MLA_NOPE + MLA_ROPE)
    q_rope = apply_rope(q[..., MLA_NOPE:], cos[:, :, None, :], sin[:, :, None, :])
    qk = jnp.concatenate([q[..., :MLA_NOPE], q_rope], axis=-1)
    kv = (rms_norm(c_kv, kv_norm) @ w_kv_b).reshape(b, s, MLA_HEADS, MLA_NOPE + MLA_V)
    k = jnp.concatenate(
        [kv[..., :MLA_NOPE], jnp.broadcast_to(k_rope[:, :, None, :], (b, s, MLA_HEADS, MLA_ROPE))],
        axis=-1)
    v = kv[..., MLA_NOPE:]
    y = blocked_attention(qk, k, v, (MLA_NOPE + MLA_ROPE) ** -0.5, CHUNK)
    return y.reshape(b, s, MLA_HEADS * MLA_V) @ w_out


def setup_inputs(seed: int = 0) -> dict:
    key = jax.random.key(seed)
    ks = jax.random.split(key, 20)

    def normal(k, shape, scale):
        return scale * jax.random.normal(k, shape, jnp.float32)

    x = normal(ks[0], (BATCH, SEQ, D_MODEL), 1.0)
    start = jax.random.randint(ks[1], (BATCH, 1), 0, 64, dtype=jnp.int32) * CHUNK
    positions = (start + jnp.arange(SEQ, dtype=jnp.int32)[None, :]).astype(jnp.int32)
    norm_ffn = 1.0 + normal(ks[2], (DEPTH, 2, D_MODEL), 0.05)
    norm_mix = 1.0 + normal(ks[3], (DEPTH, D_MODEL), 0.05)
    norm_final = 1.0 + normal(ks[4], (D_MODEL,), 0.05)
    ffn_w_gate = normal(ks[5], (DEPTH, 2, D_MODEL, D_FF), D_MODEL ** -0.5)
    ffn_w_up = normal(ks[6], (DEPTH, 2, D_MODEL, D_FF), D_MODEL ** -0.5)
    ffn_w_down = normal(ks[7], (DEPTH, 2, D_FF, D_MODEL), D_FF ** -0.5)
    ab_w_in = normal(ks[8], (N_EVEN, D_MODEL, AB_IN), D_MODEL ** -0.5)
    ab_b_forget = jax.random.uniform(ks[9], (N_EVEN, FOX_HEADS), jnp.float32, 1.0, 6.0)
    pool_w = normal(ks[10], (N_EVEN, len(POOL_WINDOWS), POOL_GROUP, POOL_GROUP), POOL_GROUP ** -0.5)
    pool_scale = 1.0 + normal(ks[11], (N_EVEN, POOL_WIDTH), 0.1)
    ab_w_out = normal(ks[12], (N_EVEN, AB_MIX, D_MODEL), AB_MIX ** -0.5)
    mla_w_in = normal(ks[13], (N_ODD, D_MODEL, MLA_IN), D_MODEL ** -0.5)
    mla_q_norm = 1.0 + normal(ks[14], (N_ODD, MLA_Q_LORA), 0.05)
    mla_kv_norm = 1.0 + normal(ks[15], (N_ODD, MLA_KV_LORA), 0.05)
    mla_w_q_b = normal(ks[16], (N_ODD, MLA_Q_LORA, MLA_HEADS * (MLA_NOPE + MLA_ROPE)), MLA_Q_LORA ** -0.5)
    mla_w_kv_b = normal(ks[17], (N_ODD, MLA_KV_LORA, MLA_HEADS * (MLA_NOPE + MLA_V)), MLA_KV_LORA ** -0.5)
    mla_w_out = normal(ks[18], (N_ODD, MLA_HEADS * MLA_V, D_MODEL), (MLA_HEADS * MLA_V) ** -0.5)
    return {'x': x, 'positions': positions, 'norm_ffn': norm_ffn, 'norm_mix': norm_mix,
            'norm_final': norm_final, 'ffn_w_gate': ffn_w_gate, 'ffn_w_up': ffn_w_up,
            'ffn_w_down': ffn_w_down, 'ab_w_in': ab_w_in, 'ab_b_forget': ab_b_forget,
            'pool_w': pool_w, 'pool_scale': pool_scale, 'ab_w_out': ab_w_out,
            'mla_w_in': mla_w_in, 'mla_q_norm': mla_q_norm, 'mla_kv_norm': mla_kv_norm,
            'mla_w_q_b': mla_w_q_b, 'mla_w_kv_b': mla_w_kv_b, 'mla_w_out': mla_w_out}


def reference(x, positions, norm_ffn, norm_mix, norm_final, ffn_w_gate, ffn_w_up, ffn_w_down,
              ab_w_in, ab_b_forget, pool_w, pool_scale, ab_w_out,
              mla_w_in, mla_q_norm, mla_kv_norm, mla_w_q_b, mla_w_kv_b, mla_w_out):
    cos, sin = rope_tables(positions)
    h = x
    for layer in range(DEPTH):
        h = h + 0.5 * swiglu(rms_norm(h, norm_ffn[layer, 0]), ffn_w_gate[layer, 0],
                             ffn_w_up[layer, 0], ffn_w_down[layer, 0])
        hn = rms_norm(h, norm_mix[layer])
        i = layer // 2
        if layer % 2 == 0:
            h = h + pool_fox_mixer(hn, ab_w_in[i], ab_b_forget[i], pool_w[i], pool_scale[i], ab_w_out[i])
        else:
            h = h + mla_mixer(hn, cos, sin, mla_w_in[i], mla_q_norm[i], mla_kv_norm[i],
                              mla_w_q_b[i], mla_w_kv_b[i], mla_w_out[i])
        h = h + 0.5 * swiglu(rms_norm(h, norm_ffn[layer, 1]), ffn_w_gate[layer, 1],
                             ffn_w_up[layer, 1], ffn_w_down[layer, 1])
    return rms_norm(h, norm_final)
```

```python
import numpy as np
from contextlib import ExitStack
import concourse.bass as bass
import concourse.mybir as mybir

F32 = mybir.dt.float32
BF16 = mybir.dt.bfloat16
I32 = mybir.dt.int32
AF = mybir.ActivationFunctionType
ALU = mybir.AluOpType

D = 1024
KC = 8
DFF = 2816
FC = 22
EPS = 1e-6


def I(name, *args, **kw):
    return (name, args, kw)


def _mk(ins, h, inc):
    name, args, kw = ins

    def run(e):
        r = getattr(e, name)(*args, **kw)
        if h is not None:
            r.then_inc(h, inc)
    return run


class SemC:
    def __init__(self, h, name):
        self.h = h
        self.v = 0
        self.name = name


class Res:
    __slots__ = ("w", "r", "name")

    def __init__(self, name=""):
        self.w = {}
        self.r = {}
        self.name = name


def RL(n, name=""):
    return [Res(f"{name}{i}") for i in range(n)]


class Prog:
    ENG = ("sp", "act", "pe", "dve", "pool")

    def __init__(self, nc, stack):
        self.nc = nc
        self.stack = stack
        self.streams = {k: [] for k in self.ENG}
        self.esem = {}
        for k in ("act", "pe", "dve", "pool"):
            self.esem[k] = self.sem("e_" + k)
        self.waited = {}
        self.sems = {}
        self.all_sems = list(self.esem.values())
        self.nwaits = 0
        self.ninstr = 0

    def sem(self, name):
        h = self.stack.enter_context(self.nc.semaphore(name))
        s = SemC(h, name)
        if hasattr(self, "all_sems"):
            self.all_sems.append(s)
        return s

    def S(self, name):
        if name not in self.sems:
            self.sems[name] = self.sem(name)
        return self.sems[name]

    def _need(self, eng, own, reads, writes, pwrites=()):
        need = {}

        import os as _os
        noskip = bool(_os.environ.get("NOSKIP"))

        def add(tok, same_skip):
            s, v = tok
            if s is own and same_skip and not (noskip and eng != "pe"):
                return
            if need.get(s, 0) < v:
                need[s] = v

        pe = eng == "pe"
        for R in reads:
            for tok in R.w.items():
                add(tok, pe)
        for R in writes:
            for tok in R.w.items():
                add(tok, True)
            for tok in R.r.items():
                add(tok, True)
        for R in pwrites:
            for tok in R.r.items():
                add(tok, True)
        return need

    def _emit_waits(self, eng, need):
        st = self.streams[eng]
        for s, v in need.items():
            key = (eng, s)
            if self.waited.get(key, 0) >= v:
                continue
            self.waited[key] = v
            self.nwaits += 1
            st.append(lambda e, s=s, v=v: e.wait_ge(s.h, v))

    def _update(self, tok, reads, writes, pwrites=()):
        s, v = tok
        for R in reads:
            if R.r.get(s, 0) < v:
                R.r[s] = v
        for R in writes:
            R.w = {s: v}
            R.r = {}
        for R in pwrites:
            if R.w.get(s, 0) < v:
                R.w[s] = v

    def capture(self, fn):
        prev = getattr(self, "_cap", None)
        self._cap = []
        fn()
        out = self._cap
        self._cap = prev
        return out

    def replay(self, calls):
        for kind, a, kw in calls:
            getattr(self, kind)(*a, **kw)

    def op(self, eng, fns, reads=(), writes=(), pwrites=()):
        if getattr(self, "_cap", None) is not None:
            self._cap.append(("op", (eng, fns), dict(reads=reads, writes=writes, pwrites=pwrites)))
            return
        if isinstance(fns[0], str):
            fns = [fns]
        own = self.esem[eng]
        need = self._need(eng, own, reads, writes, pwrites)
        self._emit_waits(eng, need)
        st = self.streams[eng]
        for f in fns[:-1]:
            st.append(_mk(f, None, 0))
        own.v += 1
        st.append(_mk(fns[-1], own.h, 1))
        self.ninstr += len(fns)
        self._update((own, own.v), reads, writes, pwrites)

    def dma(self, q, sem, fns, reads=(), writes=(), pwrites=()):
        if getattr(self, "_cap", None) is not None:
            self._cap.append(("dma", (q, sem, fns), dict(reads=reads, writes=writes, pwrites=pwrites)))
            return
        if isinstance(fns[0], str):
            fns = [fns]
        need = self._need(q, None, reads, writes, pwrites)
        self._emit_waits(q, need)
        st = self.streams[q]
        for f in fns:
            st.append(_mk(f, sem.h, 16))
        sem.v += 16 * len(fns)
        self.ninstr += len(fns)
        self._update((sem, sem.v), reads, writes, pwrites)

    def coll(self, sem, fn, reads=(), writes=()):
        need = self._need("pool", None, reads, writes)
        self._emit_waits("pool", need)
        self.streams["pool"].append(_mk(fn, sem.h, 1))
        sem.v += 1
        self._update((sem, sem.v), reads, writes)

    def barrier(self, engines=None):
        for eng in engines or self.ENG:
            need = {s: s.v for s in self.all_sems if s.v > 0 and not s.name.startswith("cc")}
            self._emit_waits(eng, need)

    def finish(self):
        self.barrier()
        with self.nc.Block() as block:
            @block.sync
            def _(e):
                for f in self.streams["sp"]:
                    f(e)

            @block.scalar
            def _(e):
                for f in self.streams["act"]:
                    f(e)

            @block.tensor
            def _(e):
                for f in self.streams["pe"]:
                    f(e)

            @block.vector
            def _(e):
                for f in self.streams["dve"]:
                    f(e)

            @block.gpsimd
            def _(e):
                for f in self.streams["pool"]:
                    f(e)


class Ctx:
    def __init__(self, P):
        self.P = P
        nc = P.nc
        st = P.stack
        self.ps2 = [st.enter_context(nc.psum_tensor(f"ps{i}", [128, 1024], F32)) for i in range(4)]
        self.ps = [self.ps2[i // 2][:, (i % 2) * 512:(i % 2 + 1) * 512] for i in range(8)]
        self.psr = RL(8, "ps")
        self.ones_bf = st.enter_context(nc.sbuf_tensor("ones_bf", [128, 128], BF16))
        self.ones_f = st.enter_context(nc.sbuf_tensor("ones_f", [128, 512], F32))
        self.r_const = Res("const")
        P.op("dve", [I("memset", self.ones_bf[:], 1.0), I("memset", self.ones_f[:], 1.0)], writes=[self.r_const])


_uid = [0]


def check_interleave(L0, L1):
    lastw = {}
    for k in range(max(len(L0), len(L1))):
        for tid, L in ((0, L0), (1, L1)):
            if k >= len(L):
                continue
            kind, a_, kw_ = L[k]
            for R in kw_["reads"]:
                if id(R) in lastw and lastw[id(R)][0] != tid:
                    print("INTERLEAVE VIOLATION: tile", tid, "op", k, kind, a_[0], "reads", R.name, "last written by", lastw[id(R)])
            for R in list(kw_["writes"]) + list(kw_["pwrites"]):
                lastw[id(R)] = (tid, k)


def sb(P, stack, name, shape, dt):
    _uid[0] += 1
    return stack.enter_context(P.nc.sbuf_tensor(f"{name}_u{_uid[0]}", shape, dt))


def tview(ap, t, TN):
    return ap[:, :, t * TN:(t + 1) * TN].rearrange("k p n -> p k n")


def emit_norm(P, C, h, h_res, g, sq, sq_res, psn, psn_res, rstd, rstd_res, xn, xn_res,
              nkc=KC, dim=D):
    for kc in range(nkc):
        P.op("dve", I("tensor_tensor", sq[kc], h[kc], h[kc], ALU.mult), reads=[h_res[kc]], writes=[sq_res[kc]])
    P.op("pe", [I("matmul", psn, C.ones_bf[:], sq[kc], start=(kc == 0), stop=(kc == nkc - 1)) for kc in range(nkc)],
         reads=[C.r_const] + list(sq_res[:nkc]), writes=[psn_res])
    P.op("dve", I("tensor_scalar", rstd, psn, 1.0 / dim, EPS, ALU.mult, ALU.add), reads=[psn_res], writes=[rstd_res])
    P.op("act", I("activation", out=rstd, in_=rstd, func=AF.Sqrt), reads=[rstd_res], writes=[rstd_res])
    P.op("dve", I("reciprocal", rstd, rstd), reads=[rstd_res], writes=[rstd_res])
    for kc in range(nkc):
        P.op("dve", I("scalar_tensor_tensor", xn[kc], h[kc], g[kc], rstd, ALU.mult, ALU.mult),
             reads=[h_res[kc], rstd_res], writes=[xn_res[kc]])


def ffn_phase(P, C, hin, hin_res, hout, hout_res, wgu, wd, gains, gidx, NT, TN=512, name="ffn"):
    nt = NT // TN
    with ExitStack() as st:
        NH = 3
        hb = [sb(P, st, f"{name}_h{i}", [128, KC, TN], F32) for i in range(NH)]
        hb_r = [RL(KC, "hb") for _ in range(NH)]
        xn = [sb(P, st, f"{name}_xn{i}", [128, KC, TN], BF16) for i in range(2)]
        xn_r = [RL(KC, "xn") for _ in range(2)]
        sq = sb(P, st, f"{name}_sq", [128, KC, TN], BF16)
        sq_r = RL(KC, "sq")
        rstd = sb(P, st, f"{name}_rstd", [128, TN], F32)
        rstd_r = Res("rstd")
        h1 = sb(P, st, f"{name}_h1", [128, FC, TN], BF16)
        h1_r = RL(FC, "h1")
        NSG = 3
        sg = [sb(P, st, f"{name}_sg{i}", [128, TN], F32) for i in range(NSG)]
        sg_r = RL(NSG, "sg")
        NW = 3
        wg_sb = [sb(P, st, f"{name}_wgu{i}", [128, 2, KC, 128], BF16) for i in range(NW)]
        wg_r = RL(NW, "wgu")
        wd_sb = [sb(P, st, f"{name}_wd{i}", [128, FC, 128], BF16) for i in range(NW)]
        wd_r = RL(NW, "wd")
        s_h = [P.S(f"ld_h{i}") for i in range(NH)]
        s_st = [P.S(f"st_h{i}") for i in range(NH)]
        s_wg = [P.S(f"ld_wg{i}") for i in range(NW)]
        s_wd = [P.S(f"ld_wd{i}") for i in range(NW)]
        psG, psU, psD, psN = [0, 1], [2, 3], [4, 5], 6
        gl = [gains[:, gidx, kc:kc + 1] for kc in range(KC)]

        def load_h(t):
            s = t % NH
            P.dma("sp", s_h[s], I("dma_start", out=hb[s][:], in_=tview(hin, t, TN)), reads=[hin_res[t]], writes=hb_r[s])

        def norm(t):
            s = t % NH
            x = t % 2
            emit_norm(P, C, [hb[s][:, kc, :] for kc in range(KC)], hb_r[s], gl,
                      [sq[:, kc, :] for kc in range(KC)], sq_r, C.ps[psN][:, :TN], C.psr[psN], rstd[:], rstd_r,
                      [xn[x][:, kc, :] for kc in range(KC)], xn_r[x])

        load_h(0)
        norm(0)
        gi = 0
        di = 0
        wgc = 0
        wdc = 0
        for t in range(nt):
            s = t % NH
            x = t % 2
            if t + 1 < nt:
                load_h(t + 1)
            for fc in range(FC):
                w = wgc % NW
                wgc += 1
                P.dma("pool", s_wg[w], I("dma_start", out=wg_sb[w][:], in_=wgu[fc]), writes=[wg_r[w]])
                b = gi % 2
                q = gi % NSG
                gi += 1
                for (gu, bank) in ((0, psG[b]), (1, psU[b])):
                    P.op("pe", [I("matmul", C.ps[bank][:, :TN], wg_sb[w][:, gu, kc, :], xn[x][:, kc, :],
                                  start=(kc == 0), stop=(kc == KC - 1)) for kc in range(KC)],
                         reads=[wg_r[w]] + xn_r[x], writes=[C.psr[bank]])
                P.op("act", I("activation", out=sg[q][:], in_=C.ps[psG[b]][:, :TN], func=AF.Silu),
                     reads=[C.psr[psG[b]]], writes=[sg_r[q]])
                P.op("dve", I("tensor_tensor", h1[:, fc, :], sg[q][:], C.ps[psU[b]][:, :TN], ALU.mult),
                     reads=[sg_r[q], C.psr[psU[b]]], writes=[h1_r[fc]])
                if fc == 11 and t + 1 < nt:
                    norm(t + 1)
            for dc in range(KC):
                w = wdc % NW
                wdc += 1
                P.dma("pool", s_wd[w], I("dma_start", out=wd_sb[w][:], in_=wd[dc]), writes=[wd_r[w]])
                b = di % 2
                di += 1
                P.op("pe", [I("matmul", C.ps[psD[b]][:, :TN], wd_sb[w][:, fc, :], h1[:, fc, :],
                              start=(fc == 0), stop=(fc == FC - 1)) for fc in range(FC)],
                     reads=[wd_r[w]] + h1_r, writes=[C.psr[psD[b]]])
                P.op("dve", I("scalar_tensor_tensor", hb[s][:, dc, :], C.ps[psD[b]][:, :TN], 0.5, hb[s][:, dc, :],
                              ALU.mult, ALU.add),
                     reads=[C.psr[psD[b]], hb_r[s][dc]], writes=[hb_r[s][dc]])
            P.dma("sp", s_st[s], I("dma_start", out=tview(hout, t, TN), in_=hb[s][:]), reads=hb_r[s], writes=[hout_res[t]])
        P.barrier()


def final_norm_phase(P, C, hin, hin_res, out, out_res, gains, gidx, NT, TN=512, name="fin"):
    nt = NT // TN
    with ExitStack() as st:
        hb = [sb(P, st, f"{name}_h{i}", [128, KC, TN], F32) for i in range(2)]
        hb_r = [RL(KC) for _ in range(2)]
        ob = [sb(P, st, f"{name}_o{i}", [128, KC, TN], F32) for i in range(2)]
        ob_r = [RL(KC) for _ in range(2)]
        sq = sb(P, st, f"{name}_sq", [128, KC, TN], BF16)
        sq_r = RL(KC)
        rstd = sb(P, st, f"{name}_rstd", [128, TN], F32)
        rstd_r = Res()
        s_h = [P.S(f"ld_h{i}") for i in range(2)]
        s_st = [P.S(f"st_h{i}") for i in range(2)]
        gl = [gains[:, gidx, kc:kc + 1] for kc in range(KC)]
        for t in range(nt):
            s = t % 2
            P.dma("sp", s_h[s], I("dma_start", out=hb[s][:], in_=tview(hin, t, TN)), reads=[hin_res[t]], writes=hb_r[s])
            emit_norm(P, C, [hb[s][:, kc, :] for kc in range(KC)], hb_r[s], gl,
                      [sq[:, kc, :] for kc in range(KC)], sq_r, C.ps[6][:, :TN], C.psr[6], rstd[:], rstd_r,
                      [ob[s][:, kc, :] for kc in range(KC)], ob_r[s])
            P.dma("sp", s_st[s], I("dma_start", out=tview(out, t, TN), in_=ob[s][:]), reads=ob_r[s], writes=[out_res[t]])
        P.barrier()


def lay_wgu(wg, wu):
    a = wg.reshape(KC, 128, FC, 128).transpose(2, 1, 0, 3)
    b = wu.reshape(KC, 128, FC, 128).transpose(2, 1, 0, 3)
    return np.ascontiguousarray(np.stack([a, b], axis=2))


def lay_wd(wd):
    return np.ascontiguousarray(wd.reshape(FC, 128, KC, 128).transpose(2, 1, 0, 3))


def lay_gain(g):
    return np.ascontiguousarray(g.reshape(KC, 128).T)


NEG = -30000.0
NB = 32
NR = 4
FH = 8
ABIN = 2056


def evenin_phase(P, C, hin, hin_res, w_in_d, bf_sb, gains, gidx, QT_d, agK_in, agV_in, agF_in, agU_in, uT_d,
                 out_res, NT, TN=512, name="ein"):
    nt = NT // TN
    nb = TN // 128
    with ExitStack() as st:
        w = sb(P, st, f"{name}_w", [128, KC, ABIN], BF16)
        w_r = Res()
        hb = [sb(P, st, f"{name}_hb{i}", [128, KC, TN], F32) for i in range(2)]
        hb_r = [RL(KC) for _ in range(2)]
        xn = [sb(P, st, f"{name}_xn{i}", [128, KC, TN], BF16) for i in range(2)]
        xn_r = [RL(KC) for _ in range(2)]
        sq = sb(P, st, f"{name}_sq", [128, KC, TN], BF16)
        sq_r = RL(KC)
        rstd = sb(P, st, f"{name}_rstd", [128, TN], F32)
        rstd_r = Res()
        NE = 4
        ub = [sb(P, st, f"{name}_u{i}", [128, TN], F32) for i in range(NE)]
        ub_r = RL(NE)
        qk = [sb(P, st, f"{name}_qk{i}", [128, TN], BF16) for i in range(NE)]
        qk_r = RL(NE)
        vb = [sb(P, st, f"{name}_v{i}", [128, FH, nb, 65], BF16) for i in range(2)]
        vb_r = RL(2)
        lf = sb(P, st, f"{name}_lf", [8, NT], F32)
        lf_r = Res()
        fa = sb(P, st, f"{name}_fa", [8, TN], F32)
        fb = sb(P, st, f"{name}_fb", [8, TN], F32)
        fc_ = sb(P, st, f"{name}_fc", [8, TN], F32)
        f_r = Res()
        s_h = [P.S(f"ld_h{i}") for i in range(2)]
        s_u = [P.S(f"st_u{i}") for i in range(NE)]
        s_qk = [P.S(f"st_qk{i}") for i in range(NE)]
        s_v = [P.S(f"st_v{i}") for i in range(2)]
        gl = [gains[:, gidx, kc:kc + 1] for kc in range(KC)]
        P.dma("pool", P.S("ld_w0"), I("dma_start", out=w[:], in_=w_in_d), writes=[w_r])
        for i in range(2):
            P.op("dve", I("memset", vb[i][:, :, :, 64:65], 1.0), writes=[vb_r[i]])
        agU_v = agU_in.rearrange("(g p) (i c) -> g p i c", p=128, c=16)

        def load_h(t):
            s = t % 2
            P.dma("sp", s_h[s], I("dma_start", out=hb[s][:], in_=tview(hin, t, TN)), reads=[hin_res[t]], writes=hb_r[s])

        def norm(t):
            s = t % 2
            emit_norm(P, C, [hb[s][:, kc, :] for kc in range(KC)], hb_r[s], gl,
                      [sq[:, kc, :] for kc in range(KC)], sq_r, C.ps[6][:, :TN], C.psr[6], rstd[:], rstd_r,
                      [xn[s][:, kc, :] for kc in range(KC)], xn_r[s])

        load_h(0)
        norm(0)
        ei = 0
        bi = 0
        for t in range(nt):
            s = t % 2
            tok = slice(t * TN, (t + 1) * TN)
            if t + 1 < nt:
                load_h(t + 1)

            def mm_fm(col0, M, bank):
                P.op("pe", [I("matmul", C.ps[bank][:M, :TN], w[:, kc, col0:col0 + M], xn[s][:, kc, :],
                              start=(kc == 0), stop=(kc == KC - 1)) for kc in range(KC)],
                     reads=[w_r] + xn_r[s], writes=[C.psr[bank]])

            for g in range(4):
                bank = bi % 4
                bi += 1
                e = ei % NE
                ei += 1
                mm_fm(g * 128, 128, bank)
                P.op("act", I("activation", out=ub[e][:], in_=C.ps[bank][:, :TN], func=AF.Copy),
                     reads=[C.psr[bank]], writes=[ub_r[e]])
                P.dma("sp", s_u[e], [I("dma_start", out=uT_d[g, :, tok], in_=ub[e][:]),
                                     I("dma_start", out=agU_v[g, :, t * nb:(t + 1) * nb, :],
                                       in_=ub[e][:].rearrange("p (b c) -> p b c", c=128)[:, :, 112:128])],
                      reads=[ub_r[e]], pwrites=[out_res["u"], out_res["agU"]])
            for which in range(2):
                if which == 1 and t + 1 < nt:
                    norm(t + 1)
                for m in range(4):
                    bank = bi % 4
                    bi += 1
                    e = ei % NE
                    ei += 1
                    mm_fm(512 + which * 512 + m * 128, 128, bank)
                    if which == 0:
                        P.op("dve", I("tensor_scalar", qk[e][:], C.ps[bank][:, :TN], 0.125, None, ALU.mult),
                             reads=[C.psr[bank]], writes=[qk_r[e]])
                        dst = QT_d[2 * m:2 * m + 2, :, tok].rearrange("h d n -> (h d) n")
                        P.dma("sp", s_qk[e], I("dma_start", out=dst, in_=qk[e][:]), reads=[qk_r[e]], pwrites=[out_res["q"]])
                    else:
                        P.op("act", I("activation", out=qk[e][:], in_=C.ps[bank][:, :TN], func=AF.Copy),
                             reads=[C.psr[bank]], writes=[qk_r[e]])
                        P.dma("sp", s_qk[e], I("dma_start", out=agK_in[m][:, tok], in_=qk[e][:]), reads=[qk_r[e]],
                              pwrites=[out_res["agK"][m]])
            vs = t % 2
            for blk in range(nb):
                bank = bi % 4
                bi += 1
                P.op("pe", [I("matmul", C.ps[bank][:, :512], xn[s][:, kc, blk * 128:(blk + 1) * 128], w[:, kc, 1536:2048],
                              start=(kc == 0), stop=(kc == KC - 1)) for kc in range(KC)],
                     reads=[w_r] + xn_r[s], writes=[C.psr[bank]])
                eng = "act" if blk % 2 == 0 else "dve"
                src = C.ps[bank][:, :512].rearrange("p (h d) -> p h d", d=64)
                if eng == "act":
                    P.op("act", I("activation", out=vb[vs][:, :, blk, 0:64], in_=src, func=AF.Copy),
                         reads=[C.psr[bank]], pwrites=[vb_r[vs]])
                else:
                    P.op("dve", I("tensor_copy", vb[vs][:, :, blk, 0:64], src), reads=[C.psr[bank]], pwrites=[vb_r[vs]])
            P.dma("sp", s_v[vs], [I("dma_start", out=agV_in[h][:, t * nb * 65:(t + 1) * nb * 65],
                                    in_=vb[vs][:, h, :, :].rearrange("p i c -> p (i c)")) for h in range(FH)],
                  reads=[vb_r[vs]], pwrites=out_res["agV"])
            bank = 4
            mm_fm(2048, 8, bank)
            P.op("dve", I("tensor_scalar", fa[:], C.ps[bank][:8, :TN], bf_sb[:, 0:1], None, ALU.add), reads=[C.psr[bank]], writes=[f_r])
            P.op("dve", I("tensor_scalar", fc_[:], fa[:], -1.0, None, ALU.mult), reads=[f_r], writes=[f_r])
            P.op("dve", I("tensor_tensor", fb[:], fa[:], fc_[:], ALU.min), reads=[f_r], writes=[f_r])
            P.op("act", I("activation", out=fb[:], in_=fb[:], func=AF.Exp), reads=[f_r], writes=[f_r])
            P.op("act", I("activation", out=fb[:], in_=fb[:], func=AF.Ln, bias=1.0), reads=[f_r], writes=[f_r])
            P.op("dve", I("tensor_scalar", fc_[:], fa[:], 0.0, None, ALU.min), reads=[f_r], writes=[f_r])
            P.op("dve", I("tensor_tensor", lf[:, tok], fc_[:], fb[:], ALU.subtract), reads=[f_r], pwrites=[lf_r])
        P.dma("sp", P.S("st_misc"), I("dma_start", out=agF_in, in_=lf[:]), reads=[lf_r], writes=[out_res["agF"]])
        P.barrier()


_cc = [0]
_cc_slots = RL(8, "ccslot")


def allgather(P, src2d, dst2d, src_res, dst_res):
    k = _cc[0] % 8
    _cc[0] += 1
    P.coll(P.S(f"cc{k}"), I("collective_compute", "AllGather", ALU.bypass,
                            replica_groups=[[0, 1, 2, 3], [4, 5, 6, 7]],
                            ins=[src2d.opt()], outs=[dst2d.opt()]),
           reads=src_res, writes=list(dst_res) + [_cc_slots[k]])


def fprep_phase(P, C, agF, agF_res, sel, FK_d, FK_res, FQ_d, FQ_res, NT, name="fp"):
    nb = NT // 128
    CH = 2048
    nch = (4 * NT) // CH
    with ExitStack() as st:
        lf = sb(P, st, f"{name}_lf", [8, nb, 4, 128], F32)
        lf_r = Res()
        ones8 = sb(P, st, f"{name}_ones", [8, CH], F32)
        o_r = Res()
        Fc = [sb(P, st, f"{name}_F{i}", [8, CH], F32) for i in range(2)]
        Fc_r = RL(2)
        G = sb(P, st, f"{name}_G", [8, CH], F32)
        R1 = sb(P, st, f"{name}_R1", [8, CH], F32)
        g_r = Res()
        kp = [sb(P, st, f"{name}_kp{i}", [8, 3, CH], BF16) for i in range(2)]
        kp_r = RL(2)
        qs = [sb(P, st, f"{name}_qs{i}", [8, 3, 512], BF16) for i in range(2)]
        qs_r = RL(2)
        s_kp = [P.S(f"st_kp{i}") for i in range(2)]
        s_qs = [P.S(f"st_qs{i}") for i in range(2)]
        P.op("dve", I("memset", ones8[:], 1.0), writes=[o_r])
        P.dma("sp", P.S("ld_misc"), [I("dma_start", out=lf[:, :, r, :], in_=agF[r * 8:(r + 1) * 8, :].rearrange("h (i p) -> h i p", p=128))
                                     for r in range(4)], reads=[agF_res], writes=[lf_r])
        lff = lf[:].rearrange("h i r p -> h (i r p)")
        FKv = FK_d.rearrange("h c (r i p) -> h c i r p", r=4, p=128)
        for c in range(nch):
            b = c % 2
            init = 0.0 if c == 0 else Fc[1 - b][:, CH - 1:CH]
            P.op("dve", I("tensor_tensor_scan", Fc[b][:], ones8[:], lff[:, c * CH:(c + 1) * CH], init, ALU.mult, ALU.add),
                 reads=[o_r, lf_r, Fc_r[1 - b]], writes=[Fc_r[b]])
            P.op("dve", I("tensor_scalar", G[:], Fc[b][:], -1.0, None, ALU.mult), reads=[Fc_r[b]], writes=[g_r])
            P.op("dve", I("tensor_copy", kp[b][:, 0, :], G[:]), reads=[g_r], writes=[kp_r[b]])
            P.op("dve", I("tensor_tensor", R1[:], G[:], kp[b][:, 0, :], ALU.subtract), reads=[g_r, kp_r[b]], writes=[g_r])
            P.op("dve", I("tensor_copy", kp[b][:, 1, :], R1[:]), reads=[g_r], pwrites=[kp_r[b]])
            P.op("dve", I("tensor_tensor", G[:], R1[:], kp[b][:, 1, :], ALU.subtract), reads=[g_r, kp_r[b]], writes=[g_r])
            P.op("dve", I("tensor_copy", kp[b][:, 2, :], G[:]), reads=[g_r], pwrites=[kp_r[b]])
            P.dma("sp", s_kp[b], [I("dma_start", out=FKv[:, c3, 4 * c:4 * c + 4, r, :],
                                    in_=kp[b][:, c3, :].rearrange("h (i r p) -> h i r p", r=4, p=128)[:, :, r, :])
                                  for c3 in range(3) for r in range(4)],
                  reads=[kp_r[b]], pwrites=[FK_res])
            kv = kp[b][:].rearrange("h c (i r p) -> h c i r p", r=4, p=128)
            for c3 in range(3):
                o = qs[b][:, c3, :].rearrange("h (i p) -> h i p", p=128)
                P.op("dve", I("tensor_scalar", o, kv[:, c3, :, 0, :], sel[0:8, 0:1], None, ALU.mult),
                     reads=[kp_r[b]], writes=[qs_r[b]] if c3 == 0 else (), pwrites=() if c3 == 0 else [qs_r[b]])
                for r in range(1, 4):
                    P.op("dve", I("scalar_tensor_tensor", o, kv[:, c3, :, r, :], sel[0:8, r:r + 1], o, ALU.mult, ALU.add),
                         reads=[kp_r[b], qs_r[b]], pwrites=[qs_r[b]])
            P.dma("sp", s_qs[b], I("dma_start", out=FQ_d[:, :, c * 512:(c + 1) * 512], in_=qs[b][:]),
                  reads=[qs_r[b]], pwrites=[FQ_res])
        P.barrier()


def pool_phase(P, C, uT_d, u_res, agU, agU_res, sel, invc, wpool_d, pscale, yT_d, y_res, NT, name="pl"):
    nb = NT // 128
    nt = NT // 512
    with ExitStack() as st:
        wp = sb(P, st, f"{name}_wp", [128, 4, 128], BF16)
        wp_r = Res()
        P.dma("pool", P.S("ld_w0"), I("dma_start", out=wp[:], in_=wpool_d), writes=[wp_r])
        uext = [sb(P, st, f"{name}_ue{i}", [128, nb, 144], F32) for i in range(2)]
        ue_r = RL(2)
        H = [sb(P, st, f"{name}_H{i}", [128, 4, nb, 16], F32) for i in range(2)]
        H_r = RL(2)
        A = sb(P, st, f"{name}_A", [128, nb, 144], F32)
        B = sb(P, st, f"{name}_B", [128, nb, 144], F32)
        ab_r = Res()
        diff = [sb(P, st, f"{name}_df{i}", [128, nb, 128], BF16) for i in range(2)]
        df_r = RL(2)
        t16 = sb(P, st, f"{name}_t16", [128, 16], F32)
        yb = [sb(P, st, f"{name}_y{i}", [128, NT], BF16) for i in range(2)]
        yb_r = RL(2)
        s_u = [P.S(f"ld_pu{i}") for i in range(2)]
        s_H = [P.S(f"ld_pH{i}") for i in range(2)]
        s_y = [P.S(f"st_py{i}") for i in range(2)]
        agUv = agU.rearrange("(r g p) (i c) -> g p r i c", r=4, g=4, c=16)
        for g in range(4):
            b = g % 2
            w = 2 << g
            P.dma("sp", s_u[b], I("dma_start", out=uext[b][:, :, 16:144], in_=uT_d[g].rearrange("p (i c) -> p i c", c=128)),
                  reads=[u_res], writes=[ue_r[b]])
            P.dma("sp", s_H[b], I("dma_start", out=H[b][:], in_=agUv[g]), reads=[agU_res], writes=[H_r[b]])
            hal = uext[b][:, :, 0:16]
            P.op("dve", I("tensor_scalar", hal, H[b][:, 0, :, :], sel[:, 4:5], None, ALU.mult), reads=[H_r[b]], pwrites=[ue_r[b]])
            for r in range(1, 4):
                P.op("dve", I("scalar_tensor_tensor", hal, H[b][:, r, :, :], sel[:, 4 + r:5 + r], hal, ALU.mult, ALU.add),
                     reads=[H_r[b], ue_r[b]], pwrites=[ue_r[b]])
            if nb > 1:
                P.op("dve", I("scalar_tensor_tensor", uext[b][:, 1:, 0:16], H[b][:, 3, 0:nb - 1, :], sel[:, 8:9], uext[b][:, 1:, 0:16],
                              ALU.mult, ALU.add), reads=[H_r[b], ue_r[b]], pwrites=[ue_r[b]])
            src = uext[b]
            bufs = [A, B]
            lo = 0
            for stp in range(g + 1):
                sh = 1 << stp
                lo = lo + sh
                dst = bufs[stp % 2]
                P.op("dve", I("tensor_tensor", dst[:, :, lo:144], src[:, :, lo:144], src[:, :, lo - sh:144 - sh], ALU.add),
                     reads=[ue_r[b], ab_r], writes=[ab_r])
                src = dst
            sw = src
            P.op("dve", I("scalar_tensor_tensor", diff[b][:], sw[:, :, 16:144], 1.0 / w, uext[b][:, :, 16:144], ALU.mult, ALU.subtract),
                 reads=[ab_r, ue_r[b]], writes=[df_r[b]])
            P.op("dve", I("tensor_tensor", t16[:], sw[:, 0, 16:32], invc[:, g, :], ALU.mult), reads=[ab_r], writes=[ab_r])
            P.op("dve", I("tensor_tensor", diff[b][:, 0, 0:16], t16[:], uext[b][:, 0, 16:32], ALU.subtract),
                 reads=[ab_r, ue_r[b], df_r[b]], pwrites=[df_r[b]])
            for t in range(nt):
                bank = t % 2
                P.op("pe", I("matmul", C.ps[bank][:, :512], wp[:, g, :], diff[b][:, 4 * t:4 * t + 4, :].rearrange("p i c -> p (i c)"),
                             start=True, stop=True), reads=[wp_r, df_r[b]], writes=[C.psr[bank]])
                P.op("act", I("activation", out=yb[b][:, t * 512:(t + 1) * 512], in_=C.ps[bank][:, :512], func=AF.Copy, scale=pscale[:, g:g + 1]),
                     reads=[C.psr[bank]], writes=[yb_r[b]] if t == 0 else (), pwrites=() if t == 0 else [yb_r[b]])
            P.dma("sp", s_y[b], I("dma_start", out=yT_d[g], in_=yb[b][:]), reads=[yb_r[b]], pwrites=[y_res])
        P.barrier()


def attn_phase(P, C, nh, KR, load_head, mask, ident, yT_d, y_res, ychunk0, NT, name="at", LA=2):
    nb = NT // 128
    nt = NT // 512
    with ExitStack() as st:
        KT = [sb(P, st, f"{name}_KT{i}", [KR, 4, NT], BF16) for i in range(2)]
        QT = [sb(P, st, f"{name}_QT{i}", [KR, NT], BF16) for i in range(2)]
        V = [sb(P, st, f"{name}_V{i}", [128, 4, nb, 65], BF16) for i in range(2)]
        hd_r = [dict(KT=Res(), QT=Res(), V=Res()) for _ in range(2)]
        NP = LA + 2
        pT = [sb(P, st, f"{name}_pT{i}", [128, 512], BF16) for i in range(NP)]
        pT_r = RL(NP)
        osb = [sb(P, st, f"{name}_o{i}", [65, 512], F32) for i in range(2)]
        osb_r = RL(2)
        rec = [sb(P, st, f"{name}_rc{i}", [65, 512], F32) for i in range(2)]
        rec_r = RL(2)
        ysb = [sb(P, st, f"{name}_y{i}", [64, 512], BF16) for i in range(2)]
        ysb_r = RL(2)
        s_y = [P.S(f"st_ay{i}") for i in range(2)]
        NS = LA + 1
        psS = list(range(NS))
        psO = [NS, NS + 1]
        psB = NS + 2
        assert psB <= 7
        sems = [dict(KT=P.S(f"ld_KT{i}"), QT=P.S(f"ld_QT{i}"), V=P.S(f"ld_V{i}")) for i in range(2)]
        init_done = [False, False]
        si = 0
        fin = 0
        load_head(0, KT[0], QT[0], V[0], sems[0], hd_r[0], True)
        for h in range(nh):
            hb = h % 2
            if h + 1 < nh:
                load_head(h + 1, KT[1 - hb], QT[1 - hb], V[1 - hb], sems[1 - hb], hd_r[1 - hb], h + 1 < 2)
            kt, qt, v, hr = KT[hb], QT[hb], V[hb], hd_r[hb]
            for T in range(nt):
                blocks = []
                for r in range(4):
                    for i in range(4 * T):
                        blocks.append((r, i, 0, None))
                for qp in range(4):
                    for r in range(4):
                        blocks.append((r, 4 * T + qp, qp * 128, r))
                ob = psO[fin % 2]
                nblk = len(blocks)
                q0 = T * 512

                def emit_S(bi_):
                    r, i, c0, mk = blocks[bi_]
                    bank = psS[(si + bi_) % NS]
                    ins = [I("matmul", C.ps[bank][:, c0:512], kt[:, r, i * 128:(i + 1) * 128], qt[:, q0 + c0:q0 + 512],
                             start=True, stop=(mk is None))]
                    if mk is not None:
                        ins.append(I("matmul", C.ps[bank][:, c0:c0 + 128], ident[:], mask[:, mk, :], start=False, stop=True))
                    P.op("pe", ins, reads=[hr["KT"], hr["QT"], C.r_const], writes=[C.psr[bank]])

                def emit_E(bi_):
                    r, i, c0, mk = blocks[bi_]
                    bank = psS[(si + bi_) % NS]
                    p = (si + bi_) % NP
                    P.op("act", I("activation", out=pT[p][:, c0:512], in_=C.ps[bank][:, c0:512], func=AF.Exp),
                         reads=[C.psr[bank]], writes=[pT_r[p]])

                def emit_PV(bi_):
                    r, i, c0, mk = blocks[bi_]
                    p = (si + bi_) % NP
                    P.op("pe", I("matmul", C.ps[ob][:65, c0:512], v[:, r, i, :], pT[p][:, c0:512],
                                 start=(bi_ == 0), stop=(bi_ == nblk - 1)),
                         reads=[hr["V"], pT_r[p]], writes=[C.psr[ob]] if bi_ == 0 else (), pwrites=() if bi_ == 0 else [C.psr[ob]])

                for bi_ in range(min(LA, nblk)):
                    emit_S(bi_)
                    emit_E(bi_)
                for bi_ in range(nblk):
                    if bi_ + LA < nblk:
                        emit_S(bi_ + LA)
                        emit_E(bi_ + LA)
                    emit_PV(bi_)
                si += nblk
                f = fin % 2
                fin += 1
                P.op("act", I("activation", out=osb[f][:], in_=C.ps[ob][:65, :512], func=AF.Copy), reads=[C.psr[ob]], writes=[osb_r[f]])
                P.op("dve", I("reciprocal", rec[f][64:65, :], osb[f][64:65, :]), reads=[osb_r[f]], writes=[rec_r[f]])
                P.op("pe", I("matmul", C.ps[psB][:64, :512], C.ones_f[64:65, 0:64], rec[f][64:65, :], start=True, stop=True),
                     reads=[rec_r[f], C.r_const], writes=[C.psr[psB]])
                P.op("dve", I("tensor_tensor", ysb[f][:], osb[f][0:64, :], C.ps[psB][:64, :512], ALU.mult),
                     reads=[osb_r[f], C.psr[psB]], writes=[ysb_r[f]])
                ch = ychunk0 + h // 2
                p0 = (h % 2) * 64
                P.dma("sp", s_y[f], I("dma_start", out=yT_d[ch, p0:p0 + 64, q0:q0 + 512], in_=ysb[f][:]),
                      reads=[ysb_r[f]], pwrites=[y_res])
        P.barrier()


def fox_loader(P, agK, agK_res, QT_d, q_res, agV, agV_res, FK_d, FK_res, FQ_d, FQ_res, NT):
    agKv = [a.rearrange("(r h d) n -> h d r n", r=4, d=64) for a in agK]
    agVv = [a.rearrange("(r p) (i c) -> p r i c", r=4, c=65) for a in agV]
    FKv = FK_d.rearrange("h c (r n) -> h c r n", r=4)

    def load_head(h, KT, QT, V, sems, res, first):
        if first:
            P.op("dve", I("memset", KT[64:70, :, :], 1.0), writes=[res["KT"]])
            P.op("dve", I("memset", QT[64:70, :], 1.0), writes=[res["QT"]])
        P.dma("sp", sems["KT"], [I("dma_start", out=KT[0:64, :, :], in_=agKv[h // 2][h % 2]),
                                 I("dma_start", out=KT[64:67, :, :], in_=FKv[h])],
              reads=[agK_res[h // 2], FK_res], writes=[res["KT"]])
        P.dma("sp", sems["QT"], [I("dma_start", out=QT[0:64, :], in_=QT_d[h]),
                                 I("dma_start", out=QT[67:70, :], in_=FQ_d[h])],
              reads=[q_res, FQ_res], writes=[res["QT"]])
        P.dma("sp", sems["V"], I("dma_start", out=V[:], in_=agVv[h]), reads=[agV_res[h]], writes=[res["V"]])
    return load_head


def outproj_phase(P, C, hin, hin_res, hout, hout_res, yT_d, y_res, wout_d, NT, TN=512, name="op"):
    nt = NT // TN
    with ExitStack() as st:
        w = sb(P, st, f"{name}_w", [128, KC, D], BF16)
        w_r = Res()
        P.dma("pool", P.S("ld_w0"), I("dma_start", out=w[:], in_=wout_d), writes=[w_r])
        hb = [sb(P, st, f"{name}_hb{i}", [128, KC, TN], F32) for i in range(2)]
        hb_r = [RL(KC) for _ in range(2)]
        yb = [sb(P, st, f"{name}_yb{i}", [128, KC, TN], BF16) for i in range(2)]
        yb_r = RL(2)
        s_h = [P.S(f"ld_h{i}") for i in range(2)]
        s_y = [P.S(f"ld_y{i}") for i in range(2)]
        s_st = [P.S(f"st_h{i}") for i in range(2)]
        bi = 0

        def load(t):
            s = t % 2
            P.dma("sp", s_h[s], I("dma_start", out=hb[s][:], in_=tview(hin, t, TN)), reads=[hin_res[t]], writes=hb_r[s])
            P.dma("sp", s_y[s], I("dma_start", out=yb[s][:], in_=tview(yT_d, t, TN)), reads=[y_res], writes=[yb_r[s]])

        load(0)
        for t in range(nt):
            s = t % 2
            if t + 1 < nt:
                load(t + 1)
            for dc in range(KC):
                bank = bi % 4
                bi += 1
                P.op("pe", [I("matmul", C.ps[bank][:, :TN], w[:, kc, dc * 128:(dc + 1) * 128], yb[s][:, kc, :],
                              start=(kc == 0), stop=(kc == KC - 1)) for kc in range(KC)],
                     reads=[w_r, yb_r[s]], writes=[C.psr[bank]])
                P.op("dve", I("tensor_tensor", hb[s][:, dc, :], hb[s][:, dc, :], C.ps[bank][:, :TN], ALU.add),
                     reads=[C.psr[bank], hb_r[s][dc]], writes=[hb_r[s][dc]])
            P.dma("sp", s_st[s], I("dma_start", out=tview(hout, t, TN), in_=hb[s][:]), reads=hb_r[s], writes=[hout_res[t]])
        P.barrier()


def core_tables(j, mla=False):
    sel = np.zeros((128, 16), np.float32)
    sel[:, j] = -1.0
    if j > 0:
        sel[:, 4 + (j - 1)] = 1.0
    else:
        sel[:, 8] = 1.0
    invc = np.zeros((128, 4, 16), np.float32)
    for g in range(4):
        w = 2 << g
        if j == 0:
            invc[:, g, :] = 1.0 / np.minimum(np.arange(16) + 1, w)
        else:
            invc[:, g, :] = 1.0 / w
    pk = np.arange(128)[:, None]
    pq = np.arange(128)[None, :]
    mf = np.zeros((128, 4, 128), np.float32)
    mm = np.zeros((128, 4, 128), np.float32)
    for r in range(4):
        if r > j:
            mf[:, r, :] = NEG
            mm[:, r, :] = NEG
        elif r == j:
            mf[:, r, :] = np.where(pk <= pq, 0.0, NEG)
            mm[:, r, :] = np.where(pk // 64 <= pq // 64, 0.0, NEG)
    return sel, invc, mf, mm


def mask01_of(m):
    return (m == 0.0).astype(np.float32)


def lay_kmajor(w):
    K, N = w.shape
    return np.ascontiguousarray(w.reshape(K // 128, 128, N).transpose(1, 0, 2))


def tok_perm(j, nb):
    i = np.arange(nb)[:, None]
    p = np.arange(128)[None, :]
    return ((4 * i + j) * 128 + p).reshape(-1)


MH = 16
MLA_IN = 416
QSCALE = 96.0 ** -0.5
PI = float(np.pi)


def rope_phase(P, C, pos_d, invf, CS_d, CS_res, NT, name="rp"):
    CH = 2048 if NT >= 2048 else NT
    C1 = 6.28125
    C2 = 2.0 * PI - C1
    with ExitStack() as st:
        pi_ = sb(P, st, f"{name}_pi", [128, CH], I32)
        ang = sb(P, st, f"{name}_ang", [128, CH], F32)
        tt = sb(P, st, f"{name}_t", [128, CH], F32)
        m = sb(P, st, f"{name}_m", [128, CH], F32)
        o = [sb(P, st, f"{name}_o{i}", [128, CH], F32) for i in range(2)]
        r = Res()
        o_r = RL(2)
        halfpi = sb(P, st, f"{name}_hpi", [128, 1], F32)
        P.op("dve", I("memset", halfpi[:], PI / 2), writes=[r])
        for c in range(NT // CH):
            sl = slice(c * CH, (c + 1) * CH)
            P.dma("sp", P.S("ld_misc"), I("dma_start", out=pi_[:], in_=pos_d[:, sl]), writes=[r])
            P.op("dve", I("tensor_copy", ang[:], pi_[:]), reads=[r], writes=[r])
            P.op("dve", I("tensor_scalar", ang[:], ang[:], invf[:, 0:1], None, ALU.mult), reads=[r], writes=[r])
            P.op("dve", I("tensor_scalar", tt[:], ang[:], 1.0 / (2.0 * PI), None, ALU.mult), reads=[r], writes=[r])
            P.op("dve", I("tensor_copy", pi_[:], tt[:]), reads=[r], writes=[r])
            P.op("dve", I("tensor_copy", tt[:], pi_[:]), reads=[r], writes=[r])
            P.op("dve", I("scalar_tensor_tensor", m[:], tt[:], -C1, ang[:], ALU.mult, ALU.add), reads=[r], writes=[r])
            P.op("dve", I("scalar_tensor_tensor", m[:], tt[:], -C2, m[:], ALU.mult, ALU.add), reads=[r], writes=[r])
            P.op("dve", I("tensor_scalar", m[:], m[:], -PI, PI, ALU.max, ALU.min), reads=[r], writes=[r])
            P.op("act", I("activation", out=o[1][:], in_=m[:], func=AF.Sin), reads=[r], writes=[o_r[1]])
            P.dma("sp", P.S("st_rp1"), I("dma_start", out=CS_d[1, :, sl], in_=o[1][:]), reads=[o_r[1]], pwrites=[CS_res])
            P.op("dve", I("tensor_scalar", tt[:], m[:], -1.0, None, ALU.mult), reads=[r], writes=[r])
            P.op("dve", I("tensor_tensor", tt[:], tt[:], m[:], ALU.max), reads=[r], writes=[r])
            P.op("act", I("activation", out=o[0][:], in_=tt[:], func=AF.Sin, scale=-1.0, bias=halfpi[:, 0:1]), reads=[r], writes=[o_r[0]])
            P.dma("sp", P.S("st_rp0"), I("dma_start", out=CS_d[0, :, sl], in_=o[0][:]), reads=[o_r[0]], pwrites=[CS_res])
        P.barrier()


def oddin_phase(P, C, hin, hin_res, w_in_d, wqb_d, wkvb_d, rot_d, qn_g, kvn_g, gains, gidx, CS_d, CS_res,
                QT_d, agKn_in, agKr_in, agV_in, out_res, NT, TN=512, name="oin"):
    nt = NT // TN
    nb = TN // 128
    with ExitStack() as st:
        w = sb(P, st, f"{name}_w", [128, KC, MLA_IN], BF16)
        wq = sb(P, st, f"{name}_wq", [128, 2, 1536], BF16)
        wkv = sb(P, st, f"{name}_wkv", [128, 2048], BF16)
        w_r = Res()
        P.dma("pool", P.S("ld_w0"), [I("dma_start", out=w[:], in_=w_in_d), I("dma_start", out=wq[:], in_=wqb_d),
                                     I("dma_start", out=wkv[:], in_=wkvb_d)], writes=[w_r])
        wqr = sb(P, st, f"{name}_wqr", [128, 2, 512], BF16)
        wkr = sb(P, st, f"{name}_wkr", [128, KC, 32], BF16)
        wr_r = Res()
        qv = wq[:, :, 1024:1536].rearrange("p j (h t x) -> p j h t x", t=2, x=16)
        qrv = wqr[:].rearrange("p j (h t x) -> p j h t x", t=2, x=16)
        for j in range(2):
            P.op("dve", I("tensor_scalar", qrv[:, j, :, 0, :], qv[:, j, :, 1, :], -1.0, None, ALU.mult), reads=[w_r], pwrites=[wr_r])
            P.op("dve", I("tensor_copy", qrv[:, j, :, 1, :], qv[:, j, :, 0, :]), reads=[w_r], pwrites=[wr_r])
        P.op("dve", I("tensor_scalar", wkr[:, :, 0:16], w[:, :, 400:416], -1.0, None, ALU.mult), reads=[w_r], pwrites=[wr_r])
        P.op("dve", I("tensor_copy", wkr[:, :, 16:32], w[:, :, 384:400]), reads=[w_r], pwrites=[wr_r])
        hb = [sb(P, st, f"{name}_hb{i}", [128, KC, TN], F32) for i in range(2)]
        hb_r = [RL(KC) for _ in range(2)]
        xn = [sb(P, st, f"{name}_xn{i}", [128, KC, TN], BF16) for i in range(2)]
        xn_r = [RL(KC) for _ in range(2)]
        cs = [sb(P, st, f"{name}_cs{i}", [128, 2, TN], F32) for i in range(2)]
        cs_r = RL(2)
        sq2 = [sb(P, st, f"{name}_sq{i}", [128, KC, TN], BF16) for i in range(2)]
        sq2_r = [RL(KC) for _ in range(2)]
        rstd2 = [sb(P, st, f"{name}_rstd{i}", [128, TN], F32) for i in range(2)]
        rstd2_r = RL(2)
        cl2 = [sb(P, st, f"{name}_cl{i}", [128, 3, TN], F32) for i in range(2)]
        cl2_r = [RL(3) for _ in range(2)]
        cn2 = [sb(P, st, f"{name}_cn{i}", [128, 3, TN], BF16) for i in range(2)]
        cn2_r = [RL(3) for _ in range(2)]
        NE = 6
        eb = [sb(P, st, f"{name}_e{i}", [128, TN], BF16) for i in range(NE)]
        eb_r = RL(NE)
        t1s = [sb(P, st, f"{name}_t1{i}", [128, TN], F32) for i in range(2)]
        t2s = [sb(P, st, f"{name}_t2{i}", [128, TN], F32) for i in range(2)]
        ts_r = RL(2)
        vb = [sb(P, st, f"{name}_v{i}", [128, MH, nb, 65], BF16) for i in range(2)]
        vb_r = RL(2)
        s_h = [P.S(f"ld_h{i}") for i in range(2)]
        s_cs = [P.S(f"ld_cs{i}") for i in range(2)]
        s_e = [P.S(f"st_qk{i}") for i in range(NE)]
        s_v = [P.S(f"st_v{i}") for i in range(2)]
        gl = [gains[:, gidx, kc:kc + 1] for kc in range(KC)]
        for i in range(2):
            P.op("dve", I("memset", vb[i][:, :, :, 64:65], 1.0), writes=[vb_r[i]])

        def load_h(t, what="hc"):
            s = t % 2
            if "h" in what:
                P.dma("pool", s_h[s], I("dma_start", out=hb[s][:], in_=tview(hin, t, TN)), reads=[hin_res[t]], writes=hb_r[s])
            if "c" in what:
                P.dma("pool", s_cs[s], I("dma_start", out=cs[s][:], in_=CS_d[:, :, t * TN:(t + 1) * TN].rearrange("w p n -> p w n")),
                      reads=[CS_res], writes=[cs_r[s]])

        def norm(t):
            s = t % 2
            emit_norm(P, C, [hb[s][:, kc, :] for kc in range(KC)], hb_r[s], gl,
                      [sq2[s][:, kc, :] for kc in range(KC)], sq2_r[s], C.ps[6 + s][:, :TN], C.psr[6 + s], rstd2[s][:], rstd2_r[s],
                      [xn[s][:, kc, :] for kc in range(KC)], xn_r[s])

        load_h(0)
        cnt = dict(b0=0, b1=0, e0=0, e1=0)

        def nbank(s):
            k = cnt[f"b{s}"]
            cnt[f"b{s}"] += 1
            return 3 * s + (k % 3)

        def neb(s):
            k = cnt[f"e{s}"]
            cnt[f"e{s}"] += 1
            return 3 * s + (k % 3)

        def rope_apply(bq, br, M, scale, s, dst_bf, dst_res):
            t1, t2, t_r = t1s[s], t2s[s], ts_r[s]
            P.op("dve", I("scalar_tensor_tensor", t1[:M, :], C.ps[bq][:M, :TN], scale, cs[s][:M, 0, :], ALU.mult, ALU.mult),
                 reads=[C.psr[bq], cs_r[s]], writes=[t_r])
            P.op("dve", I("scalar_tensor_tensor", t2[:M, :], C.ps[br][:M, :TN], scale, cs[s][:M, 1, :], ALU.mult, ALU.mult),
                 reads=[C.psr[br], cs_r[s], t_r], writes=[t_r])
            P.op("dve", I("tensor_tensor", dst_bf, t1[:M, :], t2[:M, :], ALU.add), reads=[t_r], writes=[dst_res])

        def partA(t):
            s = t % 2
            cn, cn_r = cn2[s], cn2_r[s]
            cl, cl_r, sq, sq_r, rstd, rstd_r = cl2[s], cl2_r[s], sq2[s], sq2_r[s], rstd2[s], rstd2_r[s]
            nb6 = 6 + s
            norm(t)
            for j, (c0, M) in enumerate(((0, 128), (128, 128), (256, 128))):
                bank = nbank(s)
                P.op("pe", [I("matmul", C.ps[bank][:M, :TN], w[:, kc, c0:c0 + M], xn[s][:, kc, :],
                              start=(kc == 0), stop=(kc == KC - 1)) for kc in range(KC)],
                     reads=[w_r] + xn_r[s], writes=[C.psr[bank]])
                P.op("act", I("activation", out=cl[:, j, :], in_=C.ps[bank][:, :TN], func=AF.Copy),
                     reads=[C.psr[bank]], writes=[cl_r[j]])
            emit_norm(P, C, [cl[:, j, :] for j in range(2)], cl_r[0:2], [qn_g[:, j:j + 1] for j in range(2)],
                      [sq[:, j, :] for j in range(2)], sq_r, C.ps[nb6][:, :TN], C.psr[nb6], rstd[:], rstd_r,
                      [cn[:, j, :] for j in range(2)], cn_r[0:2], nkc=2, dim=256)
            emit_norm(P, C, [cl[:, 2, :]], cl_r[2:3], [kvn_g[:, 0:1]],
                      [sq[:, 2, :]], sq_r[2:3], C.ps[nb6][:, :TN], C.psr[nb6], rstd[:], rstd_r,
                      [cn[:, 2, :]], cn_r[2:3], nkc=1, dim=128)

        def partB(t):
            s = t % 2
            cn, cn_r = cn2[s], cn2_r[s]
            tok = slice(t * TN, (t + 1) * TN)
            bank = nbank(s)
            bank2 = nbank(s)
            P.op("pe", [I("matmul", C.ps[bank][:32, :TN], w[:, kc, 384:416], xn[s][:, kc, :],
                          start=(kc == 0), stop=(kc == KC - 1)) for kc in range(KC)],
                 reads=[w_r] + xn_r[s], writes=[C.psr[bank]])
            P.op("pe", [I("matmul", C.ps[bank2][:32, :TN], wkr[:, kc, :], xn[s][:, kc, :],
                          start=(kc == 0), stop=(kc == KC - 1)) for kc in range(KC)],
                 reads=[wr_r] + xn_r[s], writes=[C.psr[bank2]])
            e = neb(s)
            rope_apply(bank, bank2, 32, 1.0, s, eb[e][0:32, :], eb_r[e])
            P.dma("sp", s_e[e], I("dma_start", out=agKr_in[:, tok], in_=eb[e][0:32, :]), reads=[eb_r[e]], pwrites=[out_res["agKr"]])
            for m in range(4):
                bank = nbank(s)
                e = neb(s)
                bank2 = nbank(s)
                P.op("pe", [I("matmul", C.ps[bank][:, :TN], wq[:, j, 1024 + m * 128:1024 + (m + 1) * 128], cn[:, j, :],
                              start=(j == 0), stop=(j == 1)) for j in range(2)],
                     reads=[w_r] + cn_r[0:2], writes=[C.psr[bank]])
                P.op("pe", [I("matmul", C.ps[bank2][:, :TN], wqr[:, j, m * 128:(m + 1) * 128], cn[:, j, :],
                              start=(j == 0), stop=(j == 1)) for j in range(2)],
                     reads=[wr_r] + cn_r[0:2], writes=[C.psr[bank2]])
                rope_apply(bank, bank2, 128, QSCALE, s, eb[e][:], eb_r[e])
                P.dma("sp", s_e[e], [I("dma_start", out=QT_d[4 * m + hh, 64:96, tok], in_=eb[e][hh * 32:(hh + 1) * 32, :]) for hh in range(4)],
                      reads=[eb_r[e]], pwrites=[out_res["q"]])
            for m in range(8):
                bank = nbank(s)
                e = neb(s)
                P.op("pe", [I("matmul", C.ps[bank][:, :TN], wq[:, j, m * 128:(m + 1) * 128], cn[:, j, :],
                              start=(j == 0), stop=(j == 1)) for j in range(2)],
                     reads=[w_r] + cn_r[0:2], writes=[C.psr[bank]])
                P.op("act", I("activation", out=eb[e][:], in_=C.ps[bank][:, :TN], func=AF.Copy, scale=QSCALE),
                     reads=[C.psr[bank]], writes=[eb_r[e]])
                P.dma("sp", s_e[e], [I("dma_start", out=QT_d[2 * m + hh, 0:64, tok], in_=eb[e][hh * 64:(hh + 1) * 64, :]) for hh in range(2)],
                      reads=[eb_r[e]], pwrites=[out_res["q"]])
            for m in range(8):
                bank = nbank(s)
                e = neb(s)
                P.op("pe", I("matmul", C.ps[bank][:, :TN], wkv[:, m * 128:(m + 1) * 128], cn[:, 2, :], start=True, stop=True),
                     reads=[w_r, cn_r[2]], writes=[C.psr[bank]])
                P.op("act", I("activation", out=eb[e][:], in_=C.ps[bank][:, :TN], func=AF.Copy),
                     reads=[C.psr[bank]], writes=[eb_r[e]])
                P.dma("sp", s_e[e], I("dma_start", out=agKn_in[m][:, tok], in_=eb[e][:]), reads=[eb_r[e]], pwrites=[out_res["agKn"][m]])
            vs = t % 2
            for blk in range(nb):
                for half in range(2):
                    bank = nbank(s)
                    P.op("pe", I("matmul", C.ps[bank][:, :512], cn[:, 2, blk * 128:(blk + 1) * 128],
                                 wkv[:, 1024 + half * 512:1024 + (half + 1) * 512], start=True, stop=True),
                         reads=[w_r, cn_r[2]], writes=[C.psr[bank]])
                    src = C.ps[bank][:, :512].rearrange("p (h d) -> p h d", d=64)
                    dst = vb[vs][:, half * 8:(half + 1) * 8, blk, 0:64]
                    if half == 0:
                        P.op("act", I("activation", out=dst, in_=src, func=AF.Copy), reads=[C.psr[bank]], pwrites=[vb_r[vs]])
                    else:
                        P.op("dve", I("tensor_copy", dst, src), reads=[C.psr[bank]], pwrites=[vb_r[vs]])
            P.dma("sp", s_v[vs], [I("dma_start", out=agV_in[h][:, t * nb * 65:(t + 1) * nb * 65],
                                    in_=vb[vs][:, h, :, :].rearrange("p i c -> p (i c)")) for h in range(MH)],
                  reads=[vb_r[vs]], pwrites=out_res["agV"])

        def tile_full(t):
            partA(t)
            partB(t)

        load_h(1) if nt > 1 else None
        for t0 in range(0, nt, 2):
            L0 = P.capture(lambda: tile_full(t0))
            L1 = P.capture(lambda: tile_full(t0 + 1)) if t0 + 1 < nt else []
            import os as _os
            if _os.environ.get("SEQ"):
                P.replay(L0)
                P.replay(L1)
                continue
            if _os.environ.get("CHECKIL"):
                lastw = {}
                for k in range(max(len(L0), len(L1))):
                    for tid, L in ((0, L0), (1, L1)):
                        if k >= len(L):
                            continue
                        kind, a_, kw_ = L[k]
                        for R in kw_["reads"]:
                            if id(R) in lastw and lastw[id(R)][0] != tid:
                                print("INTERLEAVE VIOLATION: tile", tid, "op", k, kind, a_[0], "reads", R.name, "last written by tile", lastw[id(R)], "instr", (a_[1] if kind == "op" else a_[2])[0][0] if not isinstance((a_[1] if kind == "op" else a_[2])[0], str) else (a_[1] if kind == "op" else a_[2])[0])
                        for R in list(kw_["writes"]) + list(kw_["pwrites"]):
                            lastw[id(R)] = (tid, k)
            for k in range(max(len(L0), len(L1))):
                if k < len(L0):
                    P.replay([L0[k]])
                if k < len(L1):
                    P.replay([L1[k]])
                if k == 60:
                    for tt in (t0 + 2, t0 + 3):
                        if tt < nt:
                            load_h(tt, "h")
            for tt in (t0 + 2, t0 + 3):
                if tt < nt:
                    load_h(tt, "c")
        P.barrier()


def mla_loader(P, agKn, agKn_res, agKr, agKr_res, QT_d, q_res, agV, agV_res, NT):
    agKv = [a.rearrange("(r h d) n -> h d r n", r=4, d=64) for a in agKn]
    agKrv = agKr.rearrange("(r d) n -> d r n", r=4)
    agVv = [a.rearrange("(r p) (i c) -> p r i c", r=4, c=65) for a in agV]

    def load_head(h, KT, QT, V, sems, res, first):
        P.dma("pool", sems["KT"], [I("dma_start", out=KT[0:64, :, :], in_=agKv[h // 2][h % 2]),
                                 I("dma_start", out=KT[64:96, :, :], in_=agKrv)],
              reads=[agKn_res[h // 2], agKr_res], writes=[res["KT"]])
        P.dma("pool", sems["QT"], I("dma_start", out=QT[:, :], in_=QT_d[h]), reads=[q_res], writes=[res["QT"]])
        P.dma("pool", sems["V"], I("dma_start", out=V[:], in_=agVv[h]), reads=[agV_res[h]], writes=[res["V"]])
    return load_head


def rope_consts():
    inv = (np.float32(10000.0) ** (-np.arange(0, 32, 2, dtype=np.float32) / np.float32(32))).astype(np.float32)
    invf = np.zeros((128, 1), np.float32)
    rot = np.zeros((128, 128), np.float32)
    for p in range(128):
        invf[p, 0] = inv[(p % 32) % 16]
        if p % 32 < 16:
            rot[p + 16, p] = -1.0
        else:
            rot[p - 16, p] = 1.0
    return invf, rot


def lay_wqb(wqb):
    w = wqb.reshape(256, 16, 96)
    w2 = np.concatenate([w[:, :, :64].reshape(256, 1024), w[:, :, 64:].reshape(256, 512)], axis=1)
    return np.ascontiguousarray(w2.reshape(2, 128, 1536).transpose(1, 0, 2))


def lay_wkvb(wkvb):
    w = wkvb.reshape(128, 16, 128)
    return np.ascontiguousarray(np.concatenate([w[:, :, :64].reshape(128, 1024), w[:, :, 64:].reshape(128, 1024)], axis=1))


def ffn_phase2(P, C, stages, hin0, hin0_res, hT, h_res, gains, NT, TN=1024, name="ffn", D_PF=2, final=None):
    nt = NT // TN
    NH = TN // 512
    jobs = [(s, t) for s in range(len(stages)) for t in range(nt)]
    with ExitStack() as st:
        hb = [sb(P, st, f"{name}_hb{i}", [128, KC, TN], F32) for i in range(2)]
        hb_r = [[RL(KC, "hb") for _ in range(NH)] for _ in range(2)]
        xn = [sb(P, st, f"{name}_xn{i}", [128, KC, TN], BF16) for i in range(2)]
        xn_r = [[RL(KC, "xn") for _ in range(NH)] for _ in range(2)]
        h1 = sb(P, st, f"{name}_h1", [128, FC, TN], BF16)
        h1_r = [RL(FC, "h1") for _ in range(NH)]
        NSG = 3
        sg = [sb(P, st, f"{name}_sg{i}", [128, 512], F32) for i in range(NSG)]
        sg_r = RL(NSG, "sg")
        NW = 3
        wg_sb = [sb(P, st, f"{name}_wgu{i}", [128, 2, KC, 128], BF16) for i in range(NW)]
        wg_r = RL(NW, "wgu")
        wd_sb = [sb(P, st, f"{name}_wd{i}", [128, FC, 128], BF16) for i in range(NW)]
        wd_r = RL(NW, "wd")
        s_h = [P.S(f"ld_h{i}") for i in range(2)]
        s_st = [P.S(f"st_h{i}") for i in range(2)]
        s_wg = [P.S(f"ld_wg{i}") for i in range(NW)]
        s_wd = [P.S(f"ld_wd{i}") for i in range(NW)]
        psG, psU, psD, psN = [0, 1], [2, 3], [4, 5], 6

        uses = []
        for j, (s_, t) in enumerate(jobs):
            for fc in range(FC):
                uses.append(("g", j, fc))
            for dc in range(KC):
                uses.append(("d", j, dc))
        slot_of = []
        cg = cd = 0
        for (kind, j, idx) in uses:
            if kind == "g":
                slot_of.append(cg % NW)
                cg += 1
            else:
                slot_of.append(cd % NW)
                cd += 1

        def issue_load(k):
            if k >= len(uses):
                return
            kind, j, idx = uses[k]
            w = slot_of[k]
            wgu_d, wd_d, _ = stages[jobs[j][0]]
            if kind == "g":
                P.dma("pool", s_wg[w], I("dma_start", out=wg_sb[w][:], in_=wgu_d[idx]), writes=[wg_r[w]])
            else:
                P.dma("pool", s_wd[w], I("dma_start", out=wd_sb[w][:], in_=wd_d[idx]), writes=[wd_r[w]])

        def load_h(j):
            s_, t = jobs[j]
            sl = j % 2
            src, src_res = (hin0, hin0_res) if s_ == 0 else (hT, h_res)
            P.dma("sp", s_h[sl], I("dma_start", out=hb[sl][:], in_=tview(src, t, TN)),
                  reads=src_res[t * NH:(t + 1) * NH], writes=[r for hh in range(NH) for r in hb_r[sl][hh]])

        sqf = sb(P, st, f"{name}_sqf", [128, KC, TN], BF16)
        sqf_r = [RL(KC, "sqf") for _ in range(NH)]
        rstdf = sb(P, st, f"{name}_rstdf", [128, TN], F32)
        rstdf_r = RL(NH, "rstdf")
        psNb = [6, 7]

        def norm_sq(j):
            sl = j % 2
            for hh in range(NH):
                c = slice(hh * 512, (hh + 1) * 512)
                for kc in range(KC):
                    P.op("dve", I("tensor_tensor", sqf[:, kc, c], hb[sl][:, kc, c], hb[sl][:, kc, c], ALU.mult),
                         reads=[hb_r[sl][hh][kc]], writes=[sqf_r[hh][kc]])

        def norm_rest(j):
            sl = j % 2
            gidx = stages[jobs[j][0]][2]
            for hh in range(NH):
                c = slice(hh * 512, (hh + 1) * 512)
                bank = psNb[hh % 2]
                P.op("pe", [I("matmul", C.ps[bank][:, :512], C.ones_bf[:], sqf[:, kc, c], start=(kc == 0), stop=(kc == KC - 1))
                            for kc in range(KC)], reads=[C.r_const] + sqf_r[hh], writes=[C.psr[bank]])
            for hh in range(NH):
                c = slice(hh * 512, (hh + 1) * 512)
                bank = psNb[hh % 2]
                P.op("dve", I("tensor_scalar", rstdf[:, c], C.ps[bank][:, :512], 1.0 / D, EPS, ALU.mult, ALU.add),
                     reads=[C.psr[bank]], writes=[rstdf_r[hh]])
            for hh in range(NH):
                c = slice(hh * 512, (hh + 1) * 512)
                P.op("act", I("activation", out=rstdf[:, c], in_=rstdf[:, c], func=AF.Sqrt), reads=[rstdf_r[hh]], writes=[rstdf_r[hh]])
            for hh in range(NH):
                c = slice(hh * 512, (hh + 1) * 512)
                P.op("dve", I("reciprocal", rstdf[:, c], rstdf[:, c]), reads=[rstdf_r[hh]], writes=[rstdf_r[hh]])
                for kc in range(KC):
                    P.op("dve", I("scalar_tensor_tensor", xn[sl][:, kc, c], hb[sl][:, kc, c], gains[:, gidx, kc:kc + 1], rstdf[:, c],
                                  ALU.mult, ALU.mult), reads=[hb_r[sl][hh][kc], rstdf_r[hh]], writes=[xn_r[sl][hh][kc]])

        def norm(j):
            norm_sq(j)
            norm_rest(j)

        def emit_final(j):
            gfin, out_d, out_res = final
            s_, t = jobs[j]
            sl = j % 2
            for hh in range(NH):
                c = slice(hh * 512, (hh + 1) * 512)
                for kc in range(KC):
                    P.op("dve", I("tensor_tensor", sqf[:, kc, c], hb[sl][:, kc, c], hb[sl][:, kc, c], ALU.mult),
                         reads=[hb_r[sl][hh][kc]], writes=[sqf_r[hh][kc]])
            for hh in range(NH):
                c = slice(hh * 512, (hh + 1) * 512)
                bank = psNb[hh % 2]
                P.op("pe", [I("matmul", C.ps[bank][:, :512], C.ones_bf[:], sqf[:, kc, c], start=(kc == 0), stop=(kc == KC - 1))
                            for kc in range(KC)], reads=[C.r_const] + sqf_r[hh], writes=[C.psr[bank]])
            for hh in range(NH):
                c = slice(hh * 512, (hh + 1) * 512)
                bank = psNb[hh % 2]
                P.op("dve", I("tensor_scalar", rstdf[:, c], C.ps[bank][:, :512], 1.0 / D, EPS, ALU.mult, ALU.add),
                     reads=[C.psr[bank]], writes=[rstdf_r[hh]])
            for hh in range(NH):
                c = slice(hh * 512, (hh + 1) * 512)
                P.op("act", I("activation", out=rstdf[:, c], in_=rstdf[:, c], func=AF.Sqrt), reads=[rstdf_r[hh]], writes=[rstdf_r[hh]])
            for hh in range(NH):
                c = slice(hh * 512, (hh + 1) * 512)
                P.op("dve", I("reciprocal", rstdf[:, c], rstdf[:, c]), reads=[rstdf_r[hh]], writes=[rstdf_r[hh]])
                for kc in range(KC):
                    P.op("dve", I("scalar_tensor_tensor", hb[sl][:, kc, c], hb[sl][:, kc, c], gains[:, gfin, kc:kc + 1], rstdf[:, c],
                                  ALU.mult, ALU.mult), reads=[hb_r[sl][hh][kc], rstdf_r[hh]], writes=[hb_r[sl][hh][kc]])
            P.dma("sp", s_st[sl], I("dma_start", out=tview(out_d, t, TN), in_=hb[sl][:]),
                  reads=[r for hh in range(NH) for r in hb_r[sl][hh]], writes=out_res[t * NH:(t + 1) * NH])

        last_stage = len(stages) - 1
        pend_final = [None]
        for k in range(D_PF):
            issue_load(k)
        load_h(0)
        norm(0)
        gi = 0
        di = 0
        k = 0
        for j, (s_, t) in enumerate(jobs):
            sl = j % 2
            defer_load = final is not None and pend_final[0] is not None
            if j + 1 < len(jobs) and not defer_load:
                load_h(j + 1)
            for fc in range(FC):
                if fc == 1 and defer_load:
                    emit_final(pend_final[0])
                    pend_final[0] = None
                    if j + 1 < len(jobs):
                        load_h(j + 1)
                issue_load(k + D_PF)
                w = slot_of[k]
                k += 1
                for hh in range(NH):
                    c = slice(hh * 512, (hh + 1) * 512)
                    b = gi % 2
                    q = gi % NSG
                    gi += 1
                    for (gu, bank) in ((0, psG[b]), (1, psU[b])):
                        P.op("pe", [I("matmul", C.ps[bank][:, :512], wg_sb[w][:, gu, kc, :], xn[sl][:, kc, c],
                                      start=(kc == 0), stop=(kc == KC - 1)) for kc in range(KC)],
                             reads=[wg_r[w]] + xn_r[sl][hh], writes=[C.psr[bank]])
                    P.op("act", I("activation", out=sg[q][:], in_=C.ps[psG[b]][:, :512], func=AF.Silu),
                         reads=[C.psr[psG[b]]], writes=[sg_r[q]])
                    P.op("dve", I("tensor_tensor", h1[:, fc, c], sg[q][:], C.ps[psU[b]][:, :512], ALU.mult),
                         reads=[sg_r[q], C.psr[psU[b]]], writes=[h1_r[hh][fc]])
                if fc == (7 if defer_load else 5) and j + 1 < len(jobs):
                    norm_sq(j + 1)
                if fc == 11 and j + 1 < len(jobs):
                    norm_rest(j + 1)
            for dc in range(KC):
                issue_load(k + D_PF)
                w = slot_of[k]
                k += 1
                for hh in range(NH):
                    c = slice(hh * 512, (hh + 1) * 512)
                    b = di % 2
                    di += 1
                    P.op("pe", [I("matmul", C.ps[psD[b]][:, :512], wd_sb[w][:, fc, :], h1[:, fc, c],
                                  start=(fc == 0), stop=(fc == FC - 1)) for fc in range(FC)],
                         reads=[wd_r[w]] + h1_r[hh], writes=[C.psr[psD[b]]])
                    P.op("dve", I("scalar_tensor_tensor", hb[sl][:, dc, c], C.ps[psD[b]][:, :512], 0.5, hb[sl][:, dc, c],
                                  ALU.mult, ALU.add),
                         reads=[C.psr[psD[b]], hb_r[sl][hh][dc]], writes=[hb_r[sl][hh][dc]])
            if final is not None and s_ == last_stage:
                pend_final[0] = j
            else:
                P.dma("sp", s_st[sl], I("dma_start", out=tview(hT, t, TN), in_=hb[sl][:]),
                      reads=[r for hh in range(NH) for r in hb_r[sl][hh]], writes=h_res[t * NH:(t + 1) * NH])
        if pend_final[0] is not None:
            emit_final(pend_final[0])
        P.barrier()


def attn_phase2(P, C, nh, KR, load_head, mask, ident, yT_d, y_res, ychunk0, NT, name="at", LA=2, side_fn=None, side_pull=4, side_start=6, mask01=None):
    nb = NT // 128
    nt = NT // 512
    with ExitStack() as st:
        KT = [sb(P, st, f"{name}_KT{i}", [KR, 4, NT], BF16) for i in range(2)]
        QT = [sb(P, st, f"{name}_QT{i}", [KR, NT], BF16) for i in range(2)]
        V = [sb(P, st, f"{name}_V{i}", [128, 4, nb, 65], BF16) for i in range(2)]
        hd_r = [dict(KT=Res(), QT=Res(), V=Res()) for _ in range(2)]
        NS = LA + 1
        assert NS <= 3
        NP = LA + 2
        pT = [sb(P, st, f"{name}_pT{i}", [128, 2, 512], BF16) for i in range(NP)]
        pT_r = RL(NP)
        osb = [sb(P, st, f"{name}_o{i}", [65, 512], F32) for i in range(2)]
        osb_r = RL(2)
        rec = [sb(P, st, f"{name}_rc{i}", [65, 512], F32) for i in range(2)]
        rec_r = RL(2)
        ysb = [sb(P, st, f"{name}_y{i}", [64, 512], BF16) for i in range(2)]
        ysb_r = RL(2)
        s_y = [P.S(f"st_ay{i}") for i in range(2)]
        psO = [6, 7]
        sems = [dict(KT=P.S(f"ld_KT{i}"), QT=P.S(f"ld_QT{i}"), V=P.S(f"ld_V{i}")) for i in range(2)]
        S3 = [C.ps2[k][:, :].rearrange("p (b n) -> p b n", b=2) for k in range(NS)]
        st_ = dict(si=0, pi=0, fin=0, pending=None)

        def slot_res(k):
            return [C.psr[2 * k], C.psr[2 * k + 1]]

        def flush_pending():
            pd = st_["pending"]
            if pd is None:
                return
            st_["pending"] = None
            f, h, q0 = pd
            k = st_["si"] % NS
            st_["si"] += 1
            P.op("pe", I("matmul", C.ps[2 * k][:64, :512], C.ones_f[64:65, 0:64], rec[f][64:65, :], start=True, stop=True),
                 reads=[rec_r[f], C.r_const], writes=slot_res(k))
            P.op("dve", I("tensor_tensor", ysb[f][:], osb[f][0:64, :], C.ps[2 * k][:64, :512], ALU.mult),
                 reads=[osb_r[f]] + slot_res(k), writes=[ysb_r[f]])
            ch = ychunk0 + h // 2
            p0 = (h % 2) * 64
            P.dma("sp", s_y[f], I("dma_start", out=yT_d[ch, p0:p0 + 64, q0:q0 + 512], in_=ysb[f][:]),
                  reads=[ysb_r[f]], pwrites=[y_res])

        side = list(side_fn(st)) if side_fn is not None else []
        side_pos = [0]

        def pump_side(n):
            for _ in range(n):
                if side_pos[0] >= len(side):
                    return
                kind, a, kw = side[side_pos[0]]
                side_pos[0] += 1
                if kind == "gap":
                    return
                if kind == "slotmm":
                    k = st_["si"] % NS
                    st_["si"] += 1
                    P.op("pe", I("matmul", C.ps[2 * k][:, :512], a["lhsT"], a["rhs"], start=True, stop=True),
                         reads=a["mm_reads"], writes=slot_res(k))
                    P.op("dve", a["evac"](C.ps[2 * k][:, :512]), reads=slot_res(k), writes=a["ev_writes"], pwrites=a["ev_pwrites"])
                    return
                else:
                    getattr(P, kind)(*a, **kw)
                    if side_pos[0] < len(side) and side[side_pos[0]][0] == "slotmm":
                        return

        load_head(0, KT[0], QT[0], V[0], sems[0], hd_r[0], True)
        for h in range(nh):
            hb = h % 2
            if h + 1 < nh:
                load_head(h + 1, KT[1 - hb], QT[1 - hb], V[1 - hb], sems[1 - hb], hd_r[1 - hb], h + 1 < 2)
            kt, qt, v, hr = KT[hb], QT[hb], V[hb], hd_r[hb]
            for T in range(nt):
                if h * nt + T >= side_start:
                    pump_side(side_pull)
                pairs = []
                full = [(r, i) for r in range(4) for i in range(4 * T)]
                for a in range(0, len(full), 2):
                    pairs.append((0, [(full[a][0], full[a][1], None), (full[a + 1][0], full[a + 1][1], None)]))
                for qp in range(4):
                    for r0 in (0, 2):
                        pairs.append((qp * 128, [(r0, 4 * T + qp, r0), (r0 + 1, 4 * T + qp, r0 + 1)]))
                ob = psO[st_["fin"] % 2]
                npair = len(pairs)
                q0 = T * 512
                slots = {}

                def emit_S(pi_):
                    c0, blks = pairs[pi_]
                    k = st_["si"] % NS
                    st_["si"] += 1
                    p = st_["pi"] % NP
                    st_["pi"] += 1
                    slots[pi_] = (k, p)
                    ins = []
                    for half, (r, i, mk) in enumerate(blks):
                        ins.append(I("matmul", S3[k][:, half, c0:512], kt[:, r, i * 128:(i + 1) * 128], qt[:, q0 + c0:q0 + 512],
                                     start=True, stop=(mk is None)))
                        if mk is not None and mask01 is None:
                            ins.append(I("matmul", S3[k][:, half, c0:c0 + 128], ident[:], mask[:, mk, :], start=False, stop=True))
                    if mask01 is not None:
                        ins = [(nm, ar, dict(kw_, stop=True)) for (nm, ar, kw_) in ins]
                    P.op("pe", ins, reads=[hr["KT"], hr["QT"], C.r_const], writes=slot_res(k))
                    P.op("act", I("activation", out=pT[p][:, :, c0:512], in_=S3[k][:, :, c0:512], func=AF.Exp),
                         reads=slot_res(k), writes=[pT_r[p]])
                    if mask01 is not None:
                        for half, (r, i, mk) in enumerate(blks):
                            if mk is not None:
                                P.op("dve", I("tensor_tensor", pT[p][:, half, c0:c0 + 128], pT[p][:, half, c0:c0 + 128], mask01[:, mk, :], ALU.mult),
                                     reads=[pT_r[p], C.r_const], writes=[pT_r[p]])

                def emit_PV(pi_):
                    c0, blks = pairs[pi_]
                    k, p = slots[pi_]
                    ins = []
                    for half, (r, i, mk) in enumerate(blks):
                        ins.append(I("matmul", C.ps[ob][:65, c0:512], v[:, r, i, :], pT[p][:, half, c0:512],
                                     start=(pi_ == 0 and half == 0), stop=(pi_ == npair - 1 and half == 1)))
                    P.op("pe", ins, reads=[hr["V"], pT_r[p]], writes=[C.psr[ob]] if pi_ == 0 else (), pwrites=() if pi_ == 0 else [C.psr[ob]])

                for pi_ in range(min(LA, npair)):
                    emit_S(pi_)
                for pi_ in range(npair):
                    if pi_ + LA < npair:
                        emit_S(pi_ + LA)
                    emit_PV(pi_)
                    if pi_ == 2:
                        flush_pending()
                flush_pending()
                f = st_["fin"] % 2
                st_["fin"] += 1
                P.op("act", I("activation", out=osb[f][:], in_=C.ps[ob][:65, :512], func=AF.Copy), reads=[C.psr[ob]], writes=[osb_r[f]])
                P.op("dve", I("reciprocal", rec[f][64:65, :], osb[f][64:65, :]), reads=[osb_r[f]], writes=[rec_r[f]])
                st_["pending"] = (f, h, q0)
        flush_pending()
        while side_pos[0] < len(side):
            pump_side(len(side))
        P.barrier()


def fprep_phase2(P, C, agF, agF_res, sel, fmat_d, FK_d, FK_res, FQ_d, FQ_res, NT, name="fp"):
    nb = NT // 128
    nbi = nb // 16
    CS = nbi * 4 * 128
    with ExitStack() as st:
        lf = sb(P, st, f"{name}_lf", [128, nbi, 4, 128], F32)
        lf_r = Res()
        fm = sb(P, st, f"{name}_fm", [128, 128], F32)
        ones = sb(P, st, f"{name}_ones", [128, CS], F32)
        c_r = Res()
        Fl = sb(P, st, f"{name}_Fl", [128, CS], F32)
        G = sb(P, st, f"{name}_G", [128, CS], F32)
        R1 = sb(P, st, f"{name}_R1", [128, CS], F32)
        off = sb(P, st, f"{name}_off", [128, 1], F32)
        g_r = Res()
        kp = sb(P, st, f"{name}_kp", [128, 3, CS], BF16)
        kp_r = Res()
        qs = sb(P, st, f"{name}_qs", [128, 3, nbi * 128], BF16)
        qs_r = Res()
        P.op("dve", I("memset", ones[:], 1.0), writes=[c_r])
        P.dma("sp", P.S("ld_fm"), I("dma_start", out=fm[:], in_=fmat_d), writes=[c_r])
        P.dma("sp", P.S("ld_misc"), [I("dma_start", out=lf[:, :, r, :],
                                       in_=agF[r * 8:(r + 1) * 8, :].rearrange("h (c i p) -> (h c) i p", c=16, p=128))
                                     for r in range(4)], reads=[agF_res], writes=[lf_r])
        lff = lf[:].rearrange("q i r p -> q (i r p)")
        P.op("dve", I("tensor_tensor_scan", Fl[:], ones[:], lff, 0.0, ALU.mult, ALU.add), reads=[c_r, lf_r], writes=[g_r])
        P.op("pe", I("matmul", C.ps[0][:, 0:1], fm[:], Fl[:, CS - 1:CS], start=True, stop=True), reads=[c_r, g_r], writes=[C.psr[0]])
        P.op("dve", I("tensor_copy", off[:], C.ps[0][:, 0:1]), reads=[C.psr[0]], writes=[g_r])
        P.op("dve", I("tensor_scalar", G[:], Fl[:], off[:, 0:1], -1.0, ALU.add, ALU.mult), reads=[g_r], writes=[g_r])
        P.op("dve", I("tensor_copy", kp[:, 0, :], G[:]), reads=[g_r], writes=[kp_r])
        P.op("dve", I("tensor_tensor", R1[:], G[:], kp[:, 0, :], ALU.subtract), reads=[g_r, kp_r], writes=[g_r])
        P.op("dve", I("tensor_copy", kp[:, 1, :], R1[:]), reads=[g_r], pwrites=[kp_r])
        P.op("dve", I("tensor_tensor", G[:], R1[:], kp[:, 1, :], ALU.subtract), reads=[g_r, kp_r], writes=[g_r])
        P.op("dve", I("tensor_copy", kp[:, 2, :], G[:]), reads=[g_r], pwrites=[kp_r])
        kv = kp[:].rearrange("q c (i r p) -> q c i r p", r=4, p=128)
        P.dma("sp", P.S("st_kp0"), [I("dma_start", out=FK_d[c3, r].rearrange("h (c i p) -> (h c) i p", c=16, p=128), in_=kv[:, c3, :, r, :])
                                    for c3 in range(3) for r in range(4)], reads=[kp_r], writes=[FK_res])
        for c3 in range(3):
            o = qs[:, c3, :].rearrange("q (i p) -> q i p", p=128)
            P.op("dve", I("tensor_scalar", o, kv[:, c3, :, 0, :], sel[:, 0:1], None, ALU.mult),
                 reads=[kp_r], writes=[qs_r] if c3 == 0 else (), pwrites=() if c3 == 0 else [qs_r])
            for r in range(1, 4):
                P.op("dve", I("scalar_tensor_tensor", o, kv[:, c3, :, r, :], sel[:, r:r + 1], o, ALU.mult, ALU.add),
                     reads=[kp_r, qs_r], pwrites=[qs_r])
        P.dma("sp", P.S("st_qs0"), [I("dma_start", out=FQ_d[c3].rearrange("h (c i p) -> (h c) i p", c=16, p=128),
                                      in_=qs[:, c3, :].rearrange("q (i p) -> q i p", p=128)) for c3 in range(3)],
              reads=[qs_r], writes=[FQ_res])
        P.barrier()


def fox_loader2(P, agK, agK_res, QT_d, q_res, agV, agV_res, FK_d, FK_res, FQ_d, FQ_res, NT):
    agKv = [a.rearrange("(r h d) n -> h d r n", r=4, d=64) for a in agK]
    agVv = [a.rearrange("(r p) (i c) -> p r i c", r=4, c=65) for a in agV]

    def load_head(h, KT, QT, V, sems, res, first):
        if first:
            P.op("dve", I("memset", KT[64:70, :, :], 1.0), writes=[res["KT"]])
            P.op("dve", I("memset", QT[64:70, :], 1.0), writes=[res["QT"]])
        P.dma("pool", sems["KT"], [I("dma_start", out=KT[0:64, :, :], in_=agKv[h // 2][h % 2]),
                                 I("dma_start", out=KT[64:67, :, :], in_=FK_d[:, :, h, :])],
              reads=[agK_res[h // 2], FK_res], writes=[res["KT"]])
        P.dma("pool", sems["QT"], [I("dma_start", out=QT[0:64, :], in_=QT_d[h]),
                                 I("dma_start", out=QT[67:70, :], in_=FQ_d[:, h, :])],
              reads=[q_res, FQ_res], writes=[res["QT"]])
        P.dma("pool", sems["V"], I("dma_start", out=V[:], in_=agVv[h]), reads=[agV_res[h]], writes=[res["V"]])
    return load_head


def fmat_const():
    m = np.zeros((128, 128), np.float32)
    for k in range(128):
        for mm in range(128):
            if k // 16 == mm // 16 and (k % 16) < (mm % 16):
                m[k, mm] = 1.0
    return m


def pool_side(P, C, st, uT_d, u_res, agU, agU_res, sel, invc, wpool_d, pscale, yT_d, y_res, NT, name="pls", q="pool"):
    nb = NT // 128
    nbh = nb // 2
    wp = sb(P, st, f"{name}_wp", [128, 4, 128], BF16)
    wp_r = Res()
    uext = sb(P, st, f"{name}_ue", [128, nbh, 144], F32)
    ue_r = Res()
    H = sb(P, st, f"{name}_H", [128, 4, nbh + 1, 16], F32)
    H_r = Res()
    A = sb(P, st, f"{name}_A", [128, nbh, 144], F32)
    B = sb(P, st, f"{name}_B", [128, nbh, 144], F32)
    ab_r = Res()
    diff = sb(P, st, f"{name}_df", [128, nbh, 128], BF16)
    df_r = Res()
    t16 = sb(P, st, f"{name}_t16", [128, 16], F32)
    yb = sb(P, st, f"{name}_y", [128, nbh * 128], BF16)
    yb_r = Res()
    s_u, s_H, s_y = P.S("ld_pu0"), P.S("ld_pH0"), P.S("st_py0")
    agUv = agU.rearrange("(r g p) (i c) -> g p r i c", r=4, g=4, c=16)

    def body():
        P.dma(q, P.S("ld_w0"), I("dma_start", out=wp[:], in_=wpool_d), writes=[wp_r])
        for g in range(4):
            w = 2 << g
            for hf in range(2):
                b0 = hf * nbh
                P.dma(q, s_u, I("dma_start", out=uext[:, :, 16:144],
                                in_=uT_d[g][:, b0 * 128:(b0 + nbh) * 128].rearrange("p (i c) -> p i c", c=128)),
                      reads=[u_res], writes=[ue_r])
                if hf == 0:
                    P.op("dve", I("memset", H[:, :, 0, :], 0.0), writes=[H_r])
                    P.dma(q, s_H, I("dma_start", out=H[:, :, 1:nbh + 1, :], in_=agUv[g][:, :, 0:nbh, :]), reads=[agU_res], pwrites=[H_r])
                else:
                    P.dma(q, s_H, I("dma_start", out=H[:], in_=agUv[g][:, :, b0 - 1:b0 + nbh, :]), reads=[agU_res], writes=[H_r])
                P._cap.append(("gap", None, None))
                hal = uext[:, :, 0:16]
                P.op("dve", I("tensor_scalar", hal, H[:, 0, 1:nbh + 1, :], sel[:, 4:5], None, ALU.mult), reads=[H_r], pwrites=[ue_r])
                for r in range(1, 4):
                    P.op("dve", I("scalar_tensor_tensor", hal, H[:, r, 1:nbh + 1, :], sel[:, 4 + r:5 + r], hal, ALU.mult, ALU.add),
                         reads=[H_r, ue_r], pwrites=[ue_r])
                P.op("dve", I("scalar_tensor_tensor", hal, H[:, 3, 0:nbh, :], sel[:, 8:9], hal, ALU.mult, ALU.add),
                     reads=[H_r, ue_r], pwrites=[ue_r])
                src = uext
                bufs = [A, B]
                lo = 0
                for stp in range(g + 1):
                    sh = 1 << stp
                    lo = lo + sh
                    dst = bufs[stp % 2]
                    P.op("dve", I("tensor_tensor", dst[:, :, lo:144], src[:, :, lo:144], src[:, :, lo - sh:144 - sh], ALU.add),
                         reads=[ue_r, ab_r], writes=[ab_r])
                    src = dst
                sw = src
                P.op("dve", I("scalar_tensor_tensor", diff[:], sw[:, :, 16:144], 1.0 / w, uext[:, :, 16:144], ALU.mult, ALU.subtract),
                     reads=[ab_r, ue_r], writes=[df_r])
                if hf == 0:
                    P.op("dve", I("tensor_tensor", t16[:], sw[:, 0, 16:32], invc[:, g, :], ALU.mult), reads=[ab_r], writes=[ab_r])
                    P.op("dve", I("tensor_tensor", diff[:, 0, 0:16], t16[:], uext[:, 0, 16:32], ALU.subtract),
                         reads=[ab_r, ue_r, df_r], pwrites=[df_r])
                for t in range(nbh // 4):
                    P._cap.append(("slotmm", dict(
                        lhsT=wp[:, g, :], rhs=diff[:, 4 * t:4 * t + 4, :].rearrange("p i c -> p (i c)"), mm_reads=[wp_r, df_r],
                        evac=(lambda ps, t=t, g=g: I("tensor_scalar", yb[:, t * 512:(t + 1) * 512], ps, pscale[:, g:g + 1], None, ALU.mult)),
                        ev_writes=[yb_r] if t == 0 else [], ev_pwrites=[] if t == 0 else [yb_r]), None))
                P.dma(q, s_y, I("dma_start", out=yT_d[g][:, b0 * 128:(b0 + nbh) * 128], in_=yb[:]), reads=[yb_r], pwrites=[y_res])
    return P.capture(body)


def oddin_phase3(P, C, hin, hin_res, w_in_d, wqb_d, wkvb_d, rot_d, qn_g, kvn_g, gains, gidx, CS_d, CS_res,
                 QT_d, agKn_in, agKr_in, agV_in, out_res, NT, TN=512, name="oin"):
    nt = NT // TN
    nb = TN // 128
    with ExitStack() as st:
        w = sb(P, st, f"{name}_w", [128, KC, MLA_IN], BF16)
        wq = sb(P, st, f"{name}_wq", [128, 2, 1536], BF16)
        wkv = sb(P, st, f"{name}_wkv", [128, 2048], BF16)
        w_r = Res("w")
        P.dma("pool", P.S("ld_w0"), [I("dma_start", out=w[:], in_=w_in_d), I("dma_start", out=wq[:], in_=wqb_d),
                                     I("dma_start", out=wkv[:], in_=wkvb_d)], writes=[w_r])
        wqr = sb(P, st, f"{name}_wqr", [128, 2, 512], BF16)
        wkr = sb(P, st, f"{name}_wkr", [128, KC, 32], BF16)
        wr_r = Res("wr")
        qv = wq[:, :, 1024:1536].rearrange("p j (h t x) -> p j h t x", t=2, x=16)
        qrv = wqr[:].rearrange("p j (h t x) -> p j h t x", t=2, x=16)
        for j in range(2):
            P.op("dve", I("tensor_scalar", qrv[:, j, :, 0, :], qv[:, j, :, 1, :], -1.0, None, ALU.mult), reads=[w_r], pwrites=[wr_r])
            P.op("dve", I("tensor_copy", qrv[:, j, :, 1, :], qv[:, j, :, 0, :]), reads=[w_r], pwrites=[wr_r])
        P.op("dve", I("tensor_scalar", wkr[:, :, 0:16], w[:, :, 400:416], -1.0, None, ALU.mult), reads=[w_r], pwrites=[wr_r])
        P.op("dve", I("tensor_copy", wkr[:, :, 16:32], w[:, :, 384:400]), reads=[w_r], pwrites=[wr_r])
        hb = [sb(P, st, f"{name}_hb{i}", [128, KC, TN], F32) for i in range(2)]
        hb_r = [RL(KC, f"hb{i}_") for i in range(2)]
        xn = [sb(P, st, f"{name}_xn{i}", [128, KC, TN], BF16) for i in range(3)]
        xn_r = [RL(KC, f"xn{i}_") for i in range(3)]
        cs = [sb(P, st, f"{name}_cs{i}", [128, 2, TN], F32) for i in range(2)]
        cs_r = RL(2, "cs")
        sqb = sb(P, st, f"{name}_sqb", [128, KC, TN], BF16)
        sqb_r = RL(KC, "sqb")
        sql = sb(P, st, f"{name}_sql", [128, 3, TN], BF16)
        sql_r = RL(3, "sql")
        rstdb = sb(P, st, f"{name}_rstdb", [128, TN], F32)
        rstdb_r = Res("rstdb")
        rstdl = sb(P, st, f"{name}_rstdl", [128, TN], F32)
        rstdl_r = Res("rstdl")
        cl = [sb(P, st, f"{name}_cl{i}", [128, 3, TN], F32) for i in range(2)]
        cl_r = [RL(3, f"cl{i}_") for i in range(2)]
        cn = [sb(P, st, f"{name}_cn{i}", [128, 3, TN], BF16) for i in range(2)]
        cn_r = [RL(3, f"cn{i}_") for i in range(2)]
        NE = 4
        eb = [sb(P, st, f"{name}_e{i}", [128, TN], BF16) for i in range(NE)]
        eb_r = RL(NE, "eb")
        t1 = sb(P, st, f"{name}_t1", [128, TN], F32)
        t2 = sb(P, st, f"{name}_t2", [128, TN], F32)
        t_r = Res("t")
        vb = [sb(P, st, f"{name}_v{i}", [128, MH, nb, 65], BF16) for i in range(2)]
        vb_r = RL(2, "vb")
        s_h = [P.S(f"ldp_h{i}") for i in range(2)]
        s_cs = [P.S(f"ldp_cs{i}") for i in range(2)]
        s_e = [P.S(f"st_qk{i}") for i in range(NE)]
        s_v = [P.S(f"st_v{i}") for i in range(2)]
        gl = [gains[:, gidx, kc:kc + 1] for kc in range(KC)]
        for i in range(2):
            P.op("dve", I("memset", vb[i][:, :, :, 64:65], 1.0), writes=[vb_r[i]])
        cnt = dict(b3=0, e=0)

        def norm_gen(h, h_res, g, sq, sq_res, bank, rstd, rstd_res, xo, xo_res, nkc, dim):
            for kc in range(nkc):
                P.op("act", I("activation", out=sq[kc], in_=h[kc], func=AF.Square), reads=[h_res[kc]], writes=[sq_res[kc]])
            P.op("pe", [I("matmul", C.ps[bank][:, :TN], C.ones_bf[:], sq[kc], start=(kc == 0), stop=(kc == nkc - 1)) for kc in range(nkc)],
                 reads=[C.r_const] + list(sq_res[:nkc]), writes=[C.psr[bank]])
            P.op("dve", I("tensor_scalar", rstd, C.ps[bank][:, :TN], 1.0 / dim, EPS, ALU.mult, ALU.add), reads=[C.psr[bank]], writes=[rstd_res])
            P.op("act", I("activation", out=rstd, in_=rstd, func=AF.Sqrt), reads=[rstd_res], writes=[rstd_res])
            P.op("dve", I("reciprocal", rstd, rstd), reads=[rstd_res], writes=[rstd_res])
            for kc in range(nkc):
                P.op("dve", I("scalar_tensor_tensor", xo[kc], h[kc], g[kc], rstd, ALU.mult, ALU.mult),
                     reads=[h_res[kc], rstd_res], writes=[xo_res[kc]])

        def load(t):
            s = t % 2
            P.dma("pool", s_h[s], I("dma_start", out=hb[s][:], in_=tview(hin, t, TN)), reads=[hin_res[t]], writes=hb_r[s])

        def load_cs(t):
            s = t % 2
            P.dma("pool", s_cs[s], I("dma_start", out=cs[s][:], in_=CS_d[:, :, t * TN:(t + 1) * TN].rearrange("w p n -> p w n")),
                  reads=[CS_res], writes=[cs_r[s]])

        def S1(t):
            s, x = t % 2, t % 3
            norm_gen([hb[s][:, kc, :] for kc in range(KC)], hb_r[s], gl, [sqb[:, kc, :] for kc in range(KC)], sqb_r,
                     6, rstdb[:], rstdb_r, [xn[x][:, kc, :] for kc in range(KC)], xn_r[x], KC, D)

        def S2(t):
            s, x = t % 2, t % 3
            for j, c0 in enumerate((0, 128, 256)):
                bank = j
                P.op("pe", [I("matmul", C.ps[bank][:, :TN], w[:, kc, c0:c0 + 128], xn[x][:, kc, :],
                              start=(kc == 0), stop=(kc == KC - 1)) for kc in range(KC)],
                     reads=[w_r] + xn_r[x], writes=[C.psr[bank]])
                P.op("act", I("activation", out=cl[s][:, j, :], in_=C.ps[bank][:, :TN], func=AF.Copy),
                     reads=[C.psr[bank]], writes=[cl_r[s][j]])
            norm_gen([cl[s][:, j, :] for j in range(2)], cl_r[s][0:2], [qn_g[:, j:j + 1] for j in range(2)],
                     [sql[:, j, :] for j in range(2)], sql_r, 7, rstdl[:], rstdl_r,
                     [cn[s][:, j, :] for j in range(2)], cn_r[s][0:2], 2, 256)
            norm_gen([cl[s][:, 2, :]], cl_r[s][2:3], [kvn_g[:, 0:1]], [sql[:, 2, :]], sql_r[2:3], 7, rstdl[:], rstdl_r,
                     [cn[s][:, 2, :]], cn_r[s][2:3], 1, 128)

        def nbank3():
            k = cnt["b3"]
            cnt["b3"] += 1
            return 3 + (k % 3)

        def neb():
            k = cnt["e"]
            cnt["e"] += 1
            return k % NE

        def rope_apply(bq, br, M, scale, s, dst_bf, dst_res):
            P.op("dve", I("scalar_tensor_tensor", t1[:M, :], C.ps[bq][:M, :TN], scale, cs[s][:M, 0, :], ALU.mult, ALU.mult),
                 reads=[C.psr[bq], cs_r[s]], writes=[t_r])
            P.op("dve", I("scalar_tensor_tensor", t2[:M, :], C.ps[br][:M, :TN], scale, cs[s][:M, 1, :], ALU.mult, ALU.mult),
                 reads=[C.psr[br], cs_r[s], t_r], writes=[t_r])
            P.op("dve", I("tensor_tensor", dst_bf, t1[:M, :], t2[:M, :], ALU.add), reads=[t_r], writes=[dst_res])

        def S3(t):
            s, x = t % 2, t % 3
            tok = slice(t * TN, (t + 1) * TN)
            c_, c_r = cn[s], cn_r[s]
            bank, bank2 = nbank3(), nbank3()
            P.op("pe", [I("matmul", C.ps[bank][:32, :TN], w[:, kc, 384:416], xn[x][:, kc, :],
                          start=(kc == 0), stop=(kc == KC - 1)) for kc in range(KC)], reads=[w_r] + xn_r[x], writes=[C.psr[bank]])
            P.op("pe", [I("matmul", C.ps[bank2][:32, :TN], wkr[:, kc, :], xn[x][:, kc, :],
                          start=(kc == 0), stop=(kc == KC - 1)) for kc in range(KC)], reads=[wr_r] + xn_r[x], writes=[C.psr[bank2]])
            e = neb()
            rope_apply(bank, bank2, 32, 1.0, s, eb[e][0:32, :], eb_r[e])
            P.dma("sp", s_e[e], I("dma_start", out=agKr_in[:, tok], in_=eb[e][0:32, :]), reads=[eb_r[e]], pwrites=[out_res["agKr"]])
            for m in range(8):
                bank = nbank3()
                e = neb()
                P.op("pe", I("matmul", C.ps[bank][:, :TN], wkv[:, m * 128:(m + 1) * 128], c_[:, 2, :], start=True, stop=True),
                     reads=[w_r, c_r[2]], writes=[C.psr[bank]])
                P.op("act", I("activation", out=eb[e][:], in_=C.ps[bank][:, :TN], func=AF.Copy), reads=[C.psr[bank]], writes=[eb_r[e]])
                P.dma("sp", s_e[e], I("dma_start", out=agKn_in[m][:, tok], in_=eb[e][:]), reads=[eb_r[e]], pwrites=[out_res["agKn"][m]])
            vs = t % 2
            for blk in range(nb):
                for half in range(2):
                    bank = nbank3()
                    P.op("pe", I("matmul", C.ps[bank][:, :512], c_[:, 2, blk * 128:(blk + 1) * 128],
                                 wkv[:, 1024 + half * 512:1024 + (half + 1) * 512], start=True, stop=True),
                         reads=[w_r, c_r[2]], writes=[C.psr[bank]])
                    src = C.ps[bank][:, :512].rearrange("p (h d) -> p h d", d=64)
                    dst = vb[vs][:, half * 8:(half + 1) * 8, blk, 0:64]
                    P.op("act", I("activation", out=dst, in_=src, func=AF.Copy), reads=[C.psr[bank]], pwrites=[vb_r[vs]])
            P.dma("sp", s_v[vs], [I("dma_start", out=agV_in[h][:, t * nb * 65:(t + 1) * nb * 65],
                                    in_=vb[vs][:, h, :, :].rearrange("p i c -> p (i c)")) for h in range(MH)],
                  reads=[vb_r[vs]], pwrites=out_res["agV"])
            for m in range(4):
                bank, bank2 = nbank3(), nbank3()
                e = neb()
                P.op("pe", [I("matmul", C.ps[bank][:, :TN], wq[:, j, 1024 + m * 128:1024 + (m + 1) * 128], c_[:, j, :],
                              start=(j == 0), stop=(j == 1)) for j in range(2)], reads=[w_r] + c_r[0:2], writes=[C.psr[bank]])
                P.op("pe", [I("matmul", C.ps[bank2][:, :TN], wqr[:, j, m * 128:(m + 1) * 128], c_[:, j, :],
                              start=(j == 0), stop=(j == 1)) for j in range(2)], reads=[wr_r] + c_r[0:2], writes=[C.psr[bank2]])
                rope_apply(bank, bank2, 128, QSCALE, s, eb[e][:], eb_r[e])
                P.dma("sp", s_e[e], [I("dma_start", out=QT_d[4 * m + hh, 64:96, tok], in_=eb[e][hh * 32:(hh + 1) * 32, :]) for hh in range(4)],
                      reads=[eb_r[e]], pwrites=[out_res["q"]])
            for m in range(8):
                bank = nbank3()
                e = neb()
                P.op("pe", [I("matmul", C.ps[bank][:, :TN], wq[:, j, m * 128:(m + 1) * 128], c_[:, j, :],
                              start=(j == 0), stop=(j == 1)) for j in range(2)], reads=[w_r] + c_r[0:2], writes=[C.psr[bank]])
                P.op("act", I("activation", out=eb[e][:], in_=C.ps[bank][:, :TN], func=AF.Copy, scale=QSCALE),
                     reads=[C.psr[bank]], writes=[eb_r[e]])
                P.dma("sp", s_e[e], [I("dma_start", out=QT_d[2 * m + hh, 0:64, tok], in_=eb[e][hh * 64:(hh + 1) * 64, :]) for hh in range(2)],
                      reads=[eb_r[e]], pwrites=[out_res["q"]])

        import os as _os
        load(0)
        load_cs(0)
        if nt > 1:
            load(1)
            load_cs(1)
        for step in range(nt + 2):
            lists = []
            if step - 2 >= 0:
                lists.append(P.capture(lambda: S3(step - 2)))
            if 0 <= step - 1 < nt:
                lists.append(P.capture(lambda: S2(step - 1)))
            if step < nt:
                lists.append(P.capture(lambda: S1(step)))
            if _os.environ.get("CHECKIL"):
                for a_ in range(len(lists)):
                    for b_ in range(a_ + 1, len(lists)):
                        check_interleave_prop(lists[a_], lists[b_])
            n = max(len(L) for L in lists)
            pos = [0] * len(lists)
            for k in range(n):
                for li, L in enumerate(lists):
                    want = ((k + 1) * len(L) + n - 1) // n
                    while pos[li] < min(want, len(L)):
                        P.replay([L[pos[li]]])
                        pos[li] += 1
            if step + 2 < nt:
                load(step + 2)
            if step - 2 >= 0 and step < nt:
                load_cs(step)
        P.barrier()


def check_interleave_prop(L0, L1):
    def sets(L):
        rd, wr = set(), set()
        names = {}
        for kind, a_, kw_ in L:
            for R in kw_["reads"]:
                rd.add(id(R)); names[id(R)] = R.name
            for R in list(kw_["writes"]):
                wr.add(id(R)); names[id(R)] = R.name
        return rd, wr, names
    r0, w0, n0 = sets(L0)
    r1, w1, n1 = sets(L1)
    bad = (w0 & (r1 | w1)) | (w1 & (r0 | w0))
    for b_ in bad:
        print("INTERLEAVE VIOLATION (stage overlap):", n0.get(b_, n1.get(b_)))


def rope_side(P, C, st, pos_d, invf, CS_d, CS_res, NT, name="rps"):
    CH = 1024 if NT >= 1024 else NT
    C1 = 6.28125
    C2 = 2.0 * PI - C1
    pi_ = sb(P, st, f"{name}_pi", [128, CH], I32)
    ang = sb(P, st, f"{name}_ang", [128, CH], F32)
    tt = sb(P, st, f"{name}_t", [128, CH], F32)
    m = sb(P, st, f"{name}_m", [128, CH], F32)
    o = [sb(P, st, f"{name}_o{i}", [128, CH], F32) for i in range(2)]
    r = Res()
    o_r = RL(2)
    halfpi = sb(P, st, f"{name}_hpi", [128, 1], F32)

    def body():
        P.op("dve", I("memset", halfpi[:], PI / 2), writes=[r])
        for c in range(NT // CH):
            sl = slice(c * CH, (c + 1) * CH)
            P.dma("pool", P.S("ldp_rp"), I("dma_start", out=pi_[:], in_=pos_d[:, sl]), writes=[r])
            P._cap.append(("gap", None, None))
            P.op("dve", I("tensor_copy", ang[:], pi_[:]), reads=[r], writes=[r])
            P.op("dve", I("tensor_scalar", ang[:], ang[:], invf[:, 0:1], None, ALU.mult), reads=[r], writes=[r])
            P.op("dve", I("tensor_scalar", tt[:], ang[:], 1.0 / (2.0 * PI), None, ALU.mult), reads=[r], writes=[r])
            P.op("dve", I("tensor_copy", pi_[:], tt[:]), reads=[r], writes=[r])
            P.op("dve", I("tensor_copy", tt[:], pi_[:]), reads=[r], writes=[r])
            P.op("dve", I("scalar_tensor_tensor", m[:], tt[:], -C1, ang[:], ALU.mult, ALU.add), reads=[r], writes=[r])
            P.op("dve", I("scalar_tensor_tensor", m[:], tt[:], -C2, m[:], ALU.mult, ALU.add), reads=[r], writes=[r])
            P.op("dve", I("tensor_scalar", m[:], m[:], -PI, PI, ALU.max, ALU.min), reads=[r], writes=[r])
            P.op("act", I("activation", out=o[1][:], in_=m[:], func=AF.Sin), reads=[r], writes=[o_r[1]])
            P.dma("pool", P.S("stp_rp1"), I("dma_start", out=CS_d[1, :, sl], in_=o[1][:]), reads=[o_r[1]], pwrites=[CS_res])
            P.op("dve", I("tensor_scalar", tt[:], m[:], -1.0, None, ALU.mult), reads=[r], writes=[r])
            P.op("dve", I("tensor_tensor", tt[:], tt[:], m[:], ALU.max), reads=[r], writes=[r])
            P.op("act", I("activation", out=o[0][:], in_=tt[:], func=AF.Sin, scale=-1.0, bias=halfpi[:, 0:1]), reads=[r], writes=[o_r[0]])
            P.dma("pool", P.S("stp_rp0"), I("dma_start", out=CS_d[0, :, sl], in_=o[0][:]), reads=[o_r[0]], pwrites=[CS_res])
    return P.capture(body)


import ml_dtypes
from concourse.bass_utils import run_bass_kernel_spmd

NTOK = 4096
NBLK = NTOK // 128
SEQ = 16384
DEPTH = 4
NG = 13


def build_program():
    nc = bass.Bass("TRN2", target_bir_lowering=False)
    NT = NTOK

    def ein(name, shape, dt=F32):
        return nc.dram_tensor(name, list(shape), dt, kind="ExternalInput").ap()

    def it(name, shape, dt):
        return nc.dram_tensor(name, list(shape), dt).ap()

    xT = ein("xT", [KC, 128, NT])
    pos_d = ein("pos", [128, NT], I32)
    wgu = [ein(f"wgu{i}", [FC, 128, 2, KC, 128]) for i in range(8)]
    wd = [ein(f"wd{i}", [KC, 128, FC, 128]) for i in range(8)]
    ab_w_in = [ein(f"ab_w_in{i}", [128, KC, ABIN]) for i in range(2)]
    ab_w_out = [ein(f"ab_w_out{i}", [128, KC, D]) for i in range(2)]
    wpool = [ein(f"wpool{i}", [128, 4, 128]) for i in range(2)]
    mla_w_in = [ein(f"mla_w_in{i}", [128, KC, MLA_IN]) for i in range(2)]
    mla_wqb = [ein(f"mla_wqb{i}", [128, 2, 1536]) for i in range(2)]
    mla_wkvb = [ein(f"mla_wkvb{i}", [128, 2048]) for i in range(2)]
    mla_w_out = [ein(f"mla_w_out{i}", [128, KC, D]) for i in range(2)]
    gains_d = ein("gains", [128, NG, KC])
    small_d = ein("small", [128, 64])
    invc_d = ein("invc", [128, 4, 16])
    rot_d = ein("rot", [128, 128])
    fmat_d = ein("fmat", [128, 128])
    masks_d = ein("masks", [128, 3, 4, 128], BF16)
    ident_d = ein("ident", [128, 128], BF16)
    outT = nc.dram_tensor("outT", [KC, 128, NT], F32, kind="ExternalOutput").ap()

    hT = it("hT", [KC, 128, NT], F32)
    yT_d = it("yT_d", [KC, 128, NT], BF16)
    uT_d = it("uT_d", [4, 128, NT], F32)
    QTf_d = it("QTf_d", [8, 64, NT], BF16)
    QTm_d = it("QTm_d", [16, 96, NT], BF16)
    agK_in = [it(f"agK_in{m}", [128, NT], BF16) for m in range(8)]
    agK = [it(f"agK{m}", [4 * 128, NT], BF16) for m in range(8)]
    agV_in = [it(f"agV_in{h}", [128, NBLK * 65], BF16) for h in range(16)]
    agV = [it(f"agV{h}", [4 * 128, NBLK * 65], BF16) for h in range(16)]
    agF_in = it("agF_in", [8, NT], F32)
    agF = it("agF", [32, NT], F32)
    agU_in = it("agU_in", [4 * 128, NBLK * 16], F32)
    agU = it("agU", [16 * 128, NBLK * 16], F32)
    agKr_in = it("agKr_in", [32, NT], BF16)
    agKr = it("agKr", [128, NT], BF16)
    FK_d = it("FK_d", [3, 4, 8, NT], BF16)
    FQ_d = it("FQ_d", [3, 8, NT], BF16)
    CS_d = it("CS_d", [2, 128, NT], F32)

    stack = ExitStack()
    with stack as st:
        P = Prog(nc, st)
        C = Ctx(P)
        gains = sb(P, st, "gains_sb", [128, NG, KC], F32)
        small = sb(P, st, "small_sb", [128, 64], F32)
        invc = sb(P, st, "invc_sb", [128, 4, 16], F32)
        masks = sb(P, st, "masks_sb", [128, 3, 4, 128], BF16)
        ident = sb(P, st, "ident_sb", [128, 128], BF16)
        for (a, b) in ((gains, gains_d), (small, small_d), (invc, invc_d), (masks, masks_d), (ident, ident_d)):
            P.dma("sp", P.S("misc"), I("dma_start", out=a[:], in_=b), writes=[C.r_const])
        P.barrier()
        sel = small[:, 0:16]
        invf = small[:, 16:17]
        bfg = [small[0:8, 17 + i:18 + i] for i in range(2)]
        pscale = [small[:, 20 + 4 * i:24 + 4 * i] for i in range(2)]
        qn = [small[:, 28 + 2 * i:30 + 2 * i] for i in range(2)]
        kvn = [small[:, 32 + i:33 + i] for i in range(2)]

        nt = NT // 512
        x_r = RL(nt, "x")
        h_r = RL(nt, "h")
        o_r = RL(nt, "o")
        CS_r = Res()
        R = dict(u=Res(), agU=Res(), qf=Res(), qm=Res(), agF=Res(), agKr=Res(), agK=RL(8), agV=RL(16))
        G = dict(F=Res(), U=Res(), Kr=Res(), K=RL(8), V=RL(16))
        FK_r, FQ_r, y_r = Res(), Res(), Res()
        for layer in range(DEPTH):
            i = layer // 2
            if layer == 0:
                ffn_phase2(P, C, [(wgu[0], wd[0], 0)], xT, x_r, hT, h_r, gains, NT, name="f0a")
            if layer % 2 == 0:
                oR = dict(u=R["u"], agU=R["agU"], q=R["qf"], agK=R["agK"][:4], agV=R["agV"][:8], agF=R["agF"])
                evenin_phase(P, C, hT, h_r, ab_w_in[i], bfg[i], gains, 8 + layer, QTf_d, agK_in[:4], agV_in[:8], agF_in, agU_in,
                             uT_d, oR, NT, name=f"ei{layer}")
                allgather(P, agF_in, agF, [R["agF"]], [G["F"]])
                for h in range(8):
                    if h % 2 == 0:
                        allgather(P, agK_in[h // 2], agK[h // 2], [R["agK"][h // 2]], [G["K"][h // 2]])
                    allgather(P, agV_in[h], agV[h], [R["agV"][h]], [G["V"][h]])
                    if h == 0:
                        allgather(P, agU_in, agU, [R["agU"]], [G["U"]])
                fprep_phase2(P, C, agF, G["F"], sel, fmat_d, FK_d, FK_r, FQ_d, FQ_r, NT, name=f"fp{layer}")
                ld = fox_loader2(P, agK[:4], G["K"][:4], QTf_d, R["qf"], agV[:8], G["V"][:8], FK_d, FK_r, FQ_d, FQ_r, NT)
                def sf(st_, i=i, layer=layer):
                    L = pool_side(P, C, st_, uT_d, R["u"], agU, G["U"], sel, invc, wpool[i], pscale[i], yT_d, y_r, NT, name=f"pls{layer}")
                    if layer == 0:
                        L = L + rope_side(P, C, st_, pos_d, invf, CS_d, CS_r, NT)
                    return L
                attn_phase2(P, C, 8, 70, ld, masks[:, 0, :, :], ident, yT_d, y_r, 4, NT, name=f"at{layer}", side_fn=sf)
                wout = ab_w_out[i]
            else:
                oR = dict(q=R["qm"], agKr=R["agKr"], agKn=R["agK"], agV=R["agV"])
                oddin_phase3(P, C, hT, h_r, mla_w_in[i], mla_wqb[i], mla_wkvb[i], rot_d, qn[i], kvn[i], gains, 8 + layer,
                            CS_d, CS_r, QTm_d, agK_in, agKr_in, agV_in, oR, NT, name=f"oi{layer}")
                allgather(P, agKr_in, agKr, [R["agKr"]], [G["Kr"]])
                for h in range(16):
                    if h % 2 == 0:
                        allgather(P, agK_in[h // 2], agK[h // 2], [R["agK"][h // 2]], [G["K"][h // 2]])
                    allgather(P, agV_in[h], agV[h], [R["agV"][h]], [G["V"][h]])
                ld = mla_loader(P, agK, G["K"], agKr, G["Kr"], QTm_d, R["qm"], agV, G["V"], NT)
                attn_phase2(P, C, 16, 96, ld, masks[:, 1, :, :], ident, yT_d, y_r, 0, NT, name=f"at{layer}")
                wout = mla_w_out[i]
            outproj_phase(P, C, hT, h_r, hT, h_r, yT_d, y_r, wout, NT, name=f"op{layer}")
            stages = [(wgu[2 * layer + 1], wd[2 * layer + 1], 2 * layer + 1)]
            if layer + 1 < DEPTH:
                stages.append((wgu[2 * layer + 2], wd[2 * layer + 2], 2 * layer + 2))
            ffn_phase2(P, C, stages, hT, h_r, hT, h_r, gains, NT, name=f"f{layer}b",
                       final=(12, outT, o_r) if layer + 1 == DEPTH else None)
        P.finish()
        nc._mk_stats = (P.ninstr, P.nwaits)
    return nc


_PROG = [None]


def _host_inputs(inp):
    f32 = np.float32
    shared = {}
    for l in range(DEPTH):
        for s in range(2):
            shared[f"wgu{2 * l + s}"] = lay_wgu(np.asarray(inp["ffn_w_gate"][l, s], f32), np.asarray(inp["ffn_w_up"][l, s], f32))
            shared[f"wd{2 * l + s}"] = lay_wd(np.asarray(inp["ffn_w_down"][l, s], f32))
    for i in range(2):
        shared[f"ab_w_in{i}"] = lay_kmajor(np.asarray(inp["ab_w_in"][i], f32))
        shared[f"ab_w_out{i}"] = lay_kmajor(np.asarray(inp["ab_w_out"][i], f32))
        shared[f"wpool{i}"] = np.ascontiguousarray(np.asarray(inp["pool_w"][i], f32).transpose(1, 0, 2))
        shared[f"mla_w_in{i}"] = lay_kmajor(np.asarray(inp["mla_w_in"][i], f32))
        shared[f"mla_wqb{i}"] = lay_wqb(np.asarray(inp["mla_w_q_b"][i], f32))
        shared[f"mla_wkvb{i}"] = lay_wkvb(np.asarray(inp["mla_w_kv_b"][i], f32))
        shared[f"mla_w_out{i}"] = lay_kmajor(np.asarray(inp["mla_w_out"][i], f32))
    gains = np.zeros((128, NG, KC), f32)
    for l in range(DEPTH):
        for s in range(2):
            gains[:, 2 * l + s, :] = lay_gain(np.asarray(inp["norm_ffn"][l, s], f32))
        gains[:, 8 + l, :] = lay_gain(np.asarray(inp["norm_mix"][l], f32))
    gains[:, 12, :] = lay_gain(np.asarray(inp["norm_final"], f32))
    shared["gains"] = gains
    invf, rot = rope_consts()
    shared["rot"] = rot
    shared["fmat"] = fmat_const()
    shared["ident"] = np.eye(128, dtype=f32).astype(ml_dtypes.bfloat16)
    x = np.asarray(inp["x"], f32)
    pos = np.asarray(inp["positions"], np.int32)
    maps = []
    for c in range(8):
        b, j = c // 4, c % 4
        sel, invc, mf, mm = core_tables(j)
        small = np.zeros((128, 64), f32)
        small[:, 0:16] = sel
        small[:, 16:17] = invf
        for i in range(2):
            small[0:8, 17 + i] = np.asarray(inp["ab_b_forget"][i], f32)
            small[:, 20 + 4 * i:24 + 4 * i] = np.asarray(inp["pool_scale"][i], f32).reshape(4, 128).T
            small[:, 28 + 2 * i:30 + 2 * i] = np.asarray(inp["mla_q_norm"][i], f32).reshape(2, 128).T
            small[:, 32 + i] = np.asarray(inp["mla_kv_norm"][i], f32)
        tp = tok_perm(j, NBLK)
        m = dict(shared)
        m["xT"] = np.ascontiguousarray(x[b][tp].T).reshape(KC, 128, NTOK)
        m["pos"] = np.ascontiguousarray(np.broadcast_to(pos[b][tp][None, :], (128, NTOK)))
        m["small"] = small
        m["invc"] = invc
        m["masks"] = np.ascontiguousarray(np.stack([mf, mm, mask01_of(mm)], axis=1)).astype(ml_dtypes.bfloat16)
        maps.append(m)
    return maps


def kernel(**inputs):
    if _PROG[0] is None:
        _PROG[0] = build_program()
    nc = _PROG[0]
    maps = _host_inputs(inputs)
    res = run_bass_kernel_spmd(nc, maps, core_ids=list(range(8)))
    out = np.zeros((2, SEQ, D), np.float32)
    for c in range(8):
        b, j = c // 4, c % 4
        tp = tok_perm(j, NBLK)
        out[b, tp, :] = np.asarray(res.results[c]["outT"], np.float32).reshape(D, NTOK).T
    return out
```

```python
import numpy as np
from contextlib import ExitStack
import concourse.bass as bass
import concourse.mybir as mybir

F32 = mybir.dt.float32
BF16 = mybir.dt.bfloat16
I32 = mybir.dt.int32
AF = mybir.ActivationFunctionType
ALU = mybir.AluOpType

D = 1024
KC = 8
DFF = 2816
FC = 22
EPS = 1e-6


def I(name, *args, **kw):
    return (name, args, kw)


def _mk(ins, h, inc):
    name, args, kw = ins

    def run(e):
        r = getattr(e, name)(*args, **kw)
        if h is not None:
            r.then_inc(h, inc)
    return run


class SemC:
    def __init__(self, h, name):
        self.h = h
        self.v = 0
        self.name = name


class Res:
    __slots__ = ("w", "r", "name")

    def __init__(self, name=""):
        self.w = {}
        self.r = {}
        self.name = name


def RL(n, name=""):
    return [Res(f"{name}{i}") for i in range(n)]


class Prog:
    ENG = ("sp", "act", "pe", "dve", "pool")

    def __init__(self, nc, stack):
        self.nc = nc
        self.stack = stack
        self.streams = {k: [] for k in self.ENG}
        self.esem = {}
        for k in ("act", "pe", "dve", "pool"):
            self.esem[k] = self.sem("e_" + k)
        self.waited = {}
        self.sems = {}
        self.all_sems = list(self.esem.values())
        self.nwaits = 0
        self.ninstr = 0

    def sem(self, name):
        h = self.stack.enter_context(self.nc.semaphore(name))
        s = SemC(h, name)
        if hasattr(self, "all_sems"):
            self.all_sems.append(s)
        return s

    def S(self, name):
        if name not in self.sems:
            self.sems[name] = self.sem(name)
        return self.sems[name]

    def _need(self, eng, own, reads, writes, pwrites=()):
        need = {}

        import os as _os
        noskip = bool(_os.environ.get("NOSKIP"))

        def add(tok, same_skip):
            s, v = tok
            if s is own and same_skip and not (noskip and eng != "pe"):
                return
            if need.get(s, 0) < v:
                need[s] = v

        pe = eng == "pe"
        for R in reads:
            for tok in R.w.items():
                add(tok, pe)
        for R in writes:
            for tok in R.w.items():
                add(tok, True)
            for tok in R.r.items():
                add(tok, True)
        for R in pwrites:
            for tok in R.r.items():
                add(tok, True)
        return need

    def _emit_waits(self, eng, need):
        st = self.streams[eng]
        for s, v in need.items():
            key = (eng, s)
            if self.waited.get(key, 0) >= v:
                continue
            self.waited[key] = v
            self.nwaits += 1
            st.append(lambda e, s=s, v=v: e.wait_ge(s.h, v))

    def _update(self, tok, reads, writes, pwrites=()):
        s, v = tok
        for R in reads:
            if R.r.get(s, 0) < v:
                R.r[s] = v
        for R in writes:
            R.w = {s: v}
            R.r = {}
        for R in pwrites:
            if R.w.get(s, 0) < v:
                R.w[s] = v

    def capture(self, fn):
        prev = getattr(self, "_cap", None)
        self._cap = []
        fn()
        out = self._cap
        self._cap = prev
        return out

    def replay(self, calls):
        for kind, a, kw in calls:
            getattr(self, kind)(*a, **kw)

    def op(self, eng, fns, reads=(), writes=(), pwrites=()):
        if getattr(self, "_cap", None) is not None:
            self._cap.append(("op", (eng, fns), dict(reads=reads, writes=writes, pwrites=pwrites)))
            return
        if isinstance(fns[0], str):
            fns = [fns]
        own = self.esem[eng]
        need = self._need(eng, own, reads, writes, pwrites)
        self._emit_waits(eng, need)
        st = self.streams[eng]
        for f in fns[:-1]:
            st.append(_mk(f, None, 0))
        own.v += 1
        st.append(_mk(fns[-1], own.h, 1))
        self.ninstr += len(fns)
        self._update((own, own.v), reads, writes, pwrites)

    def dma(self, q, sem, fns, reads=(), writes=(), pwrites=()):
        if getattr(self, "_cap", None) is not None:
            self._cap.append(("dma", (q, sem, fns), dict(reads=reads, writes=writes, pwrites=pwrites)))
            return
        if isinstance(fns[0], str):
            fns = [fns]
        need = self._need(q, None, reads, writes, pwrites)
        self._emit_waits(q, need)
        st = self.streams[q]
        for f in fns:
            st.append(_mk(f, sem.h, 16))
        sem.v += 16 * len(fns)
        self.ninstr += len(fns)
        self._update((sem, sem.v), reads, writes, pwrites)

    def coll(self, sem, fn, reads=(), writes=()):
        need = self._need("pool", None, reads, writes)
        self._emit_waits("pool", need)
        self.streams["pool"].append(_mk(fn, sem.h, 1))
        sem.v += 1
        self._update((sem, sem.v), reads, writes)

    def barrier(self, engines=None):
        for eng in engines or self.ENG:
            need = {s: s.v for s in self.all_sems if s.v > 0 and not s.name.startswith("cc")}
            self._emit_waits(eng, need)

    def finish(self):
        self.barrier()
        with self.nc.Block() as block:
            @block.sync
            def _(e):
                for f in self.streams["sp"]:
                    f(e)

            @block.scalar
            def _(e):
                for f in self.streams["act"]:
                    f(e)

            @block.tensor
            def _(e):
                for f in self.streams["pe"]:
                    f(e)

            @block.vector
            def _(e):
                for f in self.streams["dve"]:
                    f(e)

            @block.gpsimd
            def _(e):
                for f in self.streams["pool"]:
                    f(e)


class Ctx:
    def __init__(self, P):
        self.P = P
        nc = P.nc
        st = P.stack
        self.ps2 = [st.enter_context(nc.psum_tensor(f"ps{i}", [128, 1024], F32)) for i in range(4)]
        self.ps = [self.ps2[i // 2][:, (i % 2) * 512:(i % 2 + 1) * 512] for i in range(8)]
        self.psr = RL(8, "ps")
        self.ones_bf = st.enter_context(nc.sbuf_tensor("ones_bf", [128, 128], BF16))
        self.ones_f = st.enter_context(nc.sbuf_tensor("ones_f", [128, 512], F32))
        self.r_const = Res("const")
        P.op("dve", [I("memset", self.ones_bf[:], 1.0), I("memset", self.ones_f[:], 1.0)], writes=[self.r_const])


_uid = [0]


def check_interleave(L0, L1):
    lastw = {}
    for k in range(max(len(L0), len(L1))):
        for tid, L in ((0, L0), (1, L1)):
            if k >= len(L):
                continue
            kind, a_, kw_ = L[k]
            for R in kw_["reads"]:
                if id(R) in lastw and lastw[id(R)][0] != tid:
                    print("INTERLEAVE VIOLATION: tile", tid, "op", k, kind, a_[0], "reads", R.name, "last written by", lastw[id(R)])
            for R in list(kw_["writes"]) + list(kw_["pwrites"]):
                lastw[id(R)] = (tid, k)


def sb(P, stack, name, shape, dt):
    _uid[0] += 1
    return stack.enter_context(P.nc.sbuf_tensor(f"{name}_u{_uid[0]}", shape, dt))


def tview(ap, t, TN):
    return ap[:, :, t * TN:(t + 1) * TN].rearrange("k p n -> p k n")


def emit_norm(P, C, h, h_res, g, sq, sq_res, psn, psn_res, rstd, rstd_res, xn, xn_res,
              nkc=KC, dim=D):
    for kc in range(nkc):
        P.op("dve", I("tensor_tensor", sq[kc], h[kc], h[kc], ALU.mult), reads=[h_res[kc]], writes=[sq_res[kc]])
    P.op("pe", [I("matmul", psn, C.ones_bf[:], sq[kc], start=(kc == 0), stop=(kc == nkc - 1)) for kc in range(nkc)],
         reads=[C.r_const] + list(sq_res[:nkc]), writes=[psn_res])
    P.op("dve", I("tensor_scalar", rstd, psn, 1.0 / dim, EPS, ALU.mult, ALU.add), reads=[psn_res], writes=[rstd_res])
    P.op("act", I("activation", out=rstd, in_=rstd, func=AF.Sqrt), reads=[rstd_res], writes=[rstd_res])
    P.op("dve", I("reciprocal", rstd, rstd), reads=[rstd_res], writes=[rstd_res])
    for kc in range(nkc):
        P.op("dve", I("scalar_tensor_tensor", xn[kc], h[kc], g[kc], rstd, ALU.mult, ALU.mult),
             reads=[h_res[kc], rstd_res], writes=[xn_res[kc]])


def ffn_phase(P, C, hin, hin_res, hout, hout_res, wgu, wd, gains, gidx, NT, TN=512, name="ffn"):
    nt = NT // TN
    with ExitStack() as st:
        NH = 3
        hb = [sb(P, st, f"{name}_h{i}", [128, KC, TN], F32) for i in range(NH)]
        hb_r = [RL(KC, "hb") for _ in range(NH)]
        xn = [sb(P, st, f"{name}_xn{i}", [128, KC, TN], BF16) for i in range(2)]
        xn_r = [RL(KC, "xn") for _ in range(2)]
        sq = sb(P, st, f"{name}_sq", [128, KC, TN], BF16)
        sq_r = RL(KC, "sq")
        rstd = sb(P, st, f"{name}_rstd", [128, TN], F32)
        rstd_r = Res("rstd")
        h1 = sb(P, st, f"{name}_h1", [128, FC, TN], BF16)
        h1_r = RL(FC, "h1")
        NSG = 3
        sg = [sb(P, st, f"{name}_sg{i}", [128, TN], F32) for i in range(NSG)]
        sg_r = RL(NSG, "sg")
        NW = 3
        wg_sb = [sb(P, st, f"{name}_wgu{i}", [128, 2, KC, 128], BF16) for i in range(NW)]
        wg_r = RL(NW, "wgu")
        wd_sb = [sb(P, st, f"{name}_wd{i}", [128, FC, 128], BF16) for i in range(NW)]
        wd_r = RL(NW, "wd")
        s_h = [P.S(f"ld_h{i}") for i in range(NH)]
        s_st = [P.S(f"st_h{i}") for i in range(NH)]
        s_wg = [P.S(f"ld_wg{i}") for i in range(NW)]
        s_wd = [P.S(f"ld_wd{i}") for i in range(NW)]
        psG, psU, psD, psN = [0, 1], [2, 3], [4, 5], 6
        gl = [gains[:, gidx, kc:kc + 1] for kc in range(KC)]

        def load_h(t):
            s = t % NH
            P.dma("sp", s_h[s], I("dma_start", out=hb[s][:], in_=tview(hin, t, TN)), reads=[hin_res[t]], writes=hb_r[s])

        def norm(t):
            s = t % NH
            x = t % 2
            emit_norm(P, C, [hb[s][:, kc, :] for kc in range(KC)], hb_r[s], gl,
                      [sq[:, kc, :] for kc in range(KC)], sq_r, C.ps[psN][:, :TN], C.psr[psN], rstd[:], rstd_r,
                      [xn[x][:, kc, :] for kc in range(KC)], xn_r[x])

        load_h(0)
        norm(0)
        gi = 0
        di = 0
        wgc = 0
        wdc = 0
        for t in range(nt):
            s = t % NH
            x = t % 2
            if t + 1 < nt:
                load_h(t + 1)
            for fc in range(FC):
                w = wgc % NW
                wgc += 1
                P.dma("pool", s_wg[w], I("dma_start", out=wg_sb[w][:], in_=wgu[fc]), writes=[wg_r[w]])
                b = gi % 2
                q = gi % NSG
                gi += 1
                for (gu, bank) in ((0, psG[b]), (1, psU[b])):
                    P.op("pe", [I("matmul", C.ps[bank][:, :TN], wg_sb[w][:, gu, kc, :], xn[x][:, kc, :],
                                  start=(kc == 0), stop=(kc == KC - 1)) for kc in range(KC)],
                         reads=[wg_r[w]] + xn_r[x], writes=[C.psr[bank]])
                P.op("act", I("activation", out=sg[q][:], in_=C.ps[psG[b]][:, :TN], func=AF.Silu),
                     reads=[C.psr[psG[b]]], writes=[sg_r[q]])
                P.op("dve", I("tensor_tensor", h1[:, fc, :], sg[q][:], C.ps[psU[b]][:, :TN], ALU.mult),
                     reads=[sg_r[q], C.psr[psU[b]]], writes=[h1_r[fc]])
                if fc == 11 and t + 1 < nt:
                    norm(t + 1)
            for dc in range(KC):
                w = wdc % NW
                wdc += 1
                P.dma("pool", s_wd[w], I("dma_start", out=wd_sb[w][:], in_=wd[dc]), writes=[wd_r[w]])
                b = di % 2
                di += 1
                P.op("pe", [I("matmul", C.ps[psD[b]][:, :TN], wd_sb[w][:, fc, :], h1[:, fc, :],
                              start=(fc == 0), stop=(fc == FC - 1)) for fc in range(FC)],
                     reads=[wd_r[w]] + h1_r, writes=[C.psr[psD[b]]])
                P.op("dve", I("scalar_tensor_tensor", hb[s][:, dc, :], C.ps[psD[b]][:, :TN], 0.5, hb[s][:, dc, :],
                              ALU.mult, ALU.add),
                     reads=[C.psr[psD[b]], hb_r[s][dc]], writes=[hb_r[s][dc]])
            P.dma("sp", s_st[s], I("dma_start", out=tview(hout, t, TN), in_=hb[s][:]), reads=hb_r[s], writes=[hout_res[t]])
        P.barrier()


def final_norm_phase(P, C, hin, hin_res, out, out_res, gains, gidx, NT, TN=512, name="fin"):
    nt = NT // TN
    with ExitStack() as st:
        hb = [sb(P, st, f"{name}_h{i}", [128, KC, TN], F32) for i in range(2)]
        hb_r = [RL(KC) for _ in range(2)]
        ob = [sb(P, st, f"{name}_o{i}", [128, KC, TN], F32) for i in range(2)]
        ob_r = [RL(KC) for _ in range(2)]
        sq = sb(P, st, f"{name}_sq", [128, KC, TN], BF16)
        sq_r = RL(KC)
        rstd = sb(P, st, f"{name}_rstd", [128, TN], F32)
        rstd_r = Res()
        s_h = [P.S(f"ld_h{i}") for i in range(2)]
        s_st = [P.S(f"st_h{i}") for i in range(2)]
        gl = [gains[:, gidx, kc:kc + 1] for kc in range(KC)]
        for t in range(nt):
            s = t % 2
            P.dma("sp", s_h[s], I("dma_start", out=hb[s][:], in_=tview(hin, t, TN)), reads=[hin_res[t]], writes=hb_r[s])
            emit_norm(P, C, [hb[s][:, kc, :] for kc in range(KC)], hb_r[s], gl,
                      [sq[:, kc, :] for kc in range(KC)], sq_r, C.ps[6][:, :TN], C.psr[6], rstd[:], rstd_r,
                      [ob[s][:, kc, :] for kc in range(KC)], ob_r[s])
            P.dma("sp", s_st[s], I("dma_start", out=tview(out, t, TN), in_=ob[s][:]), reads=ob_r[s], writes=[out_res[t]])
        P.barrier()


def lay_wgu(wg, wu):
    a = wg.reshape(KC, 128, FC, 128).transpose(2, 1, 0, 3)
    b = wu.reshape(KC, 128, FC, 128).transpose(2, 1, 0, 3)
    return np.ascontiguousarray(np.stack([a, b], axis=2))


def lay_wd(wd):
    return np.ascontiguousarray(wd.reshape(FC, 128, KC, 128).transpose(2, 1, 0, 3))


def lay_gain(g):
    return np.ascontiguousarray(g.reshape(KC, 128).T)


NEG = -30000.0
NB = 32
NR = 4
FH = 8
ABIN = 2056


def evenin_phase(P, C, hin, hin_res, w_in_d, bf_sb, gains, gidx, QT_d, agK_in, agV_in, agF_in, agU_in, uT_d,
                 out_res, NT, TN=512, name="ein"):
    nt = NT // TN
    nb = TN // 128
    with ExitStack() as st:
        w = sb(P, st, f"{name}_w", [128, KC, ABIN], BF16)
        w_r = Res()
        hb = [sb(P, st, f"{name}_hb{i}", [128, KC, TN], F32) for i in range(2)]
        hb_r = [RL(KC) for _ in range(2)]
        xn = [sb(P, st, f"{name}_xn{i}", [128, KC, TN], BF16) for i in range(2)]
        xn_r = [RL(KC) for _ in range(2)]
        sq = sb(P, st, f"{name}_sq", [128, KC, TN], BF16)
        sq_r = RL(KC)
        rstd = sb(P, st, f"{name}_rstd", [128, TN], F32)
        rstd_r = Res()
        NE = 4
        ub = [sb(P, st, f"{name}_u{i}", [128, TN], F32) for i in range(NE)]
        ub_r = RL(NE)
        qk = [sb(P, st, f"{name}_qk{i}", [128, TN], BF16) for i in range(NE)]
        qk_r = RL(NE)
        vb = [sb(P, st, f"{name}_v{i}", [128, FH, nb, 65], BF16) for i in range(2)]
        vb_r = RL(2)
        lf = sb(P, st, f"{name}_lf", [8, NT], F32)
        lf_r = Res()
        fa = sb(P, st, f"{name}_fa", [8, TN], F32)
        fb = sb(P, st, f"{name}_fb", [8, TN], F32)
        fc_ = sb(P, st, f"{name}_fc", [8, TN], F32)
        f_r = Res()
        s_h = [P.S(f"ld_h{i}") for i in range(2)]
        s_u = [P.S(f"st_u{i}") for i in range(NE)]
        s_qk = [P.S(f"st_qk{i}") for i in range(NE)]
        s_v = [P.S(f"st_v{i}") for i in range(2)]
        gl = [gains[:, gidx, kc:kc + 1] for kc in range(KC)]
        P.dma("pool", P.S("ld_w0"), I("dma_start", out=w[:], in_=w_in_d), writes=[w_r])
        for i in range(2):
            P.op("dve", I("memset", vb[i][:, :, :, 64:65], 1.0), writes=[vb_r[i]])
        agU_v = agU_in.rearrange("(g p) (i c) -> g p i c", p=128, c=16)

        def load_h(t):
            s = t % 2
            P.dma("sp", s_h[s], I("dma_start", out=hb[s][:], in_=tview(hin, t, TN)), reads=[hin_res[t]], writes=hb_r[s])

        def norm(t):
            s = t % 2
            emit_norm(P, C, [hb[s][:, kc, :] for kc in range(KC)], hb_r[s], gl,
                      [sq[:, kc, :] for kc in range(KC)], sq_r, C.ps[6][:, :TN], C.psr[6], rstd[:], rstd_r,
                      [xn[s][:, kc, :] for kc in range(KC)], xn_r[s])

        load_h(0)
        norm(0)
        ei = 0
        bi = 0
        for t in range(nt):
            s = t % 2
            tok = slice(t * TN, (t + 1) * TN)
            if t + 1 < nt:
                load_h(t + 1)

            def mm_fm(col0, M, bank):
                P.op("pe", [I("matmul", C.ps[bank][:M, :TN], w[:, kc, col0:col0 + M], xn[s][:, kc, :],
                              start=(kc == 0), stop=(kc == KC - 1)) for kc in range(KC)],
                     reads=[w_r] + xn_r[s], writes=[C.psr[bank]])

            for g in range(4):
                bank = bi % 4
                bi += 1
                e = ei % NE
                ei += 1
                mm_fm(g * 128, 128, bank)
                P.op("act", I("activation", out=ub[e][:], in_=C.ps[bank][:, :TN], func=AF.Copy),
                     reads=[C.psr[bank]], writes=[ub_r[e]])
                P.dma("sp", s_u[e], [I("dma_start", out=uT_d[g, :, tok], in_=ub[e][:]),
                                     I("dma_start", out=agU_v[g, :, t * nb:(t + 1) * nb, :],
                                       in_=ub[e][:].rearrange("p (b c) -> p b c", c=128)[:, :, 112:128])],
                      reads=[ub_r[e]], pwrites=[out_res["u"], out_res["agU"]])
            for which in range(2):
                if which == 1 and t + 1 < nt:
                    norm(t + 1)
                for m in range(4):
                    bank = bi % 4
                    bi += 1
                    e = ei % NE
                    ei += 1
                    mm_fm(512 + which * 512 + m * 128, 128, bank)
                    if which == 0:
                        P.op("dve", I("tensor_scalar", qk[e][:], C.ps[bank][:, :TN], 0.125, None, ALU.mult),
                             reads=[C.psr[bank]], writes=[qk_r[e]])
                        dst = QT_d[2 * m:2 * m + 2, :, tok].rearrange("h d n -> (h d) n")
                        P.dma("sp", s_qk[e], I("dma_start", out=dst, in_=qk[e][:]), reads=[qk_r[e]], pwrites=[out_res["q"]])
                    else:
                        P.op("act", I("activation", out=qk[e][:], in_=C.ps[bank][:, :TN], func=AF.Copy),
                             reads=[C.psr[bank]], writes=[qk_r[e]])
                        P.dma("sp", s_qk[e], I("dma_start", out=agK_in[m][:, tok], in_=qk[e][:]), reads=[qk_r[e]],
                              pwrites=[out_res["agK"][m]])
            vs = t % 2
            for blk in range(nb):
                bank = bi % 4
                bi += 1
                P.op("pe", [I("matmul", C.ps[bank][:, :512], xn[s][:, kc, blk * 128:(blk + 1) * 128], w[:, kc, 1536:2048],
                              start=(kc == 0), stop=(kc == KC - 1)) for kc in range(KC)],
                     reads=[w_r] + xn_r[s], writes=[C.psr[bank]])
                eng = "act" if blk % 2 == 0 else "dve"
                src = C.ps[bank][:, :512].rearrange("p (h d) -> p h d", d=64)
                if eng == "act":
                    P.op("act", I("activation", out=vb[vs][:, :, blk, 0:64], in_=src, func=AF.Copy),
                         reads=[C.psr[bank]], pwrites=[vb_r[vs]])
                else:
                    P.op("dve", I("tensor_copy", vb[vs][:, :, blk, 0:64], src), reads=[C.psr[bank]], pwrites=[vb_r[vs]])
            P.dma("sp", s_v[vs], [I("dma_start", out=agV_in[h][:, t * nb * 65:(t + 1) * nb * 65],
                                    in_=vb[vs][:, h, :, :].rearrange("p i c -> p (i c)")) for h in range(FH)],
                  reads=[vb_r[vs]], pwrites=out_res["agV"])
            bank = 4
            mm_fm(2048, 8, bank)
            P.op("dve", I("tensor_scalar", fa[:], C.ps[bank][:8, :TN], bf_sb[:, 0:1], None, ALU.add), reads=[C.psr[bank]], writes=[f_r])
            P.op("dve", I("tensor_scalar", fc_[:], fa[:], -1.0, None, ALU.mult), reads=[f_r], writes=[f_r])
            P.op("dve", I("tensor_tensor", fb[:], fa[:], fc_[:], ALU.min), reads=[f_r], writes=[f_r])
            P.op("act", I("activation", out=fb[:], in_=fb[:], func=AF.Exp), reads=[f_r], writes=[f_r])
            P.op("act", I("activation", out=fb[:], in_=fb[:], func=AF.Ln, bias=1.0), reads=[f_r], writes=[f_r])
            P.op("dve", I("tensor_scalar", fc_[:], fa[:], 0.0, None, ALU.min), reads=[f_r], writes=[f_r])
            P.op("dve", I("tensor_tensor", lf[:, tok], fc_[:], fb[:], ALU.subtract), reads=[f_r], pwrites=[lf_r])
        P.dma("sp", P.S("st_misc"), I("dma_start", out=agF_in, in_=lf[:]), reads=[lf_r], writes=[out_res["agF"]])
        P.barrier()


_cc = [0]
_cc_slots = RL(8, "ccslot")


def allgather(P, src2d, dst2d, src_res, dst_res):
    k = _cc[0] % 8
    _cc[0] += 1
    P.coll(P.S(f"cc{k}"), I("collective_compute", "AllGather", ALU.bypass,
                            replica_groups=[[0, 1, 2, 3], [4, 5, 6, 7]],
                            ins=[src2d.opt()], outs=[dst2d.opt()]),
           reads=src_res, writes=list(dst_res) + [_cc_slots[k]])


def fprep_phase(P, C, agF, agF_res, sel, FK_d, FK_res, FQ_d, FQ_res, NT, name="fp"):
    nb = NT // 128
    CH = 2048
    nch = (4 * NT) // CH
    with ExitStack() as st:
        lf = sb(P, st, f"{name}_lf", [8, nb, 4, 128], F32)
        lf_r = Res()
        ones8 = sb(P, st, f"{name}_ones", [8, CH], F32)
        o_r = Res()
        Fc = [sb(P, st, f"{name}_F{i}", [8, CH], F32) for i in range(2)]
        Fc_r = RL(2)
        G = sb(P, st, f"{name}_G", [8, CH], F32)
        R1 = sb(P, st, f"{name}_R1", [8, CH], F32)
        g_r = Res()
        kp = [sb(P, st, f"{name}_kp{i}", [8, 3, CH], BF16) for i in range(2)]
        kp_r = RL(2)
        qs = [sb(P, st, f"{name}_qs{i}", [8, 3, 512], BF16) for i in range(2)]
        qs_r = RL(2)
        s_kp = [P.S(f"st_kp{i}") for i in range(2)]
        s_qs = [P.S(f"st_qs{i}") for i in range(2)]
        P.op("dve", I("memset", ones8[:], 1.0), writes=[o_r])
        P.dma("sp", P.S("ld_misc"), [I("dma_start", out=lf[:, :, r, :], in_=agF[r * 8:(r + 1) * 8, :].rearrange("h (i p) -> h i p", p=128))
                                     for r in range(4)], reads=[agF_res], writes=[lf_r])
        lff = lf[:].rearrange("h i r p -> h (i r p)")
        FKv = FK_d.rearrange("h c (r i p) -> h c i r p", r=4, p=128)
        for c in range(nch):
            b = c % 2
            init = 0.0 if c == 0 else Fc[1 - b][:, CH - 1:CH]
            P.op("dve", I("tensor_tensor_scan", Fc[b][:], ones8[:], lff[:, c * CH:(c + 1) * CH], init, ALU.mult, ALU.add),
                 reads=[o_r, lf_r, Fc_r[1 - b]], writes=[Fc_r[b]])
            P.op("dve", I("tensor_scalar", G[:], Fc[b][:], -1.0, None, ALU.mult), reads=[Fc_r[b]], writes=[g_r])
            P.op("dve", I("tensor_copy", kp[b][:, 0, :], G[:]), reads=[g_r], writes=[kp_r[b]])
            P.op("dve", I("tensor_tensor", R1[:], G[:], kp[b][:, 0, :], ALU.subtract), reads=[g_r, kp_r[b]], writes=[g_r])
            P.op("dve", I("tensor_copy", kp[b][:, 1, :], R1[:]), reads=[g_r], pwrites=[kp_r[b]])
            P.op("dve", I("tensor_tensor", G[:], R1[:], kp[b][:, 1, :], ALU.subtract), reads=[g_r, kp_r[b]], writes=[g_r])
            P.op("dve", I("tensor_copy", kp[b][:, 2, :], G[:]), reads=[g_r], pwrites=[kp_r[b]])
            P.dma("sp", s_kp[b], [I("dma_start", out=FKv[:, c3, 4 * c:4 * c + 4, r, :],
                                    in_=kp[b][:, c3, :].rearrange("h (i r p) -> h i r p", r=4, p=128)[:, :, r, :])
                                  for c3 in range(3) for r in range(4)],
                  reads=[kp_r[b]], pwrites=[FK_res])
            kv = kp[b][:].rearrange("h c (i r p) -> h c i r p", r=4, p=128)
            for c3 in range(3):
                o = qs[b][:, c3, :].rearrange("h (i p) -> h i p", p=128)
                P.op("dve", I("tensor_scalar", o, kv[:, c3, :, 0, :], sel[0:8, 0:1], None, ALU.mult),
                     reads=[kp_r[b]], writes=[qs_r[b]] if c3 == 0 else (), pwrites=() if c3 == 0 else [qs_r[b]])
                for r in range(1, 4):
                    P.op("dve", I("scalar_tensor_tensor", o, kv[:, c3, :, r, :], sel[0:8, r:r + 1], o, ALU.mult, ALU.add),
                         reads=[kp_r[b], qs_r[b]], pwrites=[qs_r[b]])
            P.dma("sp", s_qs[b], I("dma_start", out=FQ_d[:, :, c * 512:(c + 1) * 512], in_=qs[b][:]),
                  reads=[qs_r[b]], pwrites=[FQ_res])
        P.barrier()


def pool_phase(P, C, uT_d, u_res, agU, agU_res, sel, invc, wpool_d, pscale, yT_d, y_res, NT, name="pl"):
    nb = NT // 128
    nt = NT // 512
    with ExitStack() as st:
        wp = sb(P, st, f"{name}_wp", [128, 4, 128], BF16)
        wp_r = Res()
        P.dma("pool", P.S("ld_w0"), I("dma_start", out=wp[:], in_=wpool_d), writes=[wp_r])
        uext = [sb(P, st, f"{name}_ue{i}", [128, nb, 144], F32) for i in range(2)]
        ue_r = RL(2)
        H = [sb(P, st, f"{name}_H{i}", [128, 4, nb, 16], F32) for i in range(2)]
        H_r = RL(2)
        A = sb(P, st, f"{name}_A", [128, nb, 144], F32)
        B = sb(P, st, f"{name}_B", [128, nb, 144], F32)
        ab_r = Res()
        diff = [sb(P, st, f"{name}_df{i}", [128, nb, 128], BF16) for i in range(2)]
        df_r = RL(2)
        t16 = sb(P, st, f"{name}_t16", [128, 16], F32)
        yb = [sb(P, st, f"{name}_y{i}", [128, NT], BF16) for i in range(2)]
        yb_r = RL(2)
        s_u = [P.S(f"ld_pu{i}") for i in range(2)]
        s_H = [P.S(f"ld_pH{i}") for i in range(2)]
        s_y = [P.S(f"st_py{i}") for i in range(2)]
        agUv = agU.rearrange("(r g p) (i c) -> g p r i c", r=4, g=4, c=16)
        for g in range(4):
            b = g % 2
            w = 2 << g
            P.dma("sp", s_u[b], I("dma_start", out=uext[b][:, :, 16:144], in_=uT_d[g].rearrange("p (i c) -> p i c", c=128)),
                  reads=[u_res], writes=[ue_r[b]])
            P.dma("sp", s_H[b], I("dma_start", out=H[b][:], in_=agUv[g]), reads=[agU_res], writes=[H_r[b]])
            hal = uext[b][:, :, 0:16]
            P.op("dve", I("tensor_scalar", hal, H[b][:, 0, :, :], sel[:, 4:5], None, ALU.mult), reads=[H_r[b]], pwrites=[ue_r[b]])
            for r in range(1, 4):
                P.op("dve", I("scalar_tensor_tensor", hal, H[b][:, r, :, :], sel[:, 4 + r:5 + r], hal, ALU.mult, ALU.add),
                     reads=[H_r[b], ue_r[b]], pwrites=[ue_r[b]])
            if nb > 1:
                P.op("dve", I("scalar_tensor_tensor", uext[b][:, 1:, 0:16], H[b][:, 3, 0:nb - 1, :], sel[:, 8:9], uext[b][:, 1:, 0:16],
                              ALU.mult, ALU.add), reads=[H_r[b], ue_r[b]], pwrites=[ue_r[b]])
            src = uext[b]
            bufs = [A, B]
            lo = 0
            for stp in range(g + 1):
                sh = 1 << stp
                lo = lo + sh
                dst = bufs[stp % 2]
                P.op("dve", I("tensor_tensor", dst[:, :, lo:144], src[:, :, lo:144], src[:, :, lo - sh:144 - sh], ALU.add),
                     reads=[ue_r[b], ab_r], writes=[ab_r])
                src = dst
            sw = src
            P.op("dve", I("scalar_tensor_tensor", diff[b][:], sw[:, :, 16:144], 1.0 / w, uext[b][:, :, 16:144], ALU.mult, ALU.subtract),
                 reads=[ab_r, ue_r[b]], writes=[df_r[b]])
            P.op("dve", I("tensor_tensor", t16[:], sw[:, 0, 16:32], invc[:, g, :], ALU.mult), reads=[ab_r], writes=[ab_r])
            P.op("dve", I("tensor_tensor", diff[b][:, 0, 0:16], t16[:], uext[b][:, 0, 16:32], ALU.subtract),
                 reads=[ab_r, ue_r[b], df_r[b]], pwrites=[df_r[b]])
            for t in range(nt):
                bank = t % 2
                P.op("pe", I("matmul", C.ps[bank][:, :512], wp[:, g, :], diff[b][:, 4 * t:4 * t + 4, :].rearrange("p i c -> p (i c)"),
                             start=True, stop=True), reads=[wp_r, df_r[b]], writes=[C.psr[bank]])
                P.op("act", I("activation", out=yb[b][:, t * 512:(t + 1) * 512], in_=C.ps[bank][:, :512], func=AF.Copy, scale=pscale[:, g:g + 1]),
                     reads=[C.psr[bank]], writes=[yb_r[b]] if t == 0 else (), pwrites=() if t == 0 else [yb_r[b]])
            P.dma("sp", s_y[b], I("dma_start", out=yT_d[g], in_=yb[b][:]), reads=[yb_r[b]], pwrites=[y_res])
        P.barrier()


def attn_phase(P, C, nh, KR, load_head, mask, ident, yT_d, y_res, ychunk0, NT, name="at", LA=2):
    nb = NT // 128
    nt = NT // 512
    with ExitStack() as st:
        KT = [sb(P, st, f"{name}_KT{i}", [KR, 4, NT], BF16) for i in range(2)]
        QT = [sb(P, st, f"{name}_QT{i}", [KR, NT], BF16) for i in range(2)]
        V = [sb(P, st, f"{name}_V{i}", [128, 4, nb, 65], BF16) for i in range(2)]
        hd_r = [dict(KT=Res(), QT=Res(), V=Res()) for _ in range(2)]
        NP = LA + 2
        pT = [sb(P, st, f"{name}_pT{i}", [128, 512], BF16) for i in range(NP)]
        pT_r = RL(NP)
        osb = [sb(P, st, f"{name}_o{i}", [65, 512], F32) for i in range(2)]
        osb_r = RL(2)
        rec = [sb(P, st, f"{name}_rc{i}", [65, 512], F32) for i in range(2)]
        rec_r = RL(2)
        ysb = [sb(P, st, f"{name}_y{i}", [64, 512], BF16) for i in range(2)]
        ysb_r = RL(2)
        s_y = [P.S(f"st_ay{i}") for i in range(2)]
        NS = LA + 1
        psS = list(range(NS))
        psO = [NS, NS + 1]
        psB = NS + 2
        assert psB <= 7
        sems = [dict(KT=P.S(f"ld_KT{i}"), QT=P.S(f"ld_QT{i}"), V=P.S(f"ld_V{i}")) for i in range(2)]
        init_done = [False, False]
        si = 0
        fin = 0
        load_head(0, KT[0], QT[0], V[0], sems[0], hd_r[0], True)
        for h in range(nh):
            hb = h % 2
            if h + 1 < nh:
                load_head(h + 1, KT[1 - hb], QT[1 - hb], V[1 - hb], sems[1 - hb], hd_r[1 - hb], h + 1 < 2)
            kt, qt, v, hr = KT[hb], QT[hb], V[hb], hd_r[hb]
            for T in range(nt):
                blocks = []
                for r in range(4):
                    for i in range(4 * T):
                        blocks.append((r, i, 0, None))
                for qp in range(4):
                    for r in range(4):
                        blocks.append((r, 4 * T + qp, qp * 128, r))
                ob = psO[fin % 2]
                nblk = len(blocks)
                q0 = T * 512

                def emit_S(bi_):
                    r, i, c0, mk = blocks[bi_]
                    bank = psS[(si + bi_) % NS]
                    ins = [I("matmul", C.ps[bank][:, c0:512], kt[:, r, i * 128:(i + 1) * 128], qt[:, q0 + c0:q0 + 512],
                             start=True, stop=(mk is None))]
                    if mk is not None:
                        ins.append(I("matmul", C.ps[bank][:, c0:c0 + 128], ident[:], mask[:, mk, :], start=False, stop=True))
                    P.op("pe", ins, reads=[hr["KT"], hr["QT"], C.r_const], writes=[C.psr[bank]])

                def emit_E(bi_):
                    r, i, c0, mk = blocks[bi_]
                    bank = psS[(si + bi_) % NS]
                    p = (si + bi_) % NP
                    P.op("act", I("activation", out=pT[p][:, c0:512], in_=C.ps[bank][:, c0:512], func=AF.Exp),
                         reads=[C.psr[bank]], writes=[pT_r[p]])

                def emit_PV(bi_):
                    r, i, c0, mk = blocks[bi_]
                    p = (si + bi_) % NP
                    P.op("pe", I("matmul", C.ps[ob][:65, c0:512], v[:, r, i, :], pT[p][:, c0:512],
                                 start=(bi_ == 0), stop=(bi_ == nblk - 1)),
                         reads=[hr["V"], pT_r[p]], writes=[C.psr[ob]] if bi_ == 0 else (), pwrites=() if bi_ == 0 else [C.psr[ob]])

                for bi_ in range(min(LA, nblk)):
                    emit_S(bi_)
                    emit_E(bi_)
                for bi_ in range(nblk):
                    if bi_ + LA < nblk:
                        emit_S(bi_ + LA)
                        emit_E(bi_ + LA)
                    emit_PV(bi_)
                si += nblk
                f = fin % 2
                fin += 1
                P.op("act", I("activation", out=osb[f][:], in_=C.ps[ob][:65, :512], func=AF.Copy), reads=[C.psr[ob]], writes=[osb_r[f]])
                P.op("dve", I("reciprocal", rec[f][64:65, :], osb[f][64:65, :]), reads=[osb_r[f]], writes=[rec_r[f]])
                P.op("pe", I("matmul", C.ps[psB][:64, :512], C.ones_f[64:65, 0:64], rec[f][64:65, :], start=True, stop=True),
                     reads=[rec_r[f], C.r_const], writes=[C.psr[psB]])
                P.op("dve", I("tensor_tensor", ysb[f][:], osb[f][0:64, :], C.ps[psB][:64, :512], ALU.mult),
                     reads=[osb_r[f], C.psr[psB]], writes=[ysb_r[f]])
                ch = ychunk0 + h // 2
                p0 = (h % 2) * 64
                P.dma("sp", s_y[f], I("dma_start", out=yT_d[ch, p0:p0 + 64, q0:q0 + 512], in_=ysb[f][:]),
                      reads=[ysb_r[f]], pwrites=[y_res])
        P.barrier()


def fox_loader(P, agK, agK_res, QT_d, q_res, agV, agV_res, FK_d, FK_res, FQ_d, FQ_res, NT):
    agKv = [a.rearrange("(r h d) n -> h d r n", r=4, d=64) for a in agK]
    agVv = [a.rearrange("(r p) (i c) -> p r i c", r=4, c=65) for a in agV]
    FKv = FK_d.rearrange("h c (r n) -> h c r n", r=4)

    def load_head(h, KT, QT, V, sems, res, first):
        if first:
            P.op("dve", I("memset", KT[64:70, :, :], 1.0), writes=[res["KT"]])
            P.op("dve", I("memset", QT[64:70, :], 1.0), writes=[res["QT"]])
        P.dma("sp", sems["KT"], [I("dma_start", out=KT[0:64, :, :], in_=agKv[h // 2][h % 2]),
                                 I("dma_start", out=KT[64:67, :, :], in_=FKv[h])],
              reads=[agK_res[h // 2], FK_res], writes=[res["KT"]])
        P.dma("sp", sems["QT"], [I("dma_start", out=QT[0:64, :], in_=QT_d[h]),
                                 I("dma_start", out=QT[67:70, :], in_=FQ_d[h])],
              reads=[q_res, FQ_res], writes=[res["QT"]])
        P.dma("sp", sems["V"], I("dma_start", out=V[:], in_=agVv[h]), reads=[agV_res[h]], writes=[res["V"]])
    return load_head


def outproj_phase(P, C, hin, hin_res, hout, hout_res, yT_d, y_res, wout_d, NT, TN=512, name="op"):
    nt = NT // TN
    with ExitStack() as st:
        w = sb(P, st, f"{name}_w", [128, KC, D], BF16)
        w_r = Res()
        P.dma("pool", P.S("ld_w0"), I("dma_start", out=w[:], in_=wout_d), writes=[w_r])
        hb = [sb(P, st, f"{name}_hb{i}", [128, KC, TN], F32) for i in range(2)]
        hb_r = [RL(KC) for _ in range(2)]
        yb = [sb(P, st, f"{name}_yb{i}", [128, KC, TN], BF16) for i in range(2)]
        yb_r = RL(2)
        s_h = [P.S(f"ld_h{i}") for i in range(2)]
        s_y = [P.S(f"ld_y{i}") for i in range(2)]
        s_st = [P.S(f"st_h{i}") for i in range(2)]
        bi = 0

        def load(t):
            s = t % 2
            P.dma("sp", s_h[s], I("dma_start", out=hb[s][:], in_=tview(hin, t, TN)), reads=[hin_res[t]], writes=hb_r[s])
            P.dma("sp", s_y[s], I("dma_start", out=yb[s][:], in_=tview(yT_d, t, TN)), reads=[y_res], writes=[yb_r[s]])

        load(0)
        for t in range(nt):
            s = t % 2
            if t + 1 < nt:
                load(t + 1)
            for dc in range(KC):
                bank = bi % 4
                bi += 1
                P.op("pe", [I("matmul", C.ps[bank][:, :TN], w[:, kc, dc * 128:(dc + 1) * 128], yb[s][:, kc, :],
                              start=(kc == 0), stop=(kc == KC - 1)) for kc in range(KC)],
                     reads=[w_r, yb_r[s]], writes=[C.psr[bank]])
                P.op("dve", I("tensor_tensor", hb[s][:, dc, :], hb[s][:, dc, :], C.ps[bank][:, :TN], ALU.add),
                     reads=[C.psr[bank], hb_r[s][dc]], writes=[hb_r[s][dc]])
            P.dma("sp", s_st[s], I("dma_start", out=tview(hout, t, TN), in_=hb[s][:]), reads=hb_r[s], writes=[hout_res[t]])
        P.barrier()


def core_tables(j, mla=False):
    sel = np.zeros((128, 16), np.float32)
    sel[:, j] = -1.0
    if j > 0:
        sel[:, 4 + (j - 1)] = 1.0
    else:
        sel[:, 8] = 1.0
    invc = np.zeros((128, 4, 16), np.float32)
    for g in range(4):
        w = 2 << g
        if j == 0:
            invc[:, g, :] = 1.0 / np.minimum(np.arange(16) + 1, w)
        else:
            invc[:, g, :] = 1.0 / w
    pk = np.arange(128)[:, None]
    pq = np.arange(128)[None, :]
    mf = np.zeros((128, 4, 128), np.float32)
    mm = np.zeros((128, 4, 128), np.float32)
    for r in range(4):
        if r > j:
            mf[:, r, :] = NEG
            mm[:, r, :] = NEG
        elif r == j:
            mf[:, r, :] = np.where(pk <= pq, 0.0, NEG)
            mm[:, r, :] = np.where(pk // 64 <= pq // 64, 0.0, NEG)
    return sel, invc, mf, mm


def mask01_of(m):
    return (m == 0.0).astype(np.float32)


def lay_kmajor(w):
    K, N = w.shape
    return np.ascontiguousarray(w.reshape(K // 128, 128, N).transpose(1, 0, 2))


def tok_perm(j, nb):
    i = np.arange(nb)[:, None]
    p = np.arange(128)[None, :]
    return ((4 * i + j) * 128 + p).reshape(-1)


MH = 16
MLA_IN = 416
QSCALE = 96.0 ** -0.5
PI = float(np.pi)


def rope_phase(P, C, pos_d, invf, CS_d, CS_res, NT, name="rp"):
    CH = 2048 if NT >= 2048 else NT
    C1 = 6.28125
    C2 = 2.0 * PI - C1
    with ExitStack() as st:
        pi_ = sb(P, st, f"{name}_pi", [128, CH], I32)
        ang = sb(P, st, f"{name}_ang", [128, CH], F32)
        tt = sb(P, st, f"{name}_t", [128, CH], F32)
        m = sb(P, st, f"{name}_m", [128, CH], F32)
        o = [sb(P, st, f"{name}_o{i}", [128, CH], F32) for i in range(2)]
        r = Res()
        o_r = RL(2)
        halfpi = sb(P, st, f"{name}_hpi", [128, 1], F32)
        P.op("dve", I("memset", halfpi[:], PI / 2), writes=[r])
        for c in range(NT // CH):
            sl = slice(c * CH, (c + 1) * CH)
            P.dma("sp", P.S("ld_misc"), I("dma_start", out=pi_[:], in_=pos_d[:, sl]), writes=[r])
            P.op("dve", I("tensor_copy", ang[:], pi_[:]), reads=[r], writes=[r])
            P.op("dve", I("tensor_scalar", ang[:], ang[:], invf[:, 0:1], None, ALU.mult), reads=[r], writes=[r])
            P.op("dve", I("tensor_scalar", tt[:], ang[:], 1.0 / (2.0 * PI), None, ALU.mult), reads=[r], writes=[r])
            P.op("dve", I("tensor_copy", pi_[:], tt[:]), reads=[r], writes=[r])
            P.op("dve", I("tensor_copy", tt[:], pi_[:]), reads=[r], writes=[r])
            P.op("dve", I("scalar_tensor_tensor", m[:], tt[:], -C1, ang[:], ALU.mult, ALU.add), reads=[r], writes=[r])
            P.op("dve", I("scalar_tensor_tensor", m[:], tt[:], -C2, m[:], ALU.mult, ALU.add), reads=[r], writes=[r])
            P.op("dve", I("tensor_scalar", m[:], m[:], -PI, PI, ALU.max, ALU.min), reads=[r], writes=[r])
            P.op("act", I("activation", out=o[1][:], in_=m[:], func=AF.Sin), reads=[r], writes=[o_r[1]])
            P.dma("sp", P.S("st_rp1"), I("dma_start", out=CS_d[1, :, sl], in_=o[1][:]), reads=[o_r[1]], pwrites=[CS_res])
            P.op("dve", I("tensor_scalar", tt[:], m[:], -1.0, None, ALU.mult), reads=[r], writes=[r])
            P.op("dve", I("tensor_tensor", tt[:], tt[:], m[:], ALU.max), reads=[r], writes=[r])
            P.op("act", I("activation", out=o[0][:], in_=tt[:], func=AF.Sin, scale=-1.0, bias=halfpi[:, 0:1]), reads=[r], writes=[o_r[0]])
            P.dma("sp", P.S("st_rp0"), I("dma_start", out=CS_d[0, :, sl], in_=o[0][:]), reads=[o_r[0]], pwrites=[CS_res])
        P.barrier()


def oddin_phase(P, C, hin, hin_res, w_in_d, wqb_d, wkvb_d, rot_d, qn_g, kvn_g, gains, gidx, CS_d, CS_res,
                QT_d, agKn_in, agKr_in, agV_in, out_res, NT, TN=512, name="oin"):
    nt = NT // TN
    nb = TN // 128
    with ExitStack() as st:
        w = sb(P, st, f"{name}_w", [128, KC, MLA_IN], BF16)
        wq = sb(P, st, f"{name}_wq", [128, 2, 1536], BF16)
        wkv = sb(P, st, f"{name}_wkv", [128, 2048], BF16)
        w_r = Res()
        P.dma("pool", P.S("ld_w0"), [I("dma_start", out=w[:], in_=w_in_d), I("dma_start", out=wq[:], in_=wqb_d),
                                     I("dma_start", out=wkv[:], in_=wkvb_d)], writes=[w_r])
        wqr = sb(P, st, f"{name}_wqr", [128, 2, 512], BF16)
        wkr = sb(P, st, f"{name}_wkr", [128, KC, 32], BF16)
        wr_r = Res()
        qv = wq[:, :, 1024:1536].rearrange("p j (h t x) -> p j h t x", t=2, x=16)
        qrv = wqr[:].rearrange("p j (h t x) -> p j h t x", t=2, x=16)
        for j in range(2):
            P.op("dve", I("tensor_scalar", qrv[:, j, :, 0, :], qv[:, j, :, 1, :], -1.0, None, ALU.mult), reads=[w_r], pwrites=[wr_r])
            P.op("dve", I("tensor_copy", qrv[:, j, :, 1, :], qv[:, j, :, 0, :]), reads=[w_r], pwrites=[wr_r])
        P.op("dve", I("tensor_scalar", wkr[:, :, 0:16], w[:, :, 400:416], -1.0, None, ALU.mult), reads=[w_r], pwrites=[wr_r])
        P.op("dve", I("tensor_copy", wkr[:, :, 16:32], w[:, :, 384:400]), reads=[w_r], pwrites=[wr_r])
        hb = [sb(P, st, f"{name}_hb{i}", [128, KC, TN], F32) for i in range(2)]
        hb_r = [RL(KC) for _ in range(2)]
        xn = [sb(P, st, f"{name}_xn{i}", [128, KC, TN], BF16) for i in range(2)]
        xn_r = [RL(KC) for _ in range(2)]
        cs = [sb(P, st, f"{name}_cs{i}", [128, 2, TN], F32) for i in range(2)]
        cs_r = RL(2)
        sq2 = [sb(P, st, f"{name}_sq{i}", [128, KC, TN], BF16) for i in range(2)]
        sq2_r = [RL(KC) for _ in range(2)]
        rstd2 = [sb(P, st, f"{name}_rstd{i}", [128, TN], F32) for i in range(2)]
        rstd2_r = RL(2)
        cl2 = [sb(P, st, f"{name}_cl{i}", [128, 3, TN], F32) for i in range(2)]
        cl2_r = [RL(3) for _ in range(2)]
        cn2 = [sb(P, st, f"{name}_cn{i}", [128, 3, TN], BF16) for i in range(2)]
        cn2_r = [RL(3) for _ in range(2)]
        NE = 6
        eb = [sb(P, st, f"{name}_e{i}", [128, TN], BF16) for i in range(NE)]
        eb_r = RL(NE)
        t1s = [sb(P, st, f"{name}_t1{i}", [128, TN], F32) for i in range(2)]
        t2s = [sb(P, st, f"{name}_t2{i}", [128, TN], F32) for i in range(2)]
        ts_r = RL(2)
        vb = [sb(P, st, f"{name}_v{i}", [128, MH, nb, 65], BF16) for i in range(2)]
        vb_r = RL(2)
        s_h = [P.S(f"ld_h{i}") for i in range(2)]
        s_cs = [P.S(f"ld_cs{i}") for i in range(2)]
        s_e = [P.S(f"st_qk{i}") for i in range(NE)]
        s_v = [P.S(f"st_v{i}") for i in range(2)]
        gl = [gains[:, gidx, kc:kc + 1] for kc in range(KC)]
        for i in range(2):
            P.op("dve", I("memset", vb[i][:, :, :, 64:65], 1.0), writes=[vb_r[i]])

        def load_h(t, what="hc"):
            s = t % 2
            if "h" in what:
                P.dma("pool", s_h[s], I("dma_start", out=hb[s][:], in_=tview(hin, t, TN)), reads=[hin_res[t]], writes=hb_r[s])
            if "c" in what:
                P.dma("pool", s_cs[s], I("dma_start", out=cs[s][:], in_=CS_d[:, :, t * TN:(t + 1) * TN].rearrange("w p n -> p w n")),
                      reads=[CS_res], writes=[cs_r[s]])

        def norm(t):
            s = t % 2
            emit_norm(P, C, [hb[s][:, kc, :] for kc in range(KC)], hb_r[s], gl,
                      [sq2[s][:, kc, :] for kc in range(KC)], sq2_r[s], C.ps[6 + s][:, :TN], C.psr[6 + s], rstd2[s][:], rstd2_r[s],
                      [xn[s][:, kc, :] for kc in range(KC)], xn_r[s])

        load_h(0)
        cnt = dict(b0=0, b1=0, e0=0, e1=0)

        def nbank(s):
            k = cnt[f"b{s}"]
            cnt[f"b{s}"] += 1
            return 3 * s + (k % 3)

        def neb(s):
            k = cnt[f"e{s}"]
            cnt[f"e{s}"] += 1
            return 3 * s + (k % 3)

        def rope_apply(bq, br, M, scale, s, dst_bf, dst_res):
            t1, t2, t_r = t1s[s], t2s[s], ts_r[s]
            P.op("dve", I("scalar_tensor_tensor", t1[:M, :], C.ps[bq][:M, :TN], scale, cs[s][:M, 0, :], ALU.mult, ALU.mult),
                 reads=[C.psr[bq], cs_r[s]], writes=[t_r])
            P.op("dve", I("scalar_tensor_tensor", t2[:M, :], C.ps[br][:M, :TN], scale, cs[s][:M, 1, :], ALU.mult, ALU.mult),
                 reads=[C.psr[br], cs_r[s], t_r], writes=[t_r])
            P.op("dve", I("tensor_tensor", dst_bf, t1[:M, :], t2[:M, :], ALU.add), reads=[t_r], writes=[dst_res])

        def partA(t):
            s = t % 2
            cn, cn_r = cn2[s], cn2_r[s]
            cl, cl_r, sq, sq_r, rstd, rstd_r = cl2[s], cl2_r[s], sq2[s], sq2_r[s], rstd2[s], rstd2_r[s]
            nb6 = 6 + s
            norm(t)
            for j, (c0, M) in enumerate(((0, 128), (128, 128), (256, 128))):
                bank = nbank(s)
                P.op("pe", [I("matmul", C.ps[bank][:M, :TN], w[:, kc, c0:c0 + M], xn[s][:, kc, :],
                              start=(kc == 0), stop=(kc == KC - 1)) for kc in range(KC)],
                     reads=[w_r] + xn_r[s], writes=[C.psr[bank]])
                P.op("act", I("activation", out=cl[:, j, :], in_=C.ps[bank][:, :TN], func=AF.Copy),
                     reads=[C.psr[bank]], writes=[cl_r[j]])
            emit_norm(P, C, [cl[:, j, :] for j in range(2)], cl_r[0:2], [qn_g[:, j:j + 1] for j in range(2)],
                      [sq[:, j, :] for j in range(2)], sq_r, C.ps[nb6][:, :TN], C.psr[nb6], rstd[:], rstd_r,
                      [cn[:, j, :] for j in range(2)], cn_r[0:2], nkc=2, dim=256)
            emit_norm(P, C, [cl[:, 2, :]], cl_r[2:3], [kvn_g[:, 0:1]],
                      [sq[:, 2, :]], sq_r[2:3], C.ps[nb6][:, :TN], C.psr[nb6], rstd[:], rstd_r,
                      [cn[:, 2, :]], cn_r[2:3], nkc=1, dim=128)

        def partB(t):
            s = t % 2
            cn, cn_r = cn2[s], cn2_r[s]
            tok = slice(t * TN, (t + 1) * TN)
            bank = nbank(s)
            bank2 = nbank(s)
            P.op("pe", [I("matmul", C.ps[bank][:32, :TN], w[:, kc, 384:416], xn[s][:, kc, :],
                          start=(kc == 0), stop=(kc == KC - 1)) for kc in range(KC)],
                 reads=[w_r] + xn_r[s], writes=[C.psr[bank]])
            P.op("pe", [I("matmul", C.ps[bank2][:32, :TN], wkr[:, kc, :], xn[s][:, kc, :],
                          start=(kc == 0), stop=(kc == KC - 1)) for kc in range(KC)],
                 reads=[wr_r] + xn_r[s], writes=[C.psr[bank2]])
            e = neb(s)
            rope_apply(bank, bank2, 32, 1.0, s, eb[e][0:32, :], eb_r[e])
            P.dma("sp", s_e[e], I("dma_start", out=agKr_in[:, tok], in_=eb[e][0:32, :]), reads=[eb_r[e]], pwrites=[out_res["agKr"]])
            for m in range(4):
                bank = nbank(s)
                e = neb(s)
                bank2 = nbank(s)
                P.op("pe", [I("matmul", C.ps[bank][:, :TN], wq[:, j, 1024 + m * 128:1024 + (m + 1) * 128], cn[:, j, :],
                              start=(j == 0), stop=(j == 1)) for j in range(2)],
                     reads=[w_r] + cn_r[0:2], writes=[C.psr[bank]])
                P.op("pe", [I("matmul", C.ps[bank2][:, :TN], wqr[:, j, m * 128:(m + 1) * 128], cn[:, j, :],
                              start=(j == 0), stop=(j == 1)) for j in range(2)],
                     reads=[wr_r] + cn_r[0:2], writes=[C.psr[bank2]])
                rope_apply(bank, bank2, 128, QSCALE, s, eb[e][:], eb_r[e])
                P.dma("sp", s_e[e], [I("dma_start", out=QT_d[4 * m + hh, 64:96, tok], in_=eb[e][hh * 32:(hh + 1) * 32, :]) for hh in range(4)],
                      reads=[eb_r[e]], pwrites=[out_res["q"]])
            for m in range(8):
                bank = nbank(s)
                e = neb(s)
                P.op("pe", [I("matmul", C.ps[bank][:, :TN], wq[:, j, m * 128:(m + 1) * 128], cn[:, j, :],
                              start=(j == 0), stop=(j == 1)) for j in range(2)],
                     reads=[w_r] + cn_r[0:2], writes=[C.psr[bank]])
                P.op("act", I("activation", out=eb[e][:], in_=C.ps[bank][:, :TN], func=AF.Copy, scale=QSCALE),
                     reads=[C.psr[bank]], writes=[eb_r[e]])
                P.dma("sp", s_e[e], [I("dma_start", out=QT_d[2 * m + hh, 0:64, tok], in_=eb[e][hh * 64:(hh + 1) * 64, :]) for hh in range(2)],
                      reads=[eb_r[e]], pwrites=[out_res["q"]])
            for m in range(8):
                bank = nbank(s)
                e = neb(s)
                P.op("pe", I("matmul", C.ps[bank][:, :TN], wkv[:, m * 128:(m + 1) * 128], cn[:, 2, :], start=True, stop=True),
                     reads=[w_r, cn_r[2]], writes=[C.psr[bank]])
                P.op("act", I("activation", out=eb[e][:], in_=C.ps[bank][:, :TN], func=AF.Copy),
                     reads=[C.psr[bank]], writes=[eb_r[e]])
                P.dma("sp", s_e[e], I("dma_start", out=agKn_in[m][:, tok], in_=eb[e][:]), reads=[eb_r[e]], pwrites=[out_res["agKn"][m]])
            vs = t % 2
            for blk in range(nb):
                for half in range(2):
                    bank = nbank(s)
                    P.op("pe", I("matmul", C.ps[bank][:, :512], cn[:, 2, blk * 128:(blk + 1) * 128],
                                 wkv[:, 1024 + half * 512:1024 + (half + 1) * 512], start=True, stop=True),
                         reads=[w_r, cn_r[2]], writes=[C.psr[bank]])
                    src = C.ps[bank][:, :512].rearrange("p (h d) -> p h d", d=64)
                    dst = vb[vs][:, half * 8:(half + 1) * 8, blk, 0:64]
                    if half == 0:
                        P.op("act", I("activation", out=dst, in_=src, func=AF.Copy), reads=[C.psr[bank]], pwrites=[vb_r[vs]])
                    else:
                        P.op("dve", I("tensor_copy", dst, src), reads=[C.psr[bank]], pwrites=[vb_r[vs]])
            P.dma("sp", s_v[vs], [I("dma_start", out=agV_in[h][:, t * nb * 65:(t + 1) * nb * 65],
                                    in_=vb[vs][:, h, :, :].rearrange("p i c -> p (i c)")) for h in range(MH)],
                  reads=[vb_r[vs]], pwrites=out_res["agV"])

        def tile_full(t):
            partA(t)
            partB(t)

        load_h(1) if nt > 1 else None
        for t0 in range(0, nt, 2):
            L0 = P.capture(lambda: tile_full(t0))
            L1 = P.capture(lambda: tile_full(t0 + 1)) if t0 + 1 < nt else []
            import os as _os
            if _os.environ.get("SEQ"):
                P.replay(L0)
                P.replay(L1)
                continue
            if _os.environ.get("CHECKIL"):
                lastw = {}
                for k in range(max(len(L0), len(L1))):
                    for tid, L in ((0, L0), (1, L1)):
                        if k >= len(L):
                            continue
                        kind, a_, kw_ = L[k]
                        for R in kw_["reads"]:
                            if id(R) in lastw and lastw[id(R)][0] != tid:
                                print("INTERLEAVE VIOLATION: tile", tid, "op", k, kind, a_[0], "reads", R.name, "last written by tile", lastw[id(R)], "instr", (a_[1] if kind == "op" else a_[2])[0][0] if not isinstance((a_[1] if kind == "op" else a_[2])[0], str) else (a_[1] if kind == "op" else a_[2])[0])
                        for R in list(kw_["writes"]) + list(kw_["pwrites"]):
                            lastw[id(R)] = (tid, k)
            for k in range(max(len(L0), len(L1))):
                if k < len(L0):
                    P.replay([L0[k]])
                if k < len(L1):
                    P.replay([L1[k]])
                if k == 60:
                    for tt in (t0 + 2, t0 + 3):
                        if tt < nt:
                            load_h(tt, "h")
            for tt in (t0 + 2, t0 + 3):
                if tt < nt:
                    load_h(tt, "c")
        P.barrier()


def mla_loader(P, agKn, agKn_res, agKr, agKr_res, QT_d, q_res, agV, agV_res, NT):
    agKv = [a.rearrange("(r h d) n -> h d r n", r=4, d=64) for a in agKn]
    agKrv = agKr.rearrange("(r d) n -> d r n", r=4)
    agVv = [a.rearrange("(r p) (i c) -> p r i c", r=4, c=65) for a in agV]

    def load_head(h, KT, QT, V, sems, res, first):
        P.dma("sp", sems["KT"], [I("dma_start", out=KT[0:64, :, :], in_=agKv[h // 2][h % 2]),
                                 I("dma_start", out=KT[64:96, :, :], in_=agKrv)],
              reads=[agKn_res[h // 2], agKr_res], writes=[res["KT"]])
        P.dma("sp", sems["QT"], I("dma_start", out=QT[:, :], in_=QT_d[h]), reads=[q_res], writes=[res["QT"]])
        P.dma("sp", sems["V"], I("dma_start", out=V[:], in_=agVv[h]), reads=[agV_res[h]], writes=[res["V"]])
    return load_head


def rope_consts():
    inv = (np.float32(10000.0) ** (-np.arange(0, 32, 2, dtype=np.float32) / np.float32(32))).astype(np.float32)
    invf = np.zeros((128, 1), np.float32)
    rot = np.zeros((128, 128), np.float32)
    for p in range(128):
        invf[p, 0] = inv[(p % 32) % 16]
        if p % 32 < 16:
            rot[p + 16, p] = -1.0
        else:
            rot[p - 16, p] = 1.0
    return invf, rot


def lay_wqb(wqb):
    w = wqb.reshape(256, 16, 96)
    w2 = np.concatenate([w[:, :, :64].reshape(256, 1024), w[:, :, 64:].reshape(256, 512)], axis=1)
    return np.ascontiguousarray(w2.reshape(2, 128, 1536).transpose(1, 0, 2))


def lay_wkvb(wkvb):
    w = wkvb.reshape(128, 16, 128)
    return np.ascontiguousarray(np.concatenate([w[:, :, :64].reshape(128, 1024), w[:, :, 64:].reshape(128, 1024)], axis=1))


def ffn_phase2(P, C, stages, hin0, hin0_res, hT, h_res, gains, NT, TN=1024, name="ffn", D_PF=2, final=None):
    nt = NT // TN
    NH = TN // 512
    jobs = [(s, t) for s in range(len(stages)) for t in range(nt)]
    with ExitStack() as st:
        hb = [sb(P, st, f"{name}_hb{i}", [128, KC, TN], F32) for i in range(2)]
        hb_r = [[RL(KC, "hb") for _ in range(NH)] for _ in range(2)]
        xn = [sb(P, st, f"{name}_xn{i}", [128, KC, TN], BF16) for i in range(2)]
        xn_r = [[RL(KC, "xn") for _ in range(NH)] for _ in range(2)]
        h1 = sb(P, st, f"{name}_h1", [128, FC, TN], BF16)
        h1_r = [RL(FC, "h1") for _ in range(NH)]
        NSG = 3
        sg = [sb(P, st, f"{name}_sg{i}", [128, 512], F32) for i in range(NSG)]
        sg_r = RL(NSG, "sg")
        NW = 3
        wg_sb = [sb(P, st, f"{name}_wgu{i}", [128, 2, KC, 128], BF16) for i in range(NW)]
        wg_r = RL(NW, "wgu")
        wd_sb = [sb(P, st, f"{name}_wd{i}", [128, FC, 128], BF16) for i in range(NW)]
        wd_r = RL(NW, "wd")
        s_h = [P.S(f"ld_h{i}") for i in range(2)]
        s_st = [P.S(f"st_h{i}") for i in range(2)]
        s_wg = [P.S(f"ld_wg{i}") for i in range(NW)]
        s_wd = [P.S(f"ld_wd{i}") for i in range(NW)]
        psG, psU, psD, psN = [0, 1], [2, 3], [4, 5], 6

        uses = []
        for j, (s_, t) in enumerate(jobs):
            for fc in range(FC):
                uses.append(("g", j, fc))
            for dc in range(KC):
                uses.append(("d", j, dc))
        slot_of = []
        cg = cd = 0
        for (kind, j, idx) in uses:
            if kind == "g":
                slot_of.append(cg % NW)
                cg += 1
            else:
                slot_of.append(cd % NW)
                cd += 1

        def issue_load(k):
            if k >= len(uses):
                return
            kind, j, idx = uses[k]
            w = slot_of[k]
            wgu_d, wd_d, _ = stages[jobs[j][0]]
            if kind == "g":
                P.dma("pool", s_wg[w], I("dma_start", out=wg_sb[w][:], in_=wgu_d[idx]), writes=[wg_r[w]])
            else:
                P.dma("pool", s_wd[w], I("dma_start", out=wd_sb[w][:], in_=wd_d[idx]), writes=[wd_r[w]])

        def load_h(j):
            s_, t = jobs[j]
            sl = j % 2
            src, src_res = (hin0, hin0_res) if s_ == 0 else (hT, h_res)
            P.dma("sp", s_h[sl], I("dma_start", out=hb[sl][:], in_=tview(src, t, TN)),
                  reads=src_res[t * NH:(t + 1) * NH], writes=[r for hh in range(NH) for r in hb_r[sl][hh]])

        sqf = sb(P, st, f"{name}_sqf", [128, KC, TN], BF16)
        sqf_r = [RL(KC, "sqf") for _ in range(NH)]
        rstdf = sb(P, st, f"{name}_rstdf", [128, TN], F32)
        rstdf_r = RL(NH, "rstdf")
        psNb = [6, 7]

        def norm_sq(j):
            sl = j % 2
            for hh in range(NH):
                c = slice(hh * 512, (hh + 1) * 512)
                for kc in range(KC):
                    P.op("dve", I("tensor_tensor", sqf[:, kc, c], hb[sl][:, kc, c], hb[sl][:, kc, c], ALU.mult),
                         reads=[hb_r[sl][hh][kc]], writes=[sqf_r[hh][kc]])

        def norm_rest(j):
            sl = j % 2
            gidx = stages[jobs[j][0]][2]
            for hh in range(NH):
                c = slice(hh * 512, (hh + 1) * 512)
                bank = psNb[hh % 2]
                P.op("pe", [I("matmul", C.ps[bank][:, :512], C.ones_bf[:], sqf[:, kc, c], start=(kc == 0), stop=(kc == KC - 1))
                            for kc in range(KC)], reads=[C.r_const] + sqf_r[hh], writes=[C.psr[bank]])
            for hh in range(NH):
                c = slice(hh * 512, (hh + 1) * 512)
                bank = psNb[hh % 2]
                P.op("dve", I("tensor_scalar", rstdf[:, c], C.ps[bank][:, :512], 1.0 / D, EPS, ALU.mult, ALU.add),
                     reads=[C.psr[bank]], writes=[rstdf_r[hh]])
            for hh in range(NH):
                c = slice(hh * 512, (hh + 1) * 512)
                P.op("act", I("activation", out=rstdf[:, c], in_=rstdf[:, c], func=AF.Sqrt), reads=[rstdf_r[hh]], writes=[rstdf_r[hh]])
            for hh in range(NH):
                c = slice(hh * 512, (hh + 1) * 512)
                P.op("dve", I("reciprocal", rstdf[:, c], rstdf[:, c]), reads=[rstdf_r[hh]], writes=[rstdf_r[hh]])
                for kc in range(KC):
                    P.op("dve", I("scalar_tensor_tensor", xn[sl][:, kc, c], hb[sl][:, kc, c], gains[:, gidx, kc:kc + 1], rstdf[:, c],
                                  ALU.mult, ALU.mult), reads=[hb_r[sl][hh][kc], rstdf_r[hh]], writes=[xn_r[sl][hh][kc]])

        def norm(j):
            norm_sq(j)
            norm_rest(j)

        def emit_final(j):
            gfin, out_d, out_res = final
            s_, t = jobs[j]
            sl = j % 2
            for hh in range(NH):
                c = slice(hh * 512, (hh + 1) * 512)
                for kc in range(KC):
                    P.op("dve", I("tensor_tensor", sqf[:, kc, c], hb[sl][:, kc, c], hb[sl][:, kc, c], ALU.mult),
                         reads=[hb_r[sl][hh][kc]], writes=[sqf_r[hh][kc]])
            for hh in range(NH):
                c = slice(hh * 512, (hh + 1) * 512)
                bank = psNb[hh % 2]
                P.op("pe", [I("matmul", C.ps[bank][:, :512], C.ones_bf[:], sqf[:, kc, c], start=(kc == 0), stop=(kc == KC - 1))
                            for kc in range(KC)], reads=[C.r_const] + sqf_r[hh], writes=[C.psr[bank]])
            for hh in range(NH):
                c = slice(hh * 512, (hh + 1) * 512)
                bank = psNb[hh % 2]
                P.op("dve", I("tensor_scalar", rstdf[:, c], C.ps[bank][:, :512], 1.0 / D, EPS, ALU.mult, ALU.add),
                     reads=[C.psr[bank]], writes=[rstdf_r[hh]])
            for hh in range(NH):
                c = slice(hh * 512, (hh + 1) * 512)
                P.op("act", I("activation", out=rstdf[:, c], in_=rstdf[:, c], func=AF.Sqrt), reads=[rstdf_r[hh]], writes=[rstdf_r[hh]])
            for hh in range(NH):
                c = slice(hh * 512, (hh + 1) * 512)
                P.op("dve", I("reciprocal", rstdf[:, c], rstdf[:, c]), reads=[rstdf_r[hh]], writes=[rstdf_r[hh]])
                for kc in range(KC):
                    P.op("dve", I("scalar_tensor_tensor", hb[sl][:, kc, c], hb[sl][:, kc, c], gains[:, gfin, kc:kc + 1], rstdf[:, c],
                                  ALU.mult, ALU.mult), reads=[hb_r[sl][hh][kc], rstdf_r[hh]], writes=[hb_r[sl][hh][kc]])
            P.dma("sp", s_st[sl], I("dma_start", out=tview(out_d, t, TN), in_=hb[sl][:]),
                  reads=[r for hh in range(NH) for r in hb_r[sl][hh]], writes=out_res[t * NH:(t + 1) * NH])

        last_stage = len(stages) - 1
        pend_final = [None]
        for k in range(D_PF):
            issue_load(k)
        load_h(0)
        norm(0)
        gi = 0
        di = 0
        k = 0
        for j, (s_, t) in enumerate(jobs):
            sl = j % 2
            defer_load = final is not None and pend_final[0] is not None
            if j + 1 < len(jobs) and not defer_load:
                load_h(j + 1)
            for fc in range(FC):
                if fc == 1 and defer_load:
                    emit_final(pend_final[0])
                    pend_final[0] = None
                    if j + 1 < len(jobs):
                        load_h(j + 1)
                issue_load(k + D_PF)
                w = slot_of[k]
                k += 1
                for hh in range(NH):
                    c = slice(hh * 512, (hh + 1) * 512)
                    b = gi % 2
                    q = gi % NSG
                    gi += 1
                    for (gu, bank) in ((0, psG[b]), (1, psU[b])):
                        P.op("pe", [I("matmul", C.ps[bank][:, :512], wg_sb[w][:, gu, kc, :], xn[sl][:, kc, c],
                                      start=(kc == 0), stop=(kc == KC - 1)) for kc in range(KC)],
                             reads=[wg_r[w]] + xn_r[sl][hh], writes=[C.psr[bank]])
                    P.op("act", I("activation", out=sg[q][:], in_=C.ps[psG[b]][:, :512], func=AF.Silu),
                         reads=[C.psr[psG[b]]], writes=[sg_r[q]])
                    P.op("dve", I("tensor_tensor", h1[:, fc, c], sg[q][:], C.ps[psU[b]][:, :512], ALU.mult),
                         reads=[sg_r[q], C.psr[psU[b]]], writes=[h1_r[hh][fc]])
                if fc == (7 if defer_load else 5) and j + 1 < len(jobs):
                    norm_sq(j + 1)
                if fc == 11 and j + 1 < len(jobs):
                    norm_rest(j + 1)
            for dc in range(KC):
                issue_load(k + D_PF)
                w = slot_of[k]
                k += 1
                for hh in range(NH):
                    c = slice(hh * 512, (hh + 1) * 512)
                    b = di % 2
                    di += 1
                    P.op("pe", [I("matmul", C.ps[psD[b]][:, :512], wd_sb[w][:, fc, :], h1[:, fc, c],
                                  start=(fc == 0), stop=(fc == FC - 1)) for fc in range(FC)],
                         reads=[wd_r[w]] + h1_r[hh], writes=[C.psr[psD[b]]])
                    P.op("dve", I("scalar_tensor_tensor", hb[sl][:, dc, c], C.ps[psD[b]][:, :512], 0.5, hb[sl][:, dc, c],
                                  ALU.mult, ALU.add),
                         reads=[C.psr[psD[b]], hb_r[sl][hh][dc]], writes=[hb_r[sl][hh][dc]])
            if final is not None and s_ == last_stage:
                pend_final[0] = j
            else:
                P.dma("sp", s_st[sl], I("dma_start", out=tview(hT, t, TN), in_=hb[sl][:]),
                      reads=[r for hh in range(NH) for r in hb_r[sl][hh]], writes=h_res[t * NH:(t + 1) * NH])
        if pend_final[0] is not None:
            emit_final(pend_final[0])
        P.barrier()


def attn_phase2(P, C, nh, KR, load_head, mask, ident, yT_d, y_res, ychunk0, NT, name="at", LA=2, side_fn=None, side_pull=4, side_start=6, mask01=None):
    nb = NT // 128
    nt = NT // 512
    with ExitStack() as st:
        KT = [sb(P, st, f"{name}_KT{i}", [KR, 4, NT], BF16) for i in range(2)]
        QT = [sb(P, st, f"{name}_QT{i}", [KR, NT], BF16) for i in range(2)]
        V = [sb(P, st, f"{name}_V{i}", [128, 4, nb, 65], BF16) for i in range(2)]
        hd_r = [dict(KT=Res(), QT=Res(), V=Res()) for _ in range(2)]
        NS = LA + 1
        assert NS <= 3
        NP = LA + 2
        pT = [sb(P, st, f"{name}_pT{i}", [128, 2, 512], BF16) for i in range(NP)]
        pT_r = RL(NP)
        osb = [sb(P, st, f"{name}_o{i}", [65, 512], F32) for i in range(2)]
        osb_r = RL(2)
        rec = [sb(P, st, f"{name}_rc{i}", [65, 512], F32) for i in range(2)]
        rec_r = RL(2)
        ysb = [sb(P, st, f"{name}_y{i}", [64, 512], BF16) for i in range(2)]
        ysb_r = RL(2)
        s_y = [P.S(f"st_ay{i}") for i in range(2)]
        psO = [6, 7]
        sems = [dict(KT=P.S(f"ld_KT{i}"), QT=P.S(f"ld_QT{i}"), V=P.S(f"ld_V{i}")) for i in range(2)]
        S3 = [C.ps2[k][:, :].rearrange("p (b n) -> p b n", b=2) for k in range(NS)]
        st_ = dict(si=0, pi=0, fin=0, pending=None)

        def slot_res(k):
            return [C.psr[2 * k], C.psr[2 * k + 1]]

        def flush_pending():
            pd = st_["pending"]
            if pd is None:
                return
            st_["pending"] = None
            f, h, q0 = pd
            k = st_["si"] % NS
            st_["si"] += 1
            P.op("pe", I("matmul", C.ps[2 * k][:64, :512], C.ones_f[64:65, 0:64], rec[f][64:65, :], start=True, stop=True),
                 reads=[rec_r[f], C.r_const], writes=slot_res(k))
            P.op("dve", I("tensor_tensor", ysb[f][:], osb[f][0:64, :], C.ps[2 * k][:64, :512], ALU.mult),
                 reads=[osb_r[f]] + slot_res(k), writes=[ysb_r[f]])
            ch = ychunk0 + h // 2
            p0 = (h % 2) * 64
            P.dma("sp", s_y[f], I("dma_start", out=yT_d[ch, p0:p0 + 64, q0:q0 + 512], in_=ysb[f][:]),
                  reads=[ysb_r[f]], pwrites=[y_res])

        side = list(side_fn(st)) if side_fn is not None else []
        side_pos = [0]

        def pump_side(n):
            for _ in range(n):
                if side_pos[0] >= len(side):
                    return
                kind, a, kw = side[side_pos[0]]
                side_pos[0] += 1
                if kind == "gap":
                    return
                if kind == "slotmm":
                    k = st_["si"] % NS
                    st_["si"] += 1
                    P.op("pe", I("matmul", C.ps[2 * k][:, :512], a["lhsT"], a["rhs"], start=True, stop=True),
                         reads=a["mm_reads"], writes=slot_res(k))
                    P.op("dve", a["evac"](C.ps[2 * k][:, :512]), reads=slot_res(k), writes=a["ev_writes"], pwrites=a["ev_pwrites"])
                    return
                else:
                    getattr(P, kind)(*a, **kw)
                    if side_pos[0] < len(side) and side[side_pos[0]][0] == "slotmm":
                        return

        load_head(0, KT[0], QT[0], V[0], sems[0], hd_r[0], True)
        for h in range(nh):
            hb = h % 2
            kt, qt, v, hr = KT[hb], QT[hb], V[hb], hd_r[hb]
            for T in range(nt):
                if T == nt // 2 and h + 1 < nh:
                    load_head(h + 1, KT[1 - hb], QT[1 - hb], V[1 - hb], sems[1 - hb], hd_r[1 - hb], h + 1 < 2)
                if h * nt + T >= side_start:
                    pump_side(side_pull)
                pairs = []
                full = [(r, i) for r in range(4) for i in range(4 * T)]
                for a in range(0, len(full), 2):
                    pairs.append((0, [(full[a][0], full[a][1], None), (full[a + 1][0], full[a + 1][1], None)]))
                for qp in range(4):
                    for r0 in (0, 2):
                        pairs.append((qp * 128, [(r0, 4 * T + qp, r0), (r0 + 1, 4 * T + qp, r0 + 1)]))
                ob = psO[st_["fin"] % 2]
                npair = len(pairs)
                q0 = T * 512
                slots = {}

                def emit_S(pi_):
                    c0, blks = pairs[pi_]
                    k = st_["si"] % NS
                    st_["si"] += 1
                    p = st_["pi"] % NP
                    st_["pi"] += 1
                    slots[pi_] = (k, p)
                    ins = []
                    for half, (r, i, mk) in enumerate(blks):
                        ins.append(I("matmul", S3[k][:, half, c0:512], kt[:, r, i * 128:(i + 1) * 128], qt[:, q0 + c0:q0 + 512],
                                     start=True, stop=(mk is None)))
                        if mk is not None and mask01 is None:
                            ins.append(I("matmul", S3[k][:, half, c0:c0 + 128], ident[:], mask[:, mk, :], start=False, stop=True))
                    if mask01 is not None:
                        ins = [(nm, ar, dict(kw_, stop=True)) for (nm, ar, kw_) in ins]
                    P.op("pe", ins, reads=[hr["KT"], hr["QT"], C.r_const], writes=slot_res(k))
                    P.op("act", I("activation", out=pT[p][:, :, c0:512], in_=S3[k][:, :, c0:512], func=AF.Exp),
                         reads=slot_res(k), writes=[pT_r[p]])
                    if mask01 is not None:
                        for half, (r, i, mk) in enumerate(blks):
                            if mk is not None:
                                P.op("dve", I("tensor_tensor", pT[p][:, half, c0:c0 + 128], pT[p][:, half, c0:c0 + 128], mask01[:, mk, :], ALU.mult),
                                     reads=[pT_r[p], C.r_const], writes=[pT_r[p]])

                def emit_PV(pi_):
                    c0, blks = pairs[pi_]
                    k, p = slots[pi_]
                    ins = []
                    for half, (r, i, mk) in enumerate(blks):
                        ins.append(I("matmul", C.ps[ob][:65, c0:512], v[:, r, i, :], pT[p][:, half, c0:512],
                                     start=(pi_ == 0 and half == 0), stop=(pi_ == npair - 1 and half == 1)))
                    P.op("pe", ins, reads=[hr["V"], pT_r[p]], writes=[C.psr[ob]] if pi_ == 0 else (), pwrites=() if pi_ == 0 else [C.psr[ob]])

                for pi_ in range(min(LA, npair)):
                    emit_S(pi_)
                for pi_ in range(npair):
                    if pi_ + LA < npair:
                        emit_S(pi_ + LA)
                    emit_PV(pi_)
                    if pi_ == 2:
                        flush_pending()
                flush_pending()
                f = st_["fin"] % 2
                st_["fin"] += 1
                P.op("act", I("activation", out=osb[f][:], in_=C.ps[ob][:65, :512], func=AF.Copy), reads=[C.psr[ob]], writes=[osb_r[f]])
                P.op("dve", I("reciprocal", rec[f][64:65, :], osb[f][64:65, :]), reads=[osb_r[f]], writes=[rec_r[f]])
                st_["pending"] = (f, h, q0)
        flush_pending()
        while side_pos[0] < len(side):
            pump_side(len(side))
        P.barrier()


def fprep_phase2(P, C, agF, agF_res, sel, fmat_d, FK_d, FK_res, FQ_d, FQ_res, NT, name="fp"):
    nb = NT // 128
    nbi = nb // 16
    CS = nbi * 4 * 128
    with ExitStack() as st:
        lf = sb(P, st, f"{name}_lf", [128, nbi, 4, 128], F32)
        lf_r = Res()
        fm = sb(P, st, f"{name}_fm", [128, 128], F32)
        ones = sb(P, st, f"{name}_ones", [128, CS], F32)
        c_r = Res()
        Fl = sb(P, st, f"{name}_Fl", [128, CS], F32)
        G = sb(P, st, f"{name}_G", [128, CS], F32)
        R1 = sb(P, st, f"{name}_R1", [128, CS], F32)
        off = sb(P, st, f"{name}_off", [128, 1], F32)
        g_r = Res()
        kp = sb(P, st, f"{name}_kp", [128, 3, CS], BF16)
        kp_r = Res()
        qs = sb(P, st, f"{name}_qs", [128, 3, nbi * 128], BF16)
        qs_r = Res()
        P.op("dve", I("memset", ones[:], 1.0), writes=[c_r])
        P.dma("sp", P.S("ld_fm"), I("dma_start", out=fm[:], in_=fmat_d), writes=[c_r])
        P.dma("sp", P.S("ld_misc"), [I("dma_start", out=lf[:, :, r, :],
                                       in_=agF[r * 8:(r + 1) * 8, :].rearrange("h (c i p) -> (h c) i p", c=16, p=128))
                                     for r in range(4)], reads=[agF_res], writes=[lf_r])
        lff = lf[:].rearrange("q i r p -> q (i r p)")
        P.op("dve", I("tensor_tensor_scan", Fl[:], ones[:], lff, 0.0, ALU.mult, ALU.add), reads=[c_r, lf_r], writes=[g_r])
        P.op("pe", I("matmul", C.ps[0][:, 0:1], fm[:], Fl[:, CS - 1:CS], start=True, stop=True), reads=[c_r, g_r], writes=[C.psr[0]])
        P.op("dve", I("tensor_copy", off[:], C.ps[0][:, 0:1]), reads=[C.psr[0]], writes=[g_r])
        P.op("dve", I("tensor_scalar", G[:], Fl[:], off[:, 0:1], -1.0, ALU.add, ALU.mult), reads=[g_r], writes=[g_r])
        P.op("dve", I("tensor_copy", kp[:, 0, :], G[:]), reads=[g_r], writes=[kp_r])
        P.op("dve", I("tensor_tensor", R1[:], G[:], kp[:, 0, :], ALU.subtract), reads=[g_r, kp_r], writes=[g_r])
        P.op("dve", I("tensor_copy", kp[:, 1, :], R1[:]), reads=[g_r], pwrites=[kp_r])
        P.op("dve", I("tensor_tensor", G[:], R1[:], kp[:, 1, :], ALU.subtract), reads=[g_r, kp_r], writes=[g_r])
        P.op("dve", I("tensor_copy", kp[:, 2, :], G[:]), reads=[g_r], pwrites=[kp_r])
        kv = kp[:].rearrange("q c (i r p) -> q c i r p", r=4, p=128)
        P.dma("sp", P.S("st_kp0"), [I("dma_start", out=FK_d[c3, r].rearrange("h (c i p) -> (h c) i p", c=16, p=128), in_=kv[:, c3, :, r, :])
                                    for c3 in range(3) for r in range(4)], reads=[kp_r], writes=[FK_res])
        for c3 in range(3):
            o = qs[:, c3, :].rearrange("q (i p) -> q i p", p=128)
            P.op("dve", I("tensor_scalar", o, kv[:, c3, :, 0, :], sel[:, 0:1], None, ALU.mult),
                 reads=[kp_r], writes=[qs_r] if c3 == 0 else (), pwrites=() if c3 == 0 else [qs_r])
            for r in range(1, 4):
                P.op("dve", I("scalar_tensor_tensor", o, kv[:, c3, :, r, :], sel[:, r:r + 1], o, ALU.mult, ALU.add),
                     reads=[kp_r, qs_r], pwrites=[qs_r])
        P.dma("sp", P.S("st_qs0"), [I("dma_start", out=FQ_d[c3].rearrange("h (c i p) -> (h c) i p", c=16, p=128),
                                      in_=qs[:, c3, :].rearrange("q (i p) -> q i p", p=128)) for c3 in range(3)],
              reads=[qs_r], writes=[FQ_res])
        P.barrier()


def fox_loader2(P, agK, agK_res, QT_d, q_res, agV, agV_res, FK_d, FK_res, FQ_d, FQ_res, NT):
    agKv = [a.rearrange("(r h d) n -> h d r n", r=4, d=64) for a in agK]
    agVv = [a.rearrange("(r p) (i c) -> p r i c", r=4, c=65) for a in agV]

    def load_head(h, KT, QT, V, sems, res, first):
        if first:
            P.op("dve", I("memset", KT[64:70, :, :], 1.0), writes=[res["KT"]])
            P.op("dve", I("memset", QT[64:70, :], 1.0), writes=[res["QT"]])
        P.dma("sp", sems["KT"], [I("dma_start", out=KT[0:64, :, :], in_=agKv[h // 2][h % 2]),
                                 I("dma_start", out=KT[64:67, :, :], in_=FK_d[:, :, h, :])],
              reads=[agK_res[h // 2], FK_res], writes=[res["KT"]])
        P.dma("sp", sems["QT"], [I("dma_start", out=QT[0:64, :], in_=QT_d[h]),
                                 I("dma_start", out=QT[67:70, :], in_=FQ_d[:, h, :])],
              reads=[q_res, FQ_res], writes=[res["QT"]])
        P.dma("sp", sems["V"], I("dma_start", out=V[:], in_=agVv[h]), reads=[agV_res[h]], writes=[res["V"]])
    return load_head


def fmat_const():
    m = np.zeros((128, 128), np.float32)
    for k in range(128):
        for mm in range(128):
            if k // 16 == mm // 16 and (k % 16) < (mm % 16):
                m[k, mm] = 1.0
    return m


def pool_side(P, C, st, uT_d, u_res, agU, agU_res, sel, invc, wpool_d, pscale, yT_d, y_res, NT, name="pls", q="pool"):
    nb = NT // 128
    nbh = nb // 2
    wp = sb(P, st, f"{name}_wp", [128, 4, 128], BF16)
    wp_r = Res()
    uext = sb(P, st, f"{name}_ue", [128, nbh, 144], F32)
    ue_r = Res()
    H = sb(P, st, f"{name}_H", [128, 4, nbh + 1, 16], F32)
    H_r = Res()
    A = sb(P, st, f"{name}_A", [128, nbh, 144], F32)
    B = sb(P, st, f"{name}_B", [128, nbh, 144], F32)
    ab_r = Res()
    diff = sb(P, st, f"{name}_df", [128, nbh, 128], BF16)
    df_r = Res()
    t16 = sb(P, st, f"{name}_t16", [128, 16], F32)
    yb = sb(P, st, f"{name}_y", [128, nbh * 128], BF16)
    yb_r = Res()
    s_u, s_H, s_y = P.S("ld_pu0"), P.S("ld_pH0"), P.S("st_py0")
    agUv = agU.rearrange("(r g p) (i c) -> g p r i c", r=4, g=4, c=16)

    def body():
        P.dma(q, P.S("ld_w0"), I("dma_start", out=wp[:], in_=wpool_d), writes=[wp_r])
        for g in range(4):
            w = 2 << g
            for hf in range(2):
                b0 = hf * nbh
                P.dma(q, s_u, I("dma_start", out=uext[:, :, 16:144],
                                in_=uT_d[g][:, b0 * 128:(b0 + nbh) * 128].rearrange("p (i c) -> p i c", c=128)),
                      reads=[u_res], writes=[ue_r])
                if hf == 0:
                    P.op("dve", I("memset", H[:, :, 0, :], 0.0), writes=[H_r])
                    P.dma(q, s_H, I("dma_start", out=H[:, :, 1:nbh + 1, :], in_=agUv[g][:, :, 0:nbh, :]), reads=[agU_res], pwrites=[H_r])
                else:
                    P.dma(q, s_H, I("dma_start", out=H[:], in_=agUv[g][:, :, b0 - 1:b0 + nbh, :]), reads=[agU_res], writes=[H_r])
                P._cap.append(("gap", None, None))
                hal = uext[:, :, 0:16]
                P.op("dve", I("tensor_scalar", hal, H[:, 0, 1:nbh + 1, :], sel[:, 4:5], None, ALU.mult), reads=[H_r], pwrites=[ue_r])
                for r in range(1, 4):
                    P.op("dve", I("scalar_tensor_tensor", hal, H[:, r, 1:nbh + 1, :], sel[:, 4 + r:5 + r], hal, ALU.mult, ALU.add),
                         reads=[H_r, ue_r], pwrites=[ue_r])
                P.op("dve", I("scalar_tensor_tensor", hal, H[:, 3, 0:nbh, :], sel[:, 8:9], hal, ALU.mult, ALU.add),
                     reads=[H_r, ue_r], pwrites=[ue_r])
                src = uext
                bufs = [A, B]
                lo = 0
                for stp in range(g + 1):
                    sh = 1 << stp
                    lo = lo + sh
                    dst = bufs[stp % 2]
                    P.op("dve", I("tensor_tensor", dst[:, :, lo:144], src[:, :, lo:144], src[:, :, lo - sh:144 - sh], ALU.add),
                         reads=[ue_r, ab_r], writes=[ab_r])
                    src = dst
                sw = src
                P.op("dve", I("scalar_tensor_tensor", diff[:], sw[:, :, 16:144], 1.0 / w, uext[:, :, 16:144], ALU.mult, ALU.subtract),
                     reads=[ab_r, ue_r], writes=[df_r])
                if hf == 0:
                    P.op("dve", I("tensor_tensor", t16[:], sw[:, 0, 16:32], invc[:, g, :], ALU.mult), reads=[ab_r], writes=[ab_r])
                    P.op("dve", I("tensor_tensor", diff[:, 0, 0:16], t16[:], uext[:, 0, 16:32], ALU.subtract),
                         reads=[ab_r, ue_r, df_r], pwrites=[df_r])
                for t in range(nbh // 4):
                    P._cap.append(("slotmm", dict(
                        lhsT=wp[:, g, :], rhs=diff[:, 4 * t:4 * t + 4, :].rearrange("p i c -> p (i c)"), mm_reads=[wp_r, df_r],
                        evac=(lambda ps, t=t, g=g: I("tensor_scalar", yb[:, t * 512:(t + 1) * 512], ps, pscale[:, g:g + 1], None, ALU.mult)),
                        ev_writes=[yb_r] if t == 0 else [], ev_pwrites=[] if t == 0 else [yb_r]), None))
                P.dma(q, s_y, I("dma_start", out=yT_d[g][:, b0 * 128:(b0 + nbh) * 128], in_=yb[:]), reads=[yb_r], pwrites=[y_res])
    return P.capture(body)


def oddin_phase3(P, C, hin, hin_res, w_in_d, wqb_d, wkvb_d, rot_d, qn_g, kvn_g, gains, gidx, CS_d, CS_res,
                 QT_d, agKn_in, agKr_in, agV_in, out_res, NT, TN=512, name="oin"):
    nt = NT // TN
    nb = TN // 128
    with ExitStack() as st:
        w = sb(P, st, f"{name}_w", [128, KC, MLA_IN], BF16)
        wq = sb(P, st, f"{name}_wq", [128, 2, 1536], BF16)
        wkv = sb(P, st, f"{name}_wkv", [128, 2048], BF16)
        w_r = Res("w")
        P.dma("pool", P.S("ld_w0"), [I("dma_start", out=w[:], in_=w_in_d), I("dma_start", out=wq[:], in_=wqb_d),
                                     I("dma_start", out=wkv[:], in_=wkvb_d)], writes=[w_r])
        wqr = sb(P, st, f"{name}_wqr", [128, 2, 512], BF16)
        wkr = sb(P, st, f"{name}_wkr", [128, KC, 32], BF16)
        wr_r = Res("wr")
        qv = wq[:, :, 1024:1536].rearrange("p j (h t x) -> p j h t x", t=2, x=16)
        qrv = wqr[:].rearrange("p j (h t x) -> p j h t x", t=2, x=16)
        for j in range(2):
            P.op("dve", I("tensor_scalar", qrv[:, j, :, 0, :], qv[:, j, :, 1, :], -1.0, None, ALU.mult), reads=[w_r], pwrites=[wr_r])
            P.op("dve", I("tensor_copy", qrv[:, j, :, 1, :], qv[:, j, :, 0, :]), reads=[w_r], pwrites=[wr_r])
        P.op("dve", I("tensor_scalar", wkr[:, :, 0:16], w[:, :, 400:416], -1.0, None, ALU.mult), reads=[w_r], pwrites=[wr_r])
        P.op("dve", I("tensor_copy", wkr[:, :, 16:32], w[:, :, 384:400]), reads=[w_r], pwrites=[wr_r])
        hb = [sb(P, st, f"{name}_hb{i}", [128, KC, TN], F32) for i in range(2)]
        hb_r = [RL(KC, f"hb{i}_") for i in range(2)]
        xn = [sb(P, st, f"{name}_xn{i}", [128, KC, TN], BF16) for i in range(3)]
        xn_r = [RL(KC, f"xn{i}_") for i in range(3)]
        cs = [sb(P, st, f"{name}_cs{i}", [128, 2, TN], F32) for i in range(2)]
        cs_r = RL(2, "cs")
        sqb = sb(P, st, f"{name}_sqb", [128, KC, TN], BF16)
        sqb_r = RL(KC, "sqb")
        sql = sb(P, st, f"{name}_sql", [128, 3, TN], BF16)
        sql_r = RL(3, "sql")
        rstdb = sb(P, st, f"{name}_rstdb", [128, TN], F32)
        rstdb_r = Res("rstdb")
        rstdl = sb(P, st, f"{name}_rstdl", [128, TN], F32)
        rstdl_r = Res("rstdl")
        cl = [sb(P, st, f"{name}_cl{i}", [128, 3, TN], F32) for i in range(2)]
        cl_r = [RL(3, f"cl{i}_") for i in range(2)]
        cn = [sb(P, st, f"{name}_cn{i}", [128, 3, TN], BF16) for i in range(2)]
        cn_r = [RL(3, f"cn{i}_") for i in range(2)]
        NE = 4
        eb = [sb(P, st, f"{name}_e{i}", [128, TN], BF16) for i in range(NE)]
        eb_r = RL(NE, "eb")
        t1 = sb(P, st, f"{name}_t1", [128, TN], F32)
        t2 = sb(P, st, f"{name}_t2", [128, TN], F32)
        t_r = Res("t")
        vb = [sb(P, st, f"{name}_v{i}", [128, MH, nb, 65], BF16) for i in range(2)]
        vb_r = RL(2, "vb")
        s_h = [P.S(f"ldp_h{i}") for i in range(2)]
        s_cs = [P.S(f"ldp_cs{i}") for i in range(2)]
        s_e = [P.S(f"st_qk{i}") for i in range(NE)]
        s_v = [P.S(f"st_v{i}") for i in range(2)]
        gl = [gains[:, gidx, kc:kc + 1] for kc in range(KC)]
        for i in range(2):
            P.op("dve", I("memset", vb[i][:, :, :, 64:65], 1.0), writes=[vb_r[i]])
        cnt = dict(b3=0, e=0)

        def norm_gen(h, h_res, g, sq, sq_res, bank, rstd, rstd_res, xo, xo_res, nkc, dim):
            for kc in range(nkc):
                P.op("act", I("activation", out=sq[kc], in_=h[kc], func=AF.Square), reads=[h_res[kc]], writes=[sq_res[kc]])
            P.op("pe", [I("matmul", C.ps[bank][:, :TN], C.ones_bf[:], sq[kc], start=(kc == 0), stop=(kc == nkc - 1)) for kc in range(nkc)],
                 reads=[C.r_const] + list(sq_res[:nkc]), writes=[C.psr[bank]])
            P.op("dve", I("tensor_scalar", rstd, C.ps[bank][:, :TN], 1.0 / dim, EPS, ALU.mult, ALU.add), reads=[C.psr[bank]], writes=[rstd_res])
            P.op("act", I("activation", out=rstd, in_=rstd, func=AF.Sqrt), reads=[rstd_res], writes=[rstd_res])
            P.op("dve", I("reciprocal", rstd, rstd), reads=[rstd_res], writes=[rstd_res])
            for kc in range(nkc):
                P.op("dve", I("scalar_tensor_tensor", xo[kc], h[kc], g[kc], rstd, ALU.mult, ALU.mult),
                     reads=[h_res[kc], rstd_res], writes=[xo_res[kc]])

        def load(t):
            s = t % 2
            P.dma("pool", s_h[s], I("dma_start", out=hb[s][:], in_=tview(hin, t, TN)), reads=[hin_res[t]], writes=hb_r[s])

        def load_cs(t):
            s = t % 2
            P.dma("pool", s_cs[s], I("dma_start", out=cs[s][:], in_=CS_d[:, :, t * TN:(t + 1) * TN].rearrange("w p n -> p w n")),
                  reads=[CS_res], writes=[cs_r[s]])

        def S1(t):
            s, x = t % 2, t % 3
            norm_gen([hb[s][:, kc, :] for kc in range(KC)], hb_r[s], gl, [sqb[:, kc, :] for kc in range(KC)], sqb_r,
                     6, rstdb[:], rstdb_r, [xn[x][:, kc, :] for kc in range(KC)], xn_r[x], KC, D)

        def S2(t):
            s, x = t % 2, t % 3
            for j, c0 in enumerate((0, 128, 256)):
                bank = j
                P.op("pe", [I("matmul", C.ps[bank][:, :TN], w[:, kc, c0:c0 + 128], xn[x][:, kc, :],
                              start=(kc == 0), stop=(kc == KC - 1)) for kc in range(KC)],
                     reads=[w_r] + xn_r[x], writes=[C.psr[bank]])
                P.op("act", I("activation", out=cl[s][:, j, :], in_=C.ps[bank][:, :TN], func=AF.Copy),
                     reads=[C.psr[bank]], writes=[cl_r[s][j]])
            norm_gen([cl[s][:, j, :] for j in range(2)], cl_r[s][0:2], [qn_g[:, j:j + 1] for j in range(2)],
                     [sql[:, j, :] for j in range(2)], sql_r, 7, rstdl[:], rstdl_r,
                     [cn[s][:, j, :] for j in range(2)], cn_r[s][0:2], 2, 256)
            norm_gen([cl[s][:, 2, :]], cl_r[s][2:3], [kvn_g[:, 0:1]], [sql[:, 2, :]], sql_r[2:3], 7, rstdl[:], rstdl_r,
                     [cn[s][:, 2, :]], cn_r[s][2:3], 1, 128)

        def nbank3():
            k = cnt["b3"]
            cnt["b3"] += 1
            return 3 + (k % 3)

        def neb():
            k = cnt["e"]
            cnt["e"] += 1
            return k % NE

        def rope_apply(bq, br, M, scale, s, dst_bf, dst_res):
            P.op("dve", I("scalar_tensor_tensor", t1[:M, :], C.ps[bq][:M, :TN], scale, cs[s][:M, 0, :], ALU.mult, ALU.mult),
                 reads=[C.psr[bq], cs_r[s]], writes=[t_r])
            P.op("dve", I("scalar_tensor_tensor", t2[:M, :], C.ps[br][:M, :TN], scale, cs[s][:M, 1, :], ALU.mult, ALU.mult),
                 reads=[C.psr[br], cs_r[s], t_r], writes=[t_r])
            P.op("dve", I("tensor_tensor", dst_bf, t1[:M, :], t2[:M, :], ALU.add), reads=[t_r], writes=[dst_res])

        def S3(t):
            s, x = t % 2, t % 3
            tok = slice(t * TN, (t + 1) * TN)
            c_, c_r = cn[s], cn_r[s]
            bank, bank2 = nbank3(), nbank3()
            P.op("pe", [I("matmul", C.ps[bank][:32, :TN], w[:, kc, 384:416], xn[x][:, kc, :],
                          start=(kc == 0), stop=(kc == KC - 1)) for kc in range(KC)], reads=[w_r] + xn_r[x], writes=[C.psr[bank]])
            P.op("pe", [I("matmul", C.ps[bank2][:32, :TN], wkr[:, kc, :], xn[x][:, kc, :],
                          start=(kc == 0), stop=(kc == KC - 1)) for kc in range(KC)], reads=[wr_r] + xn_r[x], writes=[C.psr[bank2]])
            e = neb()
            rope_apply(bank, bank2, 32, 1.0, s, eb[e][0:32, :], eb_r[e])
            P.dma("sp", s_e[e], I("dma_start", out=agKr_in[:, tok], in_=eb[e][0:32, :]), reads=[eb_r[e]], pwrites=[out_res["agKr"]])
            for m in range(8):
                bank = nbank3()
                e = neb()
                P.op("pe", I("matmul", C.ps[bank][:, :TN], wkv[:, m * 128:(m + 1) * 128], c_[:, 2, :], start=True, stop=True),
                     reads=[w_r, c_r[2]], writes=[C.psr[bank]])
                P.op("act", I("activation", out=eb[e][:], in_=C.ps[bank][:, :TN], func=AF.Copy), reads=[C.psr[bank]], writes=[eb_r[e]])
                P.dma("sp", s_e[e], I("dma_start", out=agKn_in[m][:, tok], in_=eb[e][:]), reads=[eb_r[e]], pwrites=[out_res["agKn"][m]])
            vs = t % 2
            for blk in range(nb):
                for half in range(2):
                    bank = nbank3()
                    P.op("pe", I("matmul", C.ps[bank][:, :512], c_[:, 2, blk * 128:(blk + 1) * 128],
                                 wkv[:, 1024 + half * 512:1024 + (half + 1) * 512], start=True, stop=True),
                         reads=[w_r, c_r[2]], writes=[C.psr[bank]])
                    src = C.ps[bank][:, :512].rearrange("p (h d) -> p h d", d=64)
                    dst = vb[vs][:, half * 8:(half + 1) * 8, blk, 0:64]
                    P.op("act", I("activation", out=dst, in_=src, func=AF.Copy), reads=[C.psr[bank]], pwrites=[vb_r[vs]])
            P.dma("sp", s_v[vs], [I("dma_start", out=agV_in[h][:, t * nb * 65:(t + 1) * nb * 65],
                                    in_=vb[vs][:, h, :, :].rearrange("p i c -> p (i c)")) for h in range(MH)],
                  reads=[vb_r[vs]], pwrites=out_res["agV"])
            for m in range(4):
                bank, bank2 = nbank3(), nbank3()
                e = neb()
                P.op("pe", [I("matmul", C.ps[bank][:, :TN], wq[:, j, 1024 + m * 128:1024 + (m + 1) * 128], c_[:, j, :],
                              start=(j == 0), stop=(j == 1)) for j in range(2)], reads=[w_r] + c_r[0:2], writes=[C.psr[bank]])
                P.op("pe", [I("matmul", C.ps[bank2][:, :TN], wqr[:, j, m * 128:(m + 1) * 128], c_[:, j, :],
                              start=(j == 0), stop=(j == 1)) for j in range(2)], reads=[wr_r] + c_r[0:2], writes=[C.psr[bank2]])
                rope_apply(bank, bank2, 128, QSCALE, s, eb[e][:], eb_r[e])
                P.dma("sp", s_e[e], [I("dma_start", out=QT_d[4 * m + hh, 64:96, tok], in_=eb[e][hh * 32:(hh + 1) * 32, :]) for hh in range(4)],
                      reads=[eb_r[e]], pwrites=[out_res["q"]])
            for m in range(8):
                bank = nbank3()
                e = neb()
                P.op("pe", [I("matmul", C.ps[bank][:, :TN], wq[:, j, m * 128:(m + 1) * 128], c_[:, j, :],
                              start=(j == 0), stop=(j == 1)) for j in range(2)], reads=[w_r] + c_r[0:2], writes=[C.psr[bank]])
                P.op("act", I("activation", out=eb[e][:], in_=C.ps[bank][:, :TN], func=AF.Copy, scale=QSCALE),
                     reads=[C.psr[bank]], writes=[eb_r[e]])
                P.dma("sp", s_e[e], [I("dma_start", out=QT_d[2 * m + hh, 0:64, tok], in_=eb[e][hh * 64:(hh + 1) * 64, :]) for hh in range(2)],
                      reads=[eb_r[e]], pwrites=[out_res["q"]])

        import os as _os
        load(0)
        load_cs(0)
        if nt > 1:
            load(1)
            load_cs(1)
        for step in range(nt + 2):
            lists = []
            if step - 2 >= 0:
                lists.append(P.capture(lambda: S3(step - 2)))
            if 0 <= step - 1 < nt:
                lists.append(P.capture(lambda: S2(step - 1)))
            if step < nt:
                lists.append(P.capture(lambda: S1(step)))
            if _os.environ.get("CHECKIL"):
                for a_ in range(len(lists)):
                    for b_ in range(a_ + 1, len(lists)):
                        check_interleave_prop(lists[a_], lists[b_])
            n = max(len(L) for L in lists)
            pos = [0] * len(lists)
            for k in range(n):
                for li, L in enumerate(lists):
                    want = ((k + 1) * len(L) + n - 1) // n
                    while pos[li] < min(want, len(L)):
                        P.replay([L[pos[li]]])
                        pos[li] += 1
            if step + 2 < nt:
                load(step + 2)
            if step - 2 >= 0 and step < nt:
                load_cs(step)
        P.barrier()


def check_interleave_prop(L0, L1):
    def sets(L):
        rd, wr = set(), set()
        names = {}
        for kind, a_, kw_ in L:
            for R in kw_["reads"]:
                rd.add(id(R)); names[id(R)] = R.name
            for R in list(kw_["writes"]):
                wr.add(id(R)); names[id(R)] = R.name
        return rd, wr, names
    r0, w0, n0 = sets(L0)
    r1, w1, n1 = sets(L1)
    bad = (w0 & (r1 | w1)) | (w1 & (r0 | w0))
    for b_ in bad:
        print("INTERLEAVE VIOLATION (stage overlap):", n0.get(b_, n1.get(b_)))


def rope_side(P, C, st, pos_d, invf, CS_d, CS_res, NT, name="rps"):
    CH = 1024 if NT >= 1024 else NT
    C1 = 6.28125
    C2 = 2.0 * PI - C1
    pi_ = sb(P, st, f"{name}_pi", [128, CH], I32)
    ang = sb(P, st, f"{name}_ang", [128, CH], F32)
    tt = sb(P, st, f"{name}_t", [128, CH], F32)
    m = sb(P, st, f"{name}_m", [128, CH], F32)
    o = [sb(P, st, f"{name}_o{i}", [128, CH], F32) for i in range(2)]
    r = Res()
    o_r = RL(2)
    halfpi = sb(P, st, f"{name}_hpi", [128, 1], F32)

    def body():
        P.op("dve", I("memset", halfpi[:], PI / 2), writes=[r])
        for c in range(NT // CH):
            sl = slice(c * CH, (c + 1) * CH)
            P.dma("pool", P.S("ldp_rp"), I("dma_start", out=pi_[:], in_=pos_d[:, sl]), writes=[r])
            P._cap.append(("gap", None, None))
            P.op("dve", I("tensor_copy", ang[:], pi_[:]), reads=[r], writes=[r])
            P.op("dve", I("tensor_scalar", ang[:], ang[:], invf[:, 0:1], None, ALU.mult), reads=[r], writes=[r])
            P.op("dve", I("tensor_scalar", tt[:], ang[:], 1.0 / (2.0 * PI), None, ALU.mult), reads=[r], writes=[r])
            P.op("dve", I("tensor_copy", pi_[:], tt[:]), reads=[r], writes=[r])
            P.op("dve", I("tensor_copy", tt[:], pi_[:]), reads=[r], writes=[r])
            P.op("dve", I("scalar_tensor_tensor", m[:], tt[:], -C1, ang[:], ALU.mult, ALU.add), reads=[r], writes=[r])
            P.op("dve", I("scalar_tensor_tensor", m[:], tt[:], -C2, m[:], ALU.mult, ALU.add), reads=[r], writes=[r])
            P.op("dve", I("tensor_scalar", m[:], m[:], -PI, PI, ALU.max, ALU.min), reads=[r], writes=[r])
            P.op("act", I("activation", out=o[1][:], in_=m[:], func=AF.Sin), reads=[r], writes=[o_r[1]])
            P.dma("pool", P.S("stp_rp1"), I("dma_start", out=CS_d[1, :, sl], in_=o[1][:]), reads=[o_r[1]], pwrites=[CS_res])
            P.op("dve", I("tensor_scalar", tt[:], m[:], -1.0, None, ALU.mult), reads=[r], writes=[r])
            P.op("dve", I("tensor_tensor", tt[:], tt[:], m[:], ALU.max), reads=[r], writes=[r])
            P.op("act", I("activation", out=o[0][:], in_=tt[:], func=AF.Sin, scale=-1.0, bias=halfpi[:, 0:1]), reads=[r], writes=[o_r[0]])
            P.dma("pool", P.S("stp_rp0"), I("dma_start", out=CS_d[0, :, sl], in_=o[0][:]), reads=[o_r[0]], pwrites=[CS_res])
    return P.capture(body)


import ml_dtypes
from concourse.bass_utils import run_bass_kernel_spmd

NTOK = 4096
NBLK = NTOK // 128
SEQ = 16384
DEPTH = 4
NG = 13


def build_program():
    nc = bass.Bass("TRN2", target_bir_lowering=False)
    NT = NTOK

    def ein(name, shape, dt=F32):
        return nc.dram_tensor(name, list(shape), dt, kind="ExternalInput").ap()

    def it(name, shape, dt):
        return nc.dram_tensor(name, list(shape), dt).ap()

    xT = ein("xT", [KC, 128, NT])
    pos_d = ein("pos", [128, NT], I32)
    wgu = [ein(f"wgu{i}", [FC, 128, 2, KC, 128]) for i in range(8)]
    wd = [ein(f"wd{i}", [KC, 128, FC, 128]) for i in range(8)]
    ab_w_in = [ein(f"ab_w_in{i}", [128, KC, ABIN]) for i in range(2)]
    ab_w_out = [ein(f"ab_w_out{i}", [128, KC, D]) for i in range(2)]
    wpool = [ein(f"wpool{i}", [128, 4, 128]) for i in range(2)]
    mla_w_in = [ein(f"mla_w_in{i}", [128, KC, MLA_IN]) for i in range(2)]
    mla_wqb = [ein(f"mla_wqb{i}", [128, 2, 1536]) for i in range(2)]
    mla_wkvb = [ein(f"mla_wkvb{i}", [128, 2048]) for i in range(2)]
    mla_w_out = [ein(f"mla_w_out{i}", [128, KC, D]) for i in range(2)]
    gains_d = ein("gains", [128, NG, KC])
    small_d = ein("small", [128, 64])
    invc_d = ein("invc", [128, 4, 16])
    rot_d = ein("rot", [128, 128])
    fmat_d = ein("fmat", [128, 128])
    masks_d = ein("masks", [128, 3, 4, 128], BF16)
    ident_d = ein("ident", [128, 128], BF16)
    outT = nc.dram_tensor("outT", [KC, 128, NT], F32, kind="ExternalOutput").ap()

    hT = it("hT", [KC, 128, NT], F32)
    yT_d = it("yT_d", [KC, 128, NT], BF16)
    uT_d = it("uT_d", [4, 128, NT], F32)
    QTf_d = it("QTf_d", [8, 64, NT], BF16)
    QTm_d = it("QTm_d", [16, 96, NT], BF16)
    agK_in = [it(f"agK_in{m}", [128, NT], BF16) for m in range(8)]
    agK = [it(f"agK{m}", [4 * 128, NT], BF16) for m in range(8)]
    agV_in = [it(f"agV_in{h}", [128, NBLK * 65], BF16) for h in range(16)]
    agV = [it(f"agV{h}", [4 * 128, NBLK * 65], BF16) for h in range(16)]
    agF_in = it("agF_in", [8, NT], F32)
    agF = it("agF", [32, NT], F32)
    agU_in = it("agU_in", [4 * 128, NBLK * 16], F32)
    agU = it("agU", [16 * 128, NBLK * 16], F32)
    agKr_in = it("agKr_in", [32, NT], BF16)
    agKr = it("agKr", [128, NT], BF16)
    FK_d = it("FK_d", [3, 4, 8, NT], BF16)
    FQ_d = it("FQ_d", [3, 8, NT], BF16)
    CS_d = it("CS_d", [2, 128, NT], F32)

    stack = ExitStack()
    with stack as st:
        P = Prog(nc, st)
        C = Ctx(P)
        gains = sb(P, st, "gains_sb", [128, NG, KC], F32)
        small = sb(P, st, "small_sb", [128, 64], F32)
        invc = sb(P, st, "invc_sb", [128, 4, 16], F32)
        masks = sb(P, st, "masks_sb", [128, 3, 4, 128], BF16)
        ident = sb(P, st, "ident_sb", [128, 128], BF16)
        for (a, b) in ((gains, gains_d), (small, small_d), (invc, invc_d), (masks, masks_d), (ident, ident_d)):
            P.dma("sp", P.S("misc"), I("dma_start", out=a[:], in_=b), writes=[C.r_const])
        P.barrier()
        sel = small[:, 0:16]
        invf = small[:, 16:17]
        bfg = [small[0:8, 17 + i:18 + i] for i in range(2)]
        pscale = [small[:, 20 + 4 * i:24 + 4 * i] for i in range(2)]
        qn = [small[:, 28 + 2 * i:30 + 2 * i] for i in range(2)]
        kvn = [small[:, 32 + i:33 + i] for i in range(2)]

        nt = NT // 512
        x_r = RL(nt, "x")
        h_r = RL(nt, "h")
        o_r = RL(nt, "o")
        CS_r = Res()
        R = dict(u=Res(), agU=Res(), qf=Res(), qm=Res(), agF=Res(), agKr=Res(), agK=RL(8), agV=RL(16))
        G = dict(F=Res(), U=Res(), Kr=Res(), K=RL(8), V=RL(16))
        FK_r, FQ_r, y_r = Res(), Res(), Res()
        for layer in range(DEPTH):
            i = layer // 2
            if layer == 0:
                ffn_phase2(P, C, [(wgu[0], wd[0], 0)], xT, x_r, hT, h_r, gains, NT, name="f0a")
            if layer % 2 == 0:
                oR = dict(u=R["u"], agU=R["agU"], q=R["qf"], agK=R["agK"][:4], agV=R["agV"][:8], agF=R["agF"])
                evenin_phase(P, C, hT, h_r, ab_w_in[i], bfg[i], gains, 8 + layer, QTf_d, agK_in[:4], agV_in[:8], agF_in, agU_in,
                             uT_d, oR, NT, name=f"ei{layer}")
                allgather(P, agF_in, agF, [R["agF"]], [G["F"]])
                for h in range(8):
                    if h % 2 == 0:
                        allgather(P, agK_in[h // 2], agK[h // 2], [R["agK"][h // 2]], [G["K"][h // 2]])
                    allgather(P, agV_in[h], agV[h], [R["agV"][h]], [G["V"][h]])
                    if h == 0:
                        allgather(P, agU_in, agU, [R["agU"]], [G["U"]])
                fprep_phase2(P, C, agF, G["F"], sel, fmat_d, FK_d, FK_r, FQ_d, FQ_r, NT, name=f"fp{layer}")
                ld = fox_loader2(P, agK[:4], G["K"][:4], QTf_d, R["qf"], agV[:8], G["V"][:8], FK_d, FK_r, FQ_d, FQ_r, NT)
                def sf(st_, i=i, layer=layer):
                    L = pool_side(P, C, st_, uT_d, R["u"], agU, G["U"], sel, invc, wpool[i], pscale[i], yT_d, y_r, NT, name=f"pls{layer}")
                    if layer == 0:
                        L = L + rope_side(P, C, st_, pos_d, invf, CS_d, CS_r, NT)
                    return L
                attn_phase2(P, C, 8, 70, ld, masks[:, 0, :, :], ident, yT_d, y_r, 4, NT, name=f"at{layer}", side_fn=sf)
                wout = ab_w_out[i]
            else:
                oR = dict(q=R["qm"], agKr=R["agKr"], agKn=R["agK"], agV=R["agV"])
                oddin_phase3(P, C, hT, h_r, mla_w_in[i], mla_wqb[i], mla_wkvb[i], rot_d, qn[i], kvn[i], gains, 8 + layer,
                            CS_d, CS_r, QTm_d, agK_in, agKr_in, agV_in, oR, NT, name=f"oi{layer}")
                allgather(P, agKr_in, agKr, [R["agKr"]], [G["Kr"]])
                for h in range(16):
                    if h % 2 == 0:
                        allgather(P, agK_in[h // 2], agK[h // 2], [R["agK"][h // 2]], [G["K"][h // 2]])
                    allgather(P, agV_in[h], agV[h], [R["agV"][h]], [G["V"][h]])
                ld = mla_loader(P, agK, G["K"], agKr, G["Kr"], QTm_d, R["qm"], agV, G["V"], NT)
                attn_phase2(P, C, 16, 96, ld, masks[:, 1, :, :], ident, yT_d, y_r, 0, NT, name=f"at{layer}")
                wout = mla_w_out[i]
            outproj_phase(P, C, hT, h_r, hT, h_r, yT_d, y_r, wout, NT, name=f"op{layer}")
            stages = [(wgu[2 * layer + 1], wd[2 * layer + 1], 2 * layer + 1)]
            if layer + 1 < DEPTH:
                stages.append((wgu[2 * layer + 2], wd[2 * layer + 2], 2 * layer + 2))
            ffn_phase2(P, C, stages, hT, h_r, hT, h_r, gains, NT, name=f"f{layer}b",
                       final=(12, outT, o_r) if layer + 1 == DEPTH else None)
        P.finish()
        nc._mk_stats = (P.ninstr, P.nwaits)
    return nc


_PROG = [None]


def _host_inputs(inp):
    f32 = np.float32
    shared = {}
    for l in range(DEPTH):
        for s in range(2):
            shared[f"wgu{2 * l + s}"] = lay_wgu(np.asarray(inp["ffn_w_gate"][l, s], f32), np.asarray(inp["ffn_w_up"][l, s], f32))
            shared[f"wd{2 * l + s}"] = lay_wd(np.asarray(inp["ffn_w_down"][l, s], f32))
    for i in range(2):
        shared[f"ab_w_in{i}"] = lay_kmajor(np.asarray(inp["ab_w_in"][i], f32))
        shared[f"ab_w_out{i}"] = lay_kmajor(np.asarray(inp["ab_w_out"][i], f32))
        shared[f"wpool{i}"] = np.ascontiguousarray(np.asarray(inp["pool_w"][i], f32).transpose(1, 0, 2))
        shared[f"mla_w_in{i}"] = lay_kmajor(np.asarray(inp["mla_w_in"][i], f32))
        shared[f"mla_wqb{i}"] = lay_wqb(np.asarray(inp["mla_w_q_b"][i], f32))
        shared[f"mla_wkvb{i}"] = lay_wkvb(np.asarray(inp["mla_w_kv_b"][i], f32))
        shared[f"mla_w_out{i}"] = lay_kmajor(np.asarray(inp["mla_w_out"][i], f32))
    gains = np.zeros((128, NG, KC), f32)
    for l in range(DEPTH):
        for s in range(2):
            gains[:, 2 * l + s, :] = lay_gain(np.asarray(inp["norm_ffn"][l, s], f32))
        gains[:, 8 + l, :] = lay_gain(np.asarray(inp["norm_mix"][l], f32))
    gains[:, 12, :] = lay_gain(np.asarray(inp["norm_final"], f32))
    shared["gains"] = gains
    invf, rot = rope_consts()
    shared["rot"] = rot
    shared["fmat"] = fmat_const()
    shared["ident"] = np.eye(128, dtype=f32).astype(ml_dtypes.bfloat16)
    x = np.asarray(inp["x"], f32)
    pos = np.asarray(inp["positions"], np.int32)
    maps = []
    for c in range(8):
        b, j = c // 4, c % 4
        sel, invc, mf, mm = core_tables(j)
        small = np.zeros((128, 64), f32)
        small[:, 0:16] = sel
        small[:, 16:17] = invf
        for i in range(2):
            small[0:8, 17 + i] = np.asarray(inp["ab_b_forget"][i], f32)
            small[:, 20 + 4 * i:24 + 4 * i] = np.asarray(inp["pool_scale"][i], f32).reshape(4, 128).T
            small[:, 28 + 2 * i:30 + 2 * i] = np.asarray(inp["mla_q_norm"][i], f32).reshape(2, 128).T
            small[:, 32 + i] = np.asarray(inp["mla_kv_norm"][i], f32)
        tp = tok_perm(j, NBLK)
        m = dict(shared)
        m["xT"] = np.ascontiguousarray(x[b][tp].T).reshape(KC, 128, NTOK)
        m["pos"] = np.ascontiguousarray(np.broadcast_to(pos[b][tp][None, :], (128, NTOK)))
        m["small"] = small
        m["invc"] = invc
        m["masks"] = np.ascontiguousarray(np.stack([mf, mm, mask01_of(mm)], axis=1)).astype(ml_dtypes.bfloat16)
        maps.append(m)
    return maps


def kernel(**inputs):
    if _PROG[0] is None:
        _PROG[0] = build_program()
    nc = _PROG[0]
    maps = _host_inputs(inputs)
    res = run_bass_kernel_spmd(nc, maps, core_ids=list(range(8)))
    out = np.zeros((2, SEQ, D), np.float32)
    for c in range(8):
        b, j = c // 4, c % 4
        tp = tok_perm(j, NBLK)
        out[b, tp, :] = np.asarray(res.results[c]["outT"], np.float32).reshape(D, NTOK).T
    return out
```
